# Optimizing a Trainium2 kernel written in Bass

```python
import jax, jax.numpy as jnp
from jax import lax
import numpy as np

D_MODEL = 2048
BATCH = 4
SEQ = 8192
DEPTH = 4

N_MIXERS = 3
GRID_W = 64
BRANCH_W = D_MODEL
NORM_EPS = 1e-6

POOL_WINDOWS = (2, 4, 8, 16)
N_POOL_GROUPS = 4
POOL_GROUP_DIM = BRANCH_W // N_POOL_GROUPS

GLA_HEADS = 4
GLA_KEY_W = D_MODEL // 2
GLA_VAL_W = BRANCH_W
GLA_DK = GLA_KEY_W // GLA_HEADS
GLA_DV = GLA_VAL_W // GLA_HEADS
GLA_LOWRANK = 16
GLA_TAU = 16.0
GLA_CHUNK = 64

ATTN_HEAD_DIM = 128
ATTN_HEADS = BRANCH_W // ATTN_HEAD_DIM
ATTN_KV_HEADS = 4
ATTN_GROUP = ATTN_HEADS // ATTN_KV_HEADS
ATTN_Q_W = ATTN_HEADS * ATTN_HEAD_DIM
ATTN_KV_W = ATTN_KV_HEADS * ATTN_HEAD_DIM
ATTN_BLOCK = 128
ROPE_AXIS_DIM = ATTN_HEAD_DIM // 2
ROPE_THETA = 10000.0

kernel_name = "hybrid_pool_gla_gqa_encoder"


def rms_norm(x, eps=NORM_EPS):
    xf = x.astype(jnp.float32)
    return (xf * lax.rsqrt(jnp.mean(xf * xf, axis=-1, keepdims=True) + eps)).astype(x.dtype)


def ada_modulation(c, w, b):
    m = jnp.dot(jax.nn.silu(c), w) + b
    shift, scale, gate = jnp.split(m, 3, axis=-1)
    return shift[:, None], scale[:, None], gate[:, None]


def centred_window_mean(u, w):
    T = u.shape[1]
    csum = jnp.cumsum(u.astype(jnp.float32), axis=1)
    P = jnp.concatenate([jnp.zeros_like(csum[:, :1]), csum], axis=1)
    t = jnp.arange(T)
    lo = jnp.maximum(t - w // 2, 0)
    hi = jnp.minimum(t + w // 2, T)
    cnt = (hi - lo).astype(jnp.float32)
    return ((P[:, hi] - P[:, lo]) / cnt[None, :, None]).astype(u.dtype)


def pool_mixer(h, w_in, w_grp, scale, w_out):
    B, T, _ = h.shape
    u, g = jnp.split(h @ w_in, 2, axis=-1)
    ug = u.reshape(B, T, N_POOL_GROUPS, POOL_GROUP_DIM)
    pooled = jnp.stack(
        [centred_window_mean(ug[:, :, i], w) - ug[:, :, i] for i, w in enumerate(POOL_WINDOWS)],
        axis=2)
    y = jnp.einsum('btgc,gcd->btgd', pooled, w_grp).reshape(B, T, BRANCH_W) * scale
    return (y * jax.nn.silu(g)) @ w_out


def gla_chunked_scan(q, k, v, log_a, include_diag):
    B, T, H, DK = q.shape
    DV = v.shape[-1]
    nc = T // GLA_CHUNK

    def to_chunks(z):
        return z.astype(jnp.float32).reshape(B, nc, GLA_CHUNK, H, z.shape[-1]).transpose(1, 0, 3, 2, 4)

    qc, kc, vc, ac = to_chunks(q), to_chunks(k), to_chunks(v), to_chunks(log_a)
    mask = jnp.tril(jnp.ones((GLA_CHUNK, GLA_CHUNK), dtype=bool), 0 if include_diag else -1)

    def step(S, inp):
        qi, ki, vi, ai = inp
        b = jnp.cumsum(ai, axis=2)
        o_inter = jnp.einsum('bhcd,bhde->bhce', qi * jnp.exp(b), S)
        diff = b[:, :, :, None, :] - b[:, :, None, :, :]
        decay = jnp.exp(jnp.where(mask[:, :, None], diff, -jnp.inf))
        A = jnp.sum(qi[:, :, :, None, :] * ki[:, :, None, :, :] * decay, axis=-1)
        o_intra = jnp.einsum('bhij,bhje->bhie', A, vi)
        b_last = b[:, :, -1:, :]
        S_new = jnp.exp(b_last[:, :, 0, :])[..., None] * S + jnp.einsum(
            'bhjd,bhje->bhde', ki * jnp.exp(b_last - b), vi)
        return S_new, o_inter + o_intra

    S0 = jnp.zeros((B, H, DK, DV), jnp.float32)
    _, o = lax.scan(step, S0, (qc, kc, vc, ac))
    return o.transpose(1, 0, 3, 2, 4).reshape(B, T, H, DV)


def gla_mixer(h, w_in, fwd_w1, fwd_w2, fwd_b, bwd_w1, bwd_w2, bwd_b, norm_g, w_out):
    B, T, _ = h.shape
    q, k, v, g = jnp.split(h @ w_in, [GLA_KEY_W, 2 * GLA_KEY_W, 2 * GLA_KEY_W + GLA_VAL_W], axis=-1)
    q = (q * GLA_DK ** -0.5).reshape(B, T, GLA_HEADS, GLA_DK)
    k = k.reshape(B, T, GLA_HEADS, GLA_DK)
    v = v.reshape(B, T, GLA_HEADS, GLA_DV)

    def log_decay(w1, w2, b):
        z = ((h @ w1) @ w2 + b).astype(jnp.float32)
        return (jax.nn.log_sigmoid(z) / GLA_TAU).reshape(B, T, GLA_HEADS, GLA_DK)

    rev = lambda z: z[:, ::-1]
    o_fwd = gla_chunked_scan(q, k, v, log_decay(fwd_w1, fwd_w2, fwd_b), include_diag=True)
    o_bwd = rev(gla_chunked_scan(rev(q), rev(k), rev(v), rev(log_decay(bwd_w1, bwd_w2, bwd_b)),
                                 include_diag=False))
    o = (rms_norm(o_fwd + o_bwd) * norm_g).reshape(B, T, GLA_VAL_W).astype(h.dtype)
    return (o * jax.nn.silu(g)) @ w_out


def axial_rope_tables(T):
    rows = T // GRID_W
    t = jnp.arange(T)
    row = (t // GRID_W - rows // 2).astype(jnp.float32)
    col = (t % GRID_W - GRID_W // 2).astype(jnp.float32)
    inv = ROPE_THETA ** (-jnp.arange(0, ROPE_AXIS_DIM, 2, dtype=jnp.float32) / ROPE_AXIS_DIM)
    ang = jnp.concatenate([row[:, None] * inv, col[:, None] * inv], axis=-1)
    return jnp.cos(ang), jnp.sin(ang)


def apply_rope(x, cos, sin):
    shp = (cos.shape[0],) + (1,) * (x.ndim - 3) + (cos.shape[1],)
    cos = cos.reshape(shp).astype(x.dtype)
    sin = sin.reshape(shp).astype(x.dtype)
    xr = x.reshape(x.shape[:-1] + (-1, 2))
    x0, x1 = xr[..., 0], xr[..., 1]
    return jnp.stack([x0 * cos - x1 * sin, x0 * sin + x1 * cos], axis=-1).reshape(x.shape)


def attn_mixer(h, w_in, q_norm_g, k_norm_g, w_out):
    B, T, _ = h.shape
    q, k, v, g = jnp.split(h @ w_in, [ATTN_Q_W, ATTN_Q_W + ATTN_KV_W, ATTN_Q_W + 2 * ATTN_KV_W], axis=-1)
    q = rms_norm(q.reshape(B, T, ATTN_KV_HEADS, ATTN_GROUP, ATTN_HEAD_DIM)) * q_norm_g
    k = rms_norm(k.reshape(B, T, ATTN_KV_HEADS, ATTN_HEAD_DIM)) * k_norm_g
    v = v.reshape(B, T, ATTN_KV_HEADS, ATTN_HEAD_DIM)
    cos, sin = axial_rope_tables(T)
    q = apply_rope(q, cos, sin) * ATTN_HEAD_DIM ** -0.5
    k = apply_rope(k, cos, sin)
    nb = T // ATTN_BLOCK
    qb = q.reshape(B, nb, ATTN_BLOCK, ATTN_KV_HEADS, ATTN_GROUP, ATTN_HEAD_DIM).transpose(1, 0, 2, 3, 4, 5)

    def block(qi):
        s = jnp.einsum('bqkgd,bskd->bkgqs', qi, k, preferred_element_type=jnp.float32)
        p = jax.nn.softmax(s, axis=-1).astype(v.dtype)
        return jnp.einsum('bkgqs,bskd->bqkgd', p, v)

    o = lax.map(block, qb).transpose(1, 0, 2, 3, 4, 5).reshape(B, T, ATTN_Q_W)
    return (o * jax.nn.silu(g)) @ w_out


def setup_inputs(seed: int = 0) -> dict:
    key = jax.random.key(seed)
    ks = iter(jax.random.split(key, 40))
    n_pool = len(range(0, DEPTH, N_MIXERS))
    n_gla = len(range(1, DEPTH, N_MIXERS))
    n_attn = len(range(2, DEPTH, N_MIXERS))
    D = D_MODEL

    def nrm(shape, scale):
        return jax.random.normal(next(ks), shape, jnp.float32) * scale

    def gain(shape):
        return 1.0 + 0.1 * jax.random.normal(next(ks), shape, jnp.float32)

    return {
        "x": nrm((BATCH, SEQ, D), 1.0),
        "c": nrm((BATCH, D), 1.0),
        "w_mod": nrm((DEPTH, D, 3 * D), 0.5 * D ** -0.5),
        "b_mod": nrm((DEPTH, 3 * D), 0.02),
        "pool_w_in": nrm((n_pool, D, 2 * BRANCH_W), D ** -0.5),
        "pool_w_grp": nrm((n_pool, N_POOL_GROUPS, POOL_GROUP_DIM, POOL_GROUP_DIM), POOL_GROUP_DIM ** -0.5),
        "pool_scale": gain((n_pool, BRANCH_W)),
        "pool_w_out": nrm((n_pool, BRANCH_W, D), BRANCH_W ** -0.5),
        "gla_w_in": nrm((n_gla, D, 2 * GLA_KEY_W + 2 * GLA_VAL_W), D ** -0.5),
        "gla_fwd_w1": nrm((n_gla, D, GLA_LOWRANK), D ** -0.5),
        "gla_fwd_w2": nrm((n_gla, GLA_LOWRANK, GLA_KEY_W), GLA_LOWRANK ** -0.5),
        "gla_fwd_b": nrm((n_gla, GLA_KEY_W), 0.1),
        "gla_bwd_w1": nrm((n_gla, D, GLA_LOWRANK), D ** -0.5),
        "gla_bwd_w2": nrm((n_gla, GLA_LOWRANK, GLA_KEY_W), GLA_LOWRANK ** -0.5),
        "gla_bwd_b": nrm((n_gla, GLA_KEY_W), 0.1),
        "gla_norm_g": gain((n_gla, GLA_DV)),
        "gla_w_out": nrm((n_gla, GLA_VAL_W, D), GLA_VAL_W ** -0.5),
        "attn_w_in": nrm((n_attn, D, 2 * ATTN_Q_W + 2 * ATTN_KV_W), D ** -0.5),
        "attn_q_norm_g": gain((n_attn, ATTN_HEAD_DIM)),
        "attn_k_norm_g": gain((n_attn, ATTN_HEAD_DIM)),
        "attn_w_out": nrm((n_attn, ATTN_Q_W, D), ATTN_Q_W ** -0.5),
        "final_norm_g": gain((D,)),
    }


def reference(x, c, w_mod, b_mod, pool_w_in, pool_w_grp, pool_scale, pool_w_out,
              gla_w_in, gla_fwd_w1, gla_fwd_w2, gla_fwd_b, gla_bwd_w1, gla_bwd_w2, gla_bwd_b,
              gla_norm_g, gla_w_out, attn_w_in, attn_q_norm_g, attn_k_norm_g, attn_w_out,
              final_norm_g):
    for i in range(DEPTH):
        shift, scale, gate = ada_modulation(c, w_mod[i], b_mod[i])
        h = rms_norm(x) * (1.0 + scale) + shift
        kind, j = i % N_MIXERS, i // N_MIXERS
        if kind == 0:
            y = pool_mixer(h, pool_w_in[j], pool_w_grp[j], pool_scale[j], pool_w_out[j])
        elif kind == 1:
            y = gla_mixer(h, gla_w_in[j], gla_fwd_w1[j], gla_fwd_w2[j], gla_fwd_b[j],
                          gla_bwd_w1[j], gla_bwd_w2[j], gla_bwd_b[j], gla_norm_g[j], gla_w_out[j])
        else:
            y = attn_mixer(h, attn_w_in[j], attn_q_norm_g[j], attn_k_norm_g[j], attn_w_out[j])
        x = x + gate * y
    return rms_norm(x) * final_norm_g
```

```python
import numpy as np
import concourse.bass as bass
import concourse.mybir as mybir
from concourse.bass_utils import run_bass_kernel_spmd

F32 = mybir.dt.float32
BF16 = mybir.dt.bfloat16
ALU = mybir.AluOpType
AF = mybir.ActivationFunctionType
AX = mybir.AxisListType

ENGS = ("pe", "act", "dve", "pool", "sp")
SEM_CAP = 9000


class _Op:
    __slots__ = ("eng", "fn", "dma", "seq", "deps", "signal", "sig_idx", "slot", "cnt", "waits", "id")


class Prog:
    def __init__(self, nc, ring=8):
        self.nc = nc
        self.ring = ring
        self.ops = []
        self.eng_ops = {e: [] for e in ENGS}
        self.last_w = {}
        self.readers = {}
        self.n_dma = {e: 0 for e in ENGS}
        self._nm = 0
        self.pending = {e: set() for e in ENGS}
        self.dma_ops = {e: [] for e in ENGS}

    def sb(self, shape, dtype, name=None):
        self._nm += 1
        return self.nc.alloc_sbuf_tensor(f"s{self._nm}_{name or ""}", list(shape), dtype)

    def ps(self, shape, dtype=F32, name=None):
        self._nm += 1
        return self.nc.alloc_psum_tensor(f"p{self._nm}_{name or ""}", list(shape), dtype)

    def add(self, eng, fn, reads=(), writes=(), dma=False):
        op = _Op()
        op.eng, op.fn, op.dma = eng, fn, dma
        op.id = len(self.ops)
        op.seq = len(self.eng_ops[eng])
        op.signal = dma
        op.sig_idx = None
        op.slot = op.cnt = None
        deps = set()
        for r in reads:
            w = self.last_w.get(r)
            if w is not None:
                deps.add(w)
        for k in writes:
            w = self.last_w.get(k)
            if w is not None:
                deps.add(w)
            for rd in self.readers.get(k, ()):
                deps.add(rd)
        for r in reads:
            self.readers.setdefault(r, []).append(op.id)
        for k in writes:
            self.last_w[k] = op.id
            self.readers[k] = []
        deps.discard(op.id)
        if self.pending[eng]:
            deps |= self.pending[eng]
            self.pending[eng] = set()
        if dma:
            self.dma_ops[eng].append(op.id)
            j = self.n_dma[eng]
            self.n_dma[eng] += 1
            op.slot = j % self.ring
            op.cnt = j // self.ring + 1
        op.deps = deps
        self.ops.append(op)
        self.eng_ops[eng].append(op)
        return op

    def barrier(self):
        front = set()
        for e in ENGS:
            comp = [o for o in self.eng_ops[e] if not o.dma]
            if comp:
                front.add(comp[-1].id)
            for d in self.dma_ops[e][-self.ring:]:
                front.add(d)
        for e in ENGS:
            self.pending[e] |= front

    def mark(self):
        nc = self.nc
        return (nc.sbuf_base, nc.sbuf_top, nc.psum_base, nc.psum_top)

    def release(self, m):
        self.barrier()
        nc = self.nc
        nc.sbuf_base, nc.sbuf_top, nc.psum_base, nc.psum_top = m

    def pe(self, fn, reads=(), writes=()):
        return self.add("pe", fn, reads, writes)

    def act(self, fn, reads=(), writes=()):
        return self.add("act", fn, reads, writes)

    def dve(self, fn, reads=(), writes=()):
        return self.add("dve", fn, reads, writes)

    def pool(self, fn, reads=(), writes=()):
        return self.add("pool", fn, reads, writes)

    def dma(self, q, out, in_, reads=(), writes=(), **kw):
        return self.add(q, lambda e: e.dma_start(out=out, in_=in_, **kw), reads, writes, dma=True)

    def emit(self):
        nc = self.nc
        ops = self.ops
        seen_c = {e: {f: -1 for f in ENGS} for e in ENGS}
        seen_d = {e: {} for e in ENGS}
        for op in ops:
            waits = []
            e = op.eng
            if op.dma and op.cnt > 1:
                key = (e, op.slot)
                if seen_d[e].get(key, 0) < op.cnt - 1:
                    seen_d[e][key] = op.cnt - 1
                    waits.append(("d", e, op.slot, op.cnt - 1))
            cmax = {}
            for d in op.deps:
                p = ops[d]
                if not p.dma and (p.eng not in cmax or ops[cmax[p.eng]].seq < p.seq):
                    cmax[p.eng] = d
            for d in sorted(op.deps):
                p = ops[d]
                if not p.dma and cmax[p.eng] != d:
                    continue
                if p.dma:
                    key = (p.eng, p.slot)
                    if seen_d[e].get(key, 0) >= p.cnt:
                        continue
                    seen_d[e][key] = p.cnt
                    waits.append(("d", p.eng, p.slot, p.cnt))
                else:
                    if p.eng == "pe" and e == "pe":
                        continue
                    if seen_c[e][p.eng] >= p.seq:
                        continue
                    seen_c[e][p.eng] = p.seq
                    p.signal = True
                    waits.append(("c", d))
            op.waits = waits
        nsig = {}
        for e in ENGS:
            n = 0
            for op in self.eng_ops[e]:
                if op.signal and not op.dma:
                    n += 1
                    op.sig_idx = n
            nsig[e] = n
        csem = {}
        for e in ENGS:
            k = (nsig[e] + SEM_CAP - 1) // SEM_CAP
            csem[e] = [nc.alloc_semaphore(f"c_{e}_{i}") for i in range(k)]
        dsem = {}
        for e in ENGS:
            if self.n_dma[e]:
                dsem[e] = [nc.alloc_semaphore(f"d_{e}_{i}") for i in range(min(self.ring, self.n_dma[e]))]
        self.stats = {e: (len(self.eng_ops[e]), nsig[e], self.n_dma[e]) for e in ENGS}

        def run(e, eng):
            for op in self.eng_ops[e]:
                for w in op.waits:
                    if w[0] == "d":
                        eng.wait_ge(dsem[w[1]][w[2]], 16 * w[3])
                    else:
                        p = ops[w[1]]
                        i = p.sig_idx - 1
                        eng.wait_ge(csem[p.eng][i // SEM_CAP], i % SEM_CAP + 1)
                ins = op.fn(eng)
                if op.dma:
                    ins.then_inc(dsem[e][op.slot], 16)
                elif op.signal:
                    i = op.sig_idx - 1
                    ins.then_inc(csem[e][i // SEM_CAP], 1)
            if self.n_dma[e]:
                for s in range(min(self.ring, self.n_dma[e])):
                    last = (self.n_dma[e] - 1 - s) // self.ring + 1
                    eng.wait_ge(dsem[e][s], 16 * last)

        with nc.Block() as block:
            @block.tensor
            def _(eng):
                run("pe", eng)

            @block.scalar
            def _(eng):
                run("act", eng)

            @block.vector
            def _(eng):
                run("dve", eng)

            @block.gpsimd
            def _(eng):
                run("pool", eng)

            @block.sync
            def _(eng):
                run("sp", eng)


D = 2048
KC = 16
TC = 4096
TT = 512
NT = TC // TT
EPS = 1e-6
SLABW = 256


class WStream:
    def __init__(self, P, srcs, ring=4, q="sp", name="ws"):
        self.P, self.srcs, self.ring, self.q, self.name = P, srcs, ring, q, name
        self.bufs = [P.sb([128, KC, SLABW], BF16, name=f"{name}{i}") for i in range(ring)]
        self.issued = 0
        self.pos = 0

    def _issue(self):
        i = self.issued
        if i >= len(self.srcs):
            return
        s = i % self.ring
        self.P.dma(self.q, self.bufs[s][:], self.srcs[i], writes=[(self.name, s)])
        self.issued += 1

    def prefetch(self, n=None):
        n = self.ring if n is None else n
        while self.issued < min(self.pos + n, len(self.srcs)):
            self._issue()

    def next(self):
        self.prefetch(self.ring)
        i = self.pos
        self.pos += 1
        s = i % self.ring
        return self.bufs[s], (self.name, s)


def slab_src(wb, n0):
    return wb.rearrange("(k p) n -> p k n", p=128)[:, :, n0:n0 + SLABW]


def convert_w(P, src, dst, K, N, tag):
    CB = 2048
    st32 = [P.sb([128, CB], F32) for _ in range(2)]
    st16 = [P.sb([128, CB], BF16) for _ in range(2)]
    i = 0
    for kb in range(K // 128):
        for c0 in range(0, N, CB):
            cw = min(CB, N - c0)
            b = i % 2
            P.dma("sp", st32[b][:, :cw], src[kb * 128:(kb + 1) * 128, c0:c0 + cw], writes=[(tag, "s32", b)])
            eng = ("dve", "pool", "act")[i % 3]
            if eng == "act":
                P.act(lambda e, b=b, cw=cw: e.copy(out=st16[b][:, :cw], in_=st32[b][:, :cw]),
                      reads=[(tag, "s32", b)], writes=[(tag, "s16", b)])
            else:
                P.add(eng, lambda e, b=b, cw=cw: e.tensor_copy(out=st16[b][:, :cw], in_=st32[b][:, :cw]),
                      reads=[(tag, "s32", b)], writes=[(tag, "s16", b)])
            P.dma("pool", dst[kb * 128:(kb + 1) * 128, c0:c0 + cw], st16[b][:, :cw],
                  reads=[(tag, "s16", b)], writes=[(tag, "dram", i)])
            i += 1


class Common:
    def __init__(self, P, mod_ap):
        self.P = P
        self.ones = P.sb([128, 128], BF16, name="ones")
        P.dve(lambda e: e.memset(self.ones[:], 1.0), writes=["ones"])
        self.mod = P.sb([128, 3, KC], F32, name="mod")
        P.dma("sp", self.mod[:], mod_ap, writes=["mod"])
        self.s1 = P.sb([128, KC], F32, name="s1")
        P.dve(lambda e: e.tensor_scalar_add(out=self.s1[:], in0=self.mod[:, 1, :], scalar1=1.0),
              reads=["mod"], writes=["s1"])

    def shift(self, k):
        return self.mod[:, 0, k:k + 1]

    def gate(self, k):
        return self.mod[:, 2, k:k + 1]


def emit_norm_mod(P, C, xT, c_lo, ncol, hT, hkey, bufs):
    xs, sq, tt, ss_a, ss_b, rstd = bufs["xs"], bufs["sq"], bufs["tt"], bufs["ss_a"], bufs["ss_b"], bufs["rstd"]
    na = min(ncol, 512)
    nb = ncol - na
    for k in range(KC):
        b = k % len(xs)
        P.dma("pool", xs[b][:, :ncol], xT[k * 128:(k + 1) * 128, c_lo:c_lo + ncol], writes=[("xs", b)])
        sb_ = k % len(sq)
        P.act(lambda e, b=b, sb_=sb_: e.activation(out=sq[sb_][:, :ncol], in_=xs[b][:, :ncol], func=AF.Square),
              reads=[("xs", b)], writes=[("sq", sb_)])
        P.pe(lambda e, sb_=sb_, k=k: e.matmul(ss_a[:, :na], lhsT=C.ones[:], rhs=sq[sb_][:, :na],
                                              start=(k == 0), stop=(k == KC - 1)),
             reads=["ones", ("sq", sb_)], writes=["ss_a"])
        if nb:
            P.pe(lambda e, sb_=sb_, k=k: e.matmul(ss_b[:, :nb], lhsT=C.ones[:], rhs=sq[sb_][:, na:ncol],
                                                  start=(k == 0), stop=(k == KC - 1)),
                 reads=["ones", ("sq", sb_)], writes=["ss_b"])
    P.dve(lambda e: e.tensor_scalar(out=rstd[:, :na], in0=ss_a[:, :na], scalar1=1.0 / D, scalar2=EPS,
                                    op0=ALU.mult, op1=ALU.add), reads=["ss_a"], writes=["rstd"])
    if nb:
        P.dve(lambda e: e.tensor_scalar(out=rstd[:, na:ncol], in0=ss_b[:, :nb], scalar1=1.0 / D, scalar2=EPS,
                                        op0=ALU.mult, op1=ALU.add), reads=["ss_b", "rstd"], writes=["rstd"])
    P.dve(lambda e: e.reciprocal(out=rstd[:, :ncol], in_=rstd[:, :ncol]), reads=["rstd"], writes=["rstd"])
    P.act(lambda e: e.activation(out=rstd[:, :ncol], in_=rstd[:, :ncol], func=AF.Sqrt), reads=["rstd"], writes=["rstd"])
    for k in range(KC):
        b = k % len(xs)
        P.dma("pool", xs[b][:, :ncol], xT[k * 128:(k + 1) * 128, c_lo:c_lo + ncol], writes=[("xs", b)])
        tb = k % len(tt)
        P.dve(lambda e, b=b, tb=tb, k=k: e.scalar_tensor_tensor(
            out=tt[tb][:, :ncol], in0=xs[b][:, :ncol], scalar=C.s1[:, k:k + 1], in1=rstd[:, :ncol],
            op0=ALU.mult, op1=ALU.mult), reads=[("xs", b), "s1", "rstd"], writes=[("tt", tb)])
        P.act(lambda e, tb=tb, k=k: e.activation(out=hT[:, k, :ncol], in_=tt[tb][:, :ncol], func=AF.Identity,
                                                 bias=C.shift(k), scale=1.0),
              reads=[("tt", tb), "mod"], writes=[(hkey, k)])


def norm_bufs(P, width=528):
    return dict(
        xs=[P.sb([128, width], F32) for _ in range(4)],
        sq=[P.sb([128, width], BF16) for _ in range(2)],
        tt=[P.sb([128, width], F32) for _ in range(2)],
        ss_a=P.ps([128, 512], F32), ss_b=(P.ps([128, 16], F32) if width > 512 else None),
        rstd=P.sb([128, width], F32),
    )


def emit_pool_layer(P, nc, io, final=False, pfx="", wb=None):
    xT = io["xT"]
    m0 = P.mark()
    if wb is None:
        wb_in = nc.dram_tensor(pfx + "wb_in", [D, 4096], BF16).ap()
        wb_grp = nc.dram_tensor(pfx + "wb_grp", [D, 512], BF16).ap()
        wb_out = nc.dram_tensor(pfx + "wb_out", [D, D], BF16).ap()
        convert_w(P, io["w_in"], wb_in, D, 4096, "cw_in")
        convert_w(P, io["w_grp"], wb_grp, D, 512, "cw_grp")
        convert_w(P, io["w_out"], wb_out, D, D, "cw_out")
        wb = (wb_in, wb_grp, wb_out)
    wb_in, wb_grp, wb_out = wb
    P.release(m0)

    C = Common(P, io["mod"])
    pscale = P.sb([128, KC], F32, name="pscale")
    P.dma("sp", pscale[:], io["pscale"], writes=["pscale"])
    hmask = P.sb([128, 2], F32, name="hmask")
    P.dma("sp", hmask[:], io["hmask"], writes=["hmask"])
    icnt = P.sb([128, 4, 16], F32, name="icnt")
    P.dma("sp", icnt[:], io["icnt"], writes=["icnt"])
    if final:
        fg = P.sb([128, KC], F32, name="fg")
        P.dma("sp", fg[:], io["fg"], writes=["fg"])
    wg = P.sb([128, 4, 4, 512], BF16, name="wg")
    for g in range(4):
        P.dma("sp", wg[:, g, :, :], wb_grp[g * 512:(g + 1) * 512, :].rearrange("(k p) n -> p k n", p=128),
              writes=[("wg", g)])

    order = []
    for gi in range(4):
        order += [slab_src(wb_in, 512 * gi), slab_src(wb_in, 512 * gi + 256)]
        order += [slab_src(wb_in, 2048 + 512 * gi), slab_src(wb_in, 2048 + 512 * gi + 256)]
    for j in range(8):
        order.append(slab_src(wb_out, 256 * j))
    ws = WStream(P, order * NT, ring=4)

    nb = norm_bufs(P)
    hT = [P.sb([128, KC, 528], BF16, name=f"hT{i}") for i in range(2)]
    U = P.sb([128, 4, 528], F32, name="U")
    Ta = P.sb([128, 4, 528], F32, name="Ta")
    Tb = P.sb([128, 4, 528], F32, name="Tb")
    sg = P.sb([128, 4, 512], BF16, name="sg")
    pooled = P.sb([128, 4, 512], BF16, name="pooled")
    z = P.sb([128, KC, 512], BF16, name="z")
    xres = [P.sb([128, 512], F32) for _ in range(3)]
    xo_t = [P.sb([128, 512], F32) for _ in range(3)]
    mm = [P.ps([128, 512], F32) for _ in range(4)]
    psB = [P.ps([128, 4, 16], F32) for _ in range(2)]
    if final:
        xn = P.sb([128, KC, 512], F32, name="xn")
        fsq = [P.sb([128, 512], BF16) for _ in range(2)]
    mmi = [0]
    UK = [("U", i) for i in range(4)] + ["Uh"]

    def next_mm():
        i = mmi[0] % len(mm)
        mmi[0] += 1
        return mm[i], ("mm", i)

    def hk(hb, k):
        return (("hT", hb), k)

    def mixer(t):
        hb = t % 2
        h = hT[hb]
        for gi in range(4):
            w = 2 << gi
            pb = psB[gi % 2]
            pbk = ("psB", gi % 2)
            for sl in range(2):
                slab, skey = ws.next()
                for ml in range(2):
                    m4 = sl * 2 + ml
                    ps, pk = next_mm()
                    for k in range(KC):
                        P.pe(lambda e, ps=ps, slab=slab, k=k, ml=ml, h=h: e.matmul(
                            ps[:], lhsT=slab[:, k, ml * 128:(ml + 1) * 128], rhs=h[:, k, 0:512],
                            start=(k == 0), stop=(k == KC - 1)), reads=[skey, hk(hb, k)], writes=[pk])
                        P.pe(lambda e, pb=pb, slab=slab, k=k, ml=ml, h=h, m4=m4: e.matmul(
                            pb[:, m4, :], lhsT=slab[:, k, ml * 128:(ml + 1) * 128], rhs=h[:, k, 512:528],
                            start=(k == 0), stop=(k == KC - 1)), reads=[skey, hk(hb, k)], writes=[pbk])
                    P.act(lambda e, ps=ps, m4=m4: e.copy(out=U[:, m4, 0:512], in_=ps[:]),
                          reads=[pk], writes=[("U", m4)])
            P.dve(lambda e, pb=pb: e.tensor_copy(out=U[:, :, 512:528], in_=pb[:]), reads=[pbk], writes=["Uh"])
            if t == 0:
                P.dve(lambda e: e.tensor_scalar(out=U[:, :, 0:8], in0=U[:, :, 0:8], scalar1=hmask[:, 0:1],
                                                scalar2=None, op0=ALU.mult), reads=["hmask"] + UK, writes=UK)
            if t == NT - 1:
                P.dve(lambda e: e.tensor_scalar(out=U[:, :, 520:528], in0=U[:, :, 520:528], scalar1=hmask[:, 1:2],
                                                scalar2=None, op0=ALU.mult), reads=["hmask", "Uh"], writes=["Uh"])
            for sl in range(2):
                slab, skey = ws.next()
                for ml in range(2):
                    m4 = sl * 2 + ml
                    ps, pk = next_mm()
                    for k in range(KC):
                        P.pe(lambda e, ps=ps, slab=slab, k=k, ml=ml, h=h: e.matmul(
                            ps[:], lhsT=slab[:, k, ml * 128:(ml + 1) * 128], rhs=h[:, k, 8:520],
                            start=(k == 0), stop=(k == KC - 1)), reads=[skey, hk(hb, k)], writes=[pk])
                    P.act(lambda e, ps=ps, m4=m4: e.activation(out=sg[:, m4, :], in_=ps[:], func=AF.Silu),
                          reads=[pk], writes=[("sg", m4)])
            P.dve(lambda e: e.tensor_tensor(out=Ta[:, :, 1:528], in0=U[:, :, 0:527], in1=U[:, :, 1:528], op=ALU.add),
                  reads=UK, writes=["Ta"])
            S, skey_ = Ta, "Ta"
            if gi >= 1:
                P.dve(lambda e: e.tensor_tensor(out=Tb[:, :, 2:527], in0=Ta[:, :, 1:526], in1=Ta[:, :, 3:528],
                                                op=ALU.add), reads=["Ta"], writes=["Tb"])
                S, skey_ = Tb, "Tb"
            if gi >= 2:
                P.dve(lambda e: e.tensor_tensor(out=Ta[:, :, 4:525], in0=Tb[:, :, 2:523], in1=Tb[:, :, 6:527],
                                                op=ALU.add), reads=["Tb"], writes=["Ta"])
                S, skey_ = Ta, "Ta"
            if gi >= 3:
                P.dve(lambda e: e.tensor_tensor(out=Tb[:, :, 8:521], in0=Ta[:, :, 4:517], in1=Ta[:, :, 12:525],
                                                op=ALU.add), reads=["Ta"], writes=["Tb"])
                S, skey_ = Tb, "Tb"
            P.dve(lambda e, S=S, w=w: e.scalar_tensor_tensor(
                out=pooled[:], in0=S[:, :, 8:520], scalar=1.0 / w, in1=U[:, :, 8:520],
                op0=ALU.mult, op1=ALU.subtract), reads=[skey_] + UK, writes=["pooled"])
            if t == 0 or t == NT - 1:
                a, ic = (8, 0) if t == 0 else (512, 8)
                O, okey = (Ta, "Ta") if S is Tb else (Tb, "Tb")
                P.dve(lambda e, S=S, O=O, a=a, ic=ic, gi=gi: e.tensor_tensor(
                    out=O[:, :, 0:8], in0=S[:, :, a:a + 8],
                    in1=icnt[:, gi:gi + 1, ic:ic + 8].to_broadcast([128, 4, 8]), op=ALU.mult),
                    reads=[skey_, "icnt"], writes=[okey])
                P.dve(lambda e, O=O, a=a: e.tensor_tensor(
                    out=pooled[:, :, a - 8:a], in0=O[:, :, 0:8], in1=U[:, :, a:a + 8], op=ALU.subtract),
                    reads=[okey] + UK, writes=["pooled"])
            for m4 in range(4):
                ps, pk = next_mm()
                for k4 in range(4):
                    P.pe(lambda e, ps=ps, gi=gi, k4=k4, m4=m4: e.matmul(
                        ps[:], lhsT=wg[:, gi, k4, m4 * 128:(m4 + 1) * 128], rhs=pooled[:, k4, :],
                        start=(k4 == 0), stop=(k4 == 3)), reads=[("wg", gi), "pooled"], writes=[pk])
                mg = gi * 4 + m4
                P.dve(lambda e, ps=ps, mg=mg, m4=m4: e.scalar_tensor_tensor(
                    out=z[:, mg, :], in0=ps[:], scalar=pscale[:, mg:mg + 1], in1=sg[:, m4, :],
                    op0=ALU.mult, op1=ALU.mult), reads=[pk, "pscale", ("sg", m4)], writes=[("z", mg)])

    def outproj(t):
        c0 = 8 + TT * t

        def load_res(m):
            rb = m % len(xres)
            P.dma("pool", xres[rb][:], xT[m * 128:(m + 1) * 128, c0:c0 + 512], writes=[("xres", rb)])
        for m in range(min(len(xres), KC)):
            load_res(m)
        for sl in range(8):
            slab, skey = ws.next()
            for ml in range(2):
                m = sl * 2 + ml
                ps, pk = next_mm()
                for k in range(KC):
                    P.pe(lambda e, ps=ps, slab=slab, k=k, ml=ml: e.matmul(
                        ps[:], lhsT=slab[:, k, ml * 128:(ml + 1) * 128], rhs=z[:, k, :],
                        start=(k == 0), stop=(k == KC - 1)), reads=[skey, ("z", k)], writes=[pk])
                rb = m % len(xres)
                if not final:
                    ob = m % len(xo_t)
                    P.dve(lambda e, ps=ps, m=m, rb=rb, ob=ob: e.scalar_tensor_tensor(
                        out=xo_t[ob][:], in0=ps[:], scalar=C.gate(m), in1=xres[rb][:],
                        op0=ALU.mult, op1=ALU.add), reads=[pk, "mod", ("xres", rb)], writes=[("xo_t", ob)])
                    P.dma("pool", io["xo"][m * 128:(m + 1) * 128, TT * t:TT * (t + 1)], xo_t[ob][:],
                          reads=[("xo_t", ob)], writes=[("xo", t, m)])
                else:
                    P.dve(lambda e, ps=ps, m=m, rb=rb: e.scalar_tensor_tensor(
                        out=xn[:, m, :], in0=ps[:], scalar=C.gate(m), in1=xres[rb][:],
                        op0=ALU.mult, op1=ALU.add), reads=[pk, "mod", ("xres", rb)], writes=[("xn", m)])
                if m + len(xres) < KC:
                    load_res(m + len(xres))
        if final:
            ss = nb["ss_a"]
            for m in range(KC):
                fb = m % 2
                P.act(lambda e, m=m, fb=fb: e.activation(out=fsq[fb][:], in_=xn[:, m, :], func=AF.Square),
                      reads=[("xn", m)], writes=[("fsq", fb)])
                P.pe(lambda e, m=m, fb=fb: e.matmul(ss[:], lhsT=C.ones[:], rhs=fsq[fb][:],
                                                    start=(m == 0), stop=(m == KC - 1)),
                     reads=["ones", ("fsq", fb)], writes=["ss_a"])
            rstd = nb["rstd"]
            P.dve(lambda e: e.tensor_scalar(out=rstd[:, :512], in0=ss[:], scalar1=1.0 / D, scalar2=EPS,
                                            op0=ALU.mult, op1=ALU.add), reads=["ss_a"], writes=["rstd"])
            P.dve(lambda e: e.reciprocal(out=rstd[:, :512], in_=rstd[:, :512]), reads=["rstd"], writes=["rstd"])
            P.act(lambda e: e.activation(out=rstd[:, :512], in_=rstd[:, :512], func=AF.Sqrt),
                  reads=["rstd"], writes=["rstd"])
            for m in range(KC):
                ob = m % len(xo_t)
                P.dve(lambda e, m=m, ob=ob: e.scalar_tensor_tensor(
                    out=xo_t[ob][:], in0=xn[:, m, :], scalar=fg[:, m:m + 1], in1=rstd[:, :512],
                    op0=ALU.mult, op1=ALU.mult), reads=[("xn", m), "fg", "rstd"], writes=[("xo_t", ob)])
                P.dma("pool", io["xo"][m * 128:(m + 1) * 128, TT * t:TT * (t + 1)], xo_t[ob][:],
                      reads=[("xo_t", ob)], writes=[("xo", t, m)])

    def norm(t):
        emit_norm_mod(P, C, xT, TT * t, 528, hT[t % 2], ("hT", t % 2), nb)

    norm(0)
    for t in range(NT):
        mixer(t)
        if t + 1 < NT and not final:
            norm(t + 1)
        outproj(t)
        if t + 1 < NT and final:
            norm(t + 1)
    P.release(m0)
    return wb


AQ, AKV = 2048, 512
A_SCALE = 128 ** -0.5


def emit_outproj_phase(P, nc, C, ws, xT, xo, Zs, x_off, mm_ring, final_io=None):
    z = [P.sb([128, KC, 512], BF16) for _ in range(2)]
    xres = [P.sb([128, 512], F32) for _ in range(3)]
    xo_t = [P.sb([128, 512], F32) for _ in range(3)]
    mmi = [0]

    def next_mm():
        i = mmi[0] % len(mm_ring)
        mmi[0] += 1
        return mm_ring[i], ("mm", i)

    for t in range(NT):
        zb = t % 2
        for k in range(KC):
            P.dma("pool", z[zb][:, k, :], Zs[k * 128:(k + 1) * 128, t * TT:(t + 1) * TT],
                  reads=[("Zs", k, t)], writes=[("zc", zb, k)])
        c0 = x_off + TT * t

        def load_res(m):
            rb = m % len(xres)
            P.dma("pool", xres[rb][:], xT[m * 128:(m + 1) * 128, c0:c0 + 512], writes=[("xres", rb)])
        for m in range(len(xres)):
            load_res(m)
        for sl in range(8):
            slab, skey = ws.next()
            for ml in range(2):
                m = sl * 2 + ml
                ps, pk = next_mm()
                for k in range(KC):
                    P.pe(lambda e, ps=ps, slab=slab, k=k, ml=ml, zb=zb: e.matmul(
                        ps[:], lhsT=slab[:, k, ml * 128:(ml + 1) * 128], rhs=z[zb][:, k, :],
                        start=(k == 0), stop=(k == KC - 1)), reads=[skey, ("zc", zb, k)], writes=[pk])
                rb = m % len(xres)
                ob = m % len(xo_t)
                P.dve(lambda e, ps=ps, m=m, rb=rb, ob=ob: e.scalar_tensor_tensor(
                    out=xo_t[ob][:], in0=ps[:], scalar=C.gate(m), in1=xres[rb][:],
                    op0=ALU.mult, op1=ALU.add), reads=[pk, "mod", ("xres", rb)], writes=[("xo_t", ob)])
                P.dma("pool", xo[m * 128:(m + 1) * 128, TT * t:TT * (t + 1)], xo_t[ob][:],
                      reads=[("xo_t", ob)], writes=[("xo", t, m)])
                if m + len(xres) < KC:
                    load_res(m + len(xres))


def emit_attn_layer(P, nc, io, variant):
    xT = io["xT"]
    full = variant == "full"
    m0 = P.mark()
    wb_in = nc.dram_tensor("awb_in", [D, 5120], BF16).ap()
    convert_w(P, io["w_in"], wb_in, D, 5120, "acw_in")
    if full:
        wb_out = nc.dram_tensor("awb_out", [D, D], BF16).ap()
        convert_w(P, io["w_out"], wb_out, D, D, "acw_out")
        Qs = nc.dram_tensor("aQs", [16, 128, TC], BF16).ap()
        SG = nc.dram_tensor("aSG", [D, TC], BF16).ap()
        Zs = nc.dram_tensor("aZs", [D, TC], BF16).ap()
    KT_own, V_own = io["KT_own"], io["V_own"]
    P.release(m0)

    C = Common(P, io["mod"])
    gains = P.sb([128, 2], F32, name="gains")
    P.dma("sp", gains[:], io["gains"], writes=["gains"])
    rot32 = P.sb([128, 128], F32, name="rot32")
    P.dma("sp", rot32[:], io["rotT"], writes=["rot32"])
    rotb = P.sb([128, 128], BF16, name="rotb")
    P.dve(lambda e: e.tensor_copy(out=rotb[:], in_=rot32[:]), reads=["rot32"], writes=["rotb"])

    order = []
    if full:
        order += [slab_src(wb_in, 256 * j) for j in range(8)]
    order += [slab_src(wb_in, 2048 + 256 * j) for j in range(4)]
    if full:
        order += [slab_src(wb_in, 3072 + 256 * j) for j in range(8)]
    ws = WStream(P, order * NT, ring=4)
    nb = norm_bufs(P, 512)
    hT = [P.sb([128, KC, 512], BF16, name=f"ahT{i}") for i in range(2)]
    cs = [P.sb([128, 512], F32) for _ in range(2)]
    sn = [P.sb([128, 512], F32) for _ in range(2)]
    qg_t = [P.sb([128, 512], BF16) for _ in range(2)]
    sq_t = [P.sb([128, 512], BF16) for _ in range(2)]
    rs_t = [P.sb([128, 512], F32) for _ in range(2)]
    t1 = [P.sb([128, 512], F32) for _ in range(2)]
    t2 = [P.sb([128, 512], F32) for _ in range(2)]
    qo = [P.sb([128, 512], BF16) for _ in range(3)]
    vbuf = [P.sb([128, 4, 512], BF16) for _ in range(2)]
    sgt = [P.sb([128, 512], BF16) for _ in range(3)]
    mm = [P.ps([128, 512], F32) for _ in range(3)]
    ssq = [P.ps([128, 512], F32) for _ in range(2)]
    rotp = [P.ps([128, 512], F32) for _ in range(2)]
    cnt = {"mm": 0, "hd": 0, "qo": 0, "sg": 0}

    def next_mm():
        i = cnt["mm"] % len(mm)
        cnt["mm"] += 1
        return mm[i], ("mm", i)

    def qk_head(t, hb, slab, skey, ml, gcol, dst, dkey):
        h = hT[hb]
        ps, pk = next_mm()
        for k in range(KC):
            P.pe(lambda e, ps=ps, slab=slab, k=k, ml=ml, h=h: e.matmul(
                ps[:], lhsT=slab[:, k, ml * 128:(ml + 1) * 128], rhs=h[:, k, :],
                start=(k == 0), stop=(k == KC - 1)), reads=[skey, (("ahT", hb), k)], writes=[pk])
        i2 = cnt["hd"] % 2
        cnt["hd"] += 1
        P.act(lambda e, ps=ps, i2=i2: e.activation(out=qg_t[i2][:], in_=ps[:], func=AF.Copy,
                                                   scale=gains[:, gcol:gcol + 1]),
              reads=[pk, "gains"], writes=[("qg", i2)])
        P.act(lambda e, ps=ps, i2=i2: e.activation(out=sq_t[i2][:], in_=ps[:], func=AF.Square),
              reads=[pk], writes=[("sq_t", i2)])
        P.pe(lambda e, i2=i2: e.matmul(ssq[i2][:], lhsT=C.ones[:], rhs=sq_t[i2][:], start=True, stop=True),
             reads=["ones", ("sq_t", i2)], writes=[("ssq", i2)])
        P.pe(lambda e, i2=i2: e.matmul(rotp[i2][:], lhsT=rotb[:], rhs=qg_t[i2][:], start=True, stop=True),
             reads=["rotb", ("qg", i2)], writes=[("rotp", i2)])
        P.act(lambda e, i2=i2: e.activation(out=rs_t[i2][:], in_=ssq[i2][:], func=AF.Ln, scale=1.0 / 128, bias=EPS),
              reads=[("ssq", i2)], writes=[("rs_t", i2)])
        P.act(lambda e, i2=i2: e.activation(out=rs_t[i2][:], in_=rs_t[i2][:], func=AF.Exp, scale=-0.5),
              reads=[("rs_t", i2)], writes=[("rs_t", i2)])
        cb = t % 2
        P.pool(lambda e, i2=i2, cb=cb: e.tensor_tensor(out=t1[i2][:], in0=qg_t[i2][:], in1=cs[cb][:], op=ALU.mult),
               reads=[("qg", i2), ("cs", cb)], writes=[("t1", i2)])
        P.dve(lambda e, i2=i2, cb=cb: e.tensor_tensor(out=t2[i2][:], in0=rotp[i2][:], in1=sn[cb][:], op=ALU.mult),
              reads=[("rotp", i2), ("sn", cb)], writes=[("t2", i2)])
        P.pool(lambda e, i2=i2: e.tensor_tensor(out=t1[i2][:], in0=t1[i2][:], in1=t2[i2][:], op=ALU.add),
               reads=[("t1", i2), ("t2", i2)], writes=[("t1", i2)])
        i3 = cnt["qo"] % 3
        cnt["qo"] += 1
        P.dve(lambda e, i2=i2, i3=i3: e.tensor_tensor(out=qo[i3][:], in0=t1[i2][:], in1=rs_t[i2][:], op=ALU.mult),
              reads=[("t1", i2), ("rs_t", i2)], writes=[("qo", i3)])
        P.dma("pool", dst, qo[i3][:], reads=[("qo", i3)], writes=[dkey])

    def projections(t):
        hb = t % 2
        h = hT[hb]
        cb = t % 2
        P.dma("sp", cs[cb][:], io["cos"][:, t * TT:(t + 1) * TT], writes=[("cs", cb)])
        P.dma("sp", sn[cb][:], io["sin"][:, t * TT:(t + 1) * TT], writes=[("sn", cb)])
        if full:
            for sl in range(8):
                slab, skey = ws.next()
                for ml in range(2):
                    hd = sl * 2 + ml
                    qk_head(t, hb, slab, skey, ml, 0, Qs[hd][:, t * TT:(t + 1) * TT], ("Qs", hd, t))
        for sl in range(2):
            slab, skey = ws.next()
            for ml in range(2):
                kh = sl * 2 + ml
                qk_head(t, hb, slab, skey, ml, 1, KT_own[kh][:, t * TT:(t + 1) * TT], ("KT", kh, t))
        vb = t % 2
        for sl in range(2):
            slab, skey = ws.next()
            for j in range(4):
                ps, pk = next_mm()
                for k in range(KC):
                    P.pe(lambda e, ps=ps, slab=slab, k=k, j=j, h=h: e.matmul(
                        ps[:, 0:256], lhsT=h[:, k, j * 128:(j + 1) * 128], rhs=slab[:, k, :],
                        start=(k == 0), stop=(k == KC - 1)), reads=[skey, (("ahT", hb), k)], writes=[pk])
                P.act(lambda e, ps=ps, j=j, sl=sl, vb=vb: e.copy(out=vbuf[vb][:, j, sl * 256:(sl + 1) * 256],
                                                                 in_=ps[:, 0:256]),
                      reads=[pk], writes=[("vbuf", vb, j, sl)])
        P.dma("pool", V_own[t * TT:(t + 1) * TT, :].rearrange("(j p) n -> p j n", p=128), vbuf[vb][:],
              reads=[("vbuf", vb, j, sl) for j in range(4) for sl in range(2)], writes=[("V", t)])
        if full:
            for sl in range(8):
                slab, skey = ws.next()
                for ml in range(2):
                    m = sl * 2 + ml
                    ps, pk = next_mm()
                    for k in range(KC):
                        P.pe(lambda e, ps=ps, slab=slab, k=k, ml=ml, h=h: e.matmul(
                            ps[:], lhsT=slab[:, k, ml * 128:(ml + 1) * 128], rhs=h[:, k, :],
                            start=(k == 0), stop=(k == KC - 1)), reads=[skey, (("ahT", hb), k)], writes=[pk])
                    i3 = cnt["sg"] % 3
                    cnt["sg"] += 1
                    P.act(lambda e, ps=ps, i3=i3: e.activation(out=sgt[i3][:], in_=ps[:], func=AF.Silu),
                          reads=[pk], writes=[("sgt", i3)])
                    P.dma("pool", SG[m * 128:(m + 1) * 128, t * TT:(t + 1) * TT], sgt[i3][:],
                          reads=[("sgt", i3)], writes=[("SG", m, t)])

    def norm(t):
        emit_norm_mod(P, C, xT, TT * t, 512, hT[t % 2], ("ahT", t % 2), nb)

    norm(0)
    for t in range(NT):
        if t + 1 < NT:
            norm(t + 1)
        projections(t)
    if not full:
        return
    P.release(m0)

    C = Common(P, io["mod"])
    ones32 = P.sb([128, 128], F32, name="ones32")
    P.dve(lambda e: e.memset(ones32[:], 1.0), writes=["ones32"])
    kt = [P.sb([128, 2 * TC], BF16, name=f"kt{i}") for i in range(2)]
    vt = [P.sb([128, 64, 128], BF16, name=f"vt{i}") for i in range(2)]
    qt = [P.sb([128, 512], BF16) for _ in range(3)]
    sgq = [P.sb([128, 512], BF16) for _ in range(3)]
    pt = [P.sb([128, 512], BF16) for _ in range(4)]
    accD = [P.sb([128, 512], F32) for _ in range(2)]
    accP = [P.sb([128, 512], F32) for _ in range(2)]
    rinv = [P.sb([128, 512], F32) for _ in range(2)]
    zt = [P.sb([128, 512], BF16) for _ in range(2)]
    st = [P.ps([128, 512], F32) for _ in range(3)]
    ot = [P.ps([128, 512], F32) for _ in range(2)]
    rsp = [P.ps([128, 512], F32) for _ in range(2)]
    KT_all, V_all = io["KT_all"], io["V_all"]
    it = 0
    sti = 0
    pti = 0
    for kvh in range(4):
        kb = kvh % 2
        for r in range(2):
            P.dma("sp", kt[kb][:, r * TC:(r + 1) * TC], KT_all[r, kvh], writes=[("kt", kb, r)])
            P.dma("sp", vt[kb][:, r * 32:(r + 1) * 32, :],
                  V_all[r][:, kvh * 128:(kvh + 1) * 128].rearrange("(c p) d -> p c d", p=128),
                  writes=[("vt", kb, r)])
        ktk = [("kt", kb, 0), ("kt", kb, 1)]
        vtk = [("vt", kb, 0), ("vt", kb, 1)]
        for gq_ in range(4):
            hd = kvh * 4 + gq_
            for t in range(NT):
                ab = it % 2
                q3 = it % 3
                it += 1
                P.dma("sp", qt[q3][:], Qs[hd][:, t * TT:(t + 1) * TT], reads=[("Qs", hd, t)], writes=[("qt", q3)])
                P.dma("sp", sgq[q3][:], SG[hd * 128:(hd + 1) * 128, t * TT:(t + 1) * TT],
                      reads=[("SG", hd, t)], writes=[("sgq", q3)])
                for sc in range(64):
                    s3 = sti % 3
                    sti += 1
                    p4 = pti % 4
                    pti += 1
                    P.pe(lambda e, s3=s3, kb=kb, sc=sc, q3=q3: e.matmul(
                        st[s3][:], lhsT=kt[kb][:, sc * 128:(sc + 1) * 128], rhs=qt[q3][:], start=True, stop=True),
                        reads=[ktk[sc // 32], ("qt", q3)], writes=[("st", s3)])
                    P.act(lambda e, s3=s3, p4=p4: e.activation(out=pt[p4][:], in_=st[s3][:], func=AF.Exp, scale=A_SCALE),
                          reads=[("st", s3)], writes=[("pt", p4)])
                    P.pe(lambda e, ab=ab, kb=kb, sc=sc, p4=p4: e.matmul(
                        ot[ab][:], lhsT=vt[kb][:, sc, :], rhs=pt[p4][:], start=(sc == 0), stop=(sc == 63)),
                        reads=[vtk[sc // 32], ("pt", p4)], writes=[("ot", ab)])
                    eng, acc, akey = ("dve", accD[ab], ("accD", ab)) if sc % 2 == 0 else ("pool", accP[ab], ("accP", ab))
                    if sc < 2:
                        P.add(eng, lambda e, acc=acc, p4=p4: e.tensor_copy(out=acc[:], in_=pt[p4][:]),
                              reads=[("pt", p4)], writes=[akey])
                    else:
                        P.add(eng, lambda e, acc=acc, p4=p4: e.tensor_tensor(out=acc[:], in0=acc[:], in1=pt[p4][:], op=ALU.add),
                              reads=[("pt", p4), akey], writes=[akey])
                P.pe(lambda e, ab=ab: e.matmul(rsp[ab][:], lhsT=ones32[:], rhs=accD[ab][:], start=True, stop=False),
                     reads=["ones32", ("accD", ab)], writes=[("rsp", ab)])
                P.pe(lambda e, ab=ab: e.matmul(rsp[ab][:], lhsT=ones32[:], rhs=accP[ab][:], start=False, stop=True),
                     reads=["ones32", ("accP", ab)], writes=[("rsp", ab)])
                P.dve(lambda e, ab=ab: e.reciprocal(out=rinv[ab][:], in_=rsp[ab][:]), reads=[("rsp", ab)], writes=[("rinv", ab)])
                P.pool(lambda e, ab=ab, q3=q3: e.tensor_tensor(out=rinv[ab][:], in0=rinv[ab][:], in1=sgq[q3][:], op=ALU.mult),
                       reads=[("rinv", ab), ("sgq", q3)], writes=[("rinv", ab)])
                P.dve(lambda e, ab=ab: e.tensor_tensor(out=zt[ab][:], in0=ot[ab][:], in1=rinv[ab][:], op=ALU.mult),
                      reads=[("ot", ab), ("rinv", ab)], writes=[("zt", ab)])
                P.dma("pool", Zs[hd * 128:(hd + 1) * 128, t * TT:(t + 1) * TT], zt[ab][:],
                      reads=[("zt", ab)], writes=[("Zs", hd, t)])
    P.release(m0)

    C = Common(P, io["mod"])
    ws2 = WStream(P, [slab_src(wb_out, 256 * j) for j in range(8)] * NT, ring=4, name="wsC")
    mmC = [P.ps([128, 512], F32) for _ in range(4)]
    emit_outproj_phase(P, nc, C, ws2, xT, io["xo"], Zs, 0, mmC)


GK = 1024
GV = 2048
NCH = TC // 128
LN_QS = -2.772588722239781


def emit_gla_layer(P, nc, io, variant, pfx="", wb=None, scr=None):
    xT = io["xT"]
    full = variant in ("full", "sweep3")
    mtop = P.mark()
    if variant == "sweep3":
        m0 = mtop
        wb_in, wb_out = wb
        QC, OT, SG = scr["QC"], scr["OT"], scr["SG"]
    else:
        el = {d: P.sb([128, 8, NCH], F32, name=f"el_{d}") for d in "fb"}
        m0 = P.mark()
        if wb is None:
            wb_in = nc.dram_tensor(pfx + "gwb_in", [D, 6144], BF16).ap()
            convert_w(P, io["w_in"], wb_in, D, 6144, "gcw_in")
            wb_out = nc.dram_tensor(pfx + "gwb_out", [D, D], BF16).ap()
            convert_w(P, io["w_out"], wb_out, D, D, "gcw_out")
            wb = (wb_in, wb_out)
        wb_in, wb_out = wb
        QT = {d: nc.dram_tensor(pfx + f"gQ{d}", [GK, TC], BF16).ap() for d in "fb"}
        KT = {d: nc.dram_tensor(pfx + f"gK{d}", [GK, TC], BF16).ap() for d in "fb"}
        KH = {d: nc.dram_tensor(pfx + f"gKH{d}", [TC, GK], BF16).ap() for d in "fb"}
        QC = {d: nc.dram_tensor(pfx + f"gQC{d}", [GK, TC], BF16).ap() for d in "fb"}
        OT = {d: nc.dram_tensor(pfx + f"gO{d}", [GV, TC], F32).ap() for d in "fb"}
        Vs = nc.dram_tensor(pfx + "gVs", [TC, GV], BF16).ap()
        SG = nc.dram_tensor(pfx + "gSG", [D, TC], BF16).ap()
        P.release(m0)
    if variant != "sweep3":

        C = Common(P, io["mod"])
        tri = {}
        for d in "fb":
            tri[d] = P.sb([128, 128], F32, name=f"tri{d}")
            P.dma("sp", tri[d][:], io["tri" + d], writes=[("tri", d)])
        id32 = P.sb([128, 128], F32, name="id32")
        P.dma("sp", id32[:], io["ident"], writes=["id32"])
        ident = P.sb([128, 128], BF16, name="ident")
        P.dve(lambda e: e.tensor_copy(out=ident[:], in_=id32[:]), reads=["id32"], writes=["ident"])
        w1s = P.sb([128, KC, 64], F32, name="w1s")
        P.dma("sp", w1s[:], io["w1cat"].rearrange("(k p) n -> p k n", p=128), writes=["w1s"])
        w1b = P.sb([128, KC, 64], BF16, name="w1b")
        P.dve(lambda e: e.tensor_copy(out=w1b[:], in_=w1s[:]), reads=["w1s"], writes=["w1b"])
        w2s = P.sb([64, GK], F32, name="w2s")
        P.dma("sp", w2s[:], io["w2aug"], writes=["w2s"])
        w2b = P.sb([64, GK], BF16, name="w2b")
        P.dve(lambda e: e.tensor_copy(out=w2b[:], in_=w2s[:]), reads=["w2s"], writes=["w2b"])

        order = [slab_src(wb_in, 256 * j) for j in range(24)]
        ws = WStream(P, order * NT, ring=3)
        nb = norm_bufs(P, 512)
        hT = [P.sb([128, KC, 512], BF16, name=f"ghT{i}") for i in range(2)]
        rT = [P.sb([64, 512], BF16, name=f"rT{i}") for i in range(2)]
        for i in range(2):
            P.dve(lambda e, i=i: e.memset(rT[i][:], 1.0), writes=[("rT", i)])
        qT_sb = P.sb([128, 8, 512], F32, name="qT_sb")
        kT_sb = P.sb([128, 8, 512], F32, name="kT_sb")
        e1 = P.sb([128, GK], F32, name="e1")
        la = {d: P.sb([128, GK], F32, name=f"la{d}") for d in "fb"}
        Eq = P.sb([128, 8, 128], F32, name="Eq")
        Einv = P.sb([128, 8, 128], F32, name="Einv")
        ktmp = P.sb([128, 8, 128], F32, name="ktmp")
        qtl = [P.sb([128, 8, 128], BF16) for _ in range(2)]
        ktl = [P.sb([128, 8, 128], BF16) for _ in range(2)]
        khT = [P.sb([128, 8, 128], BF16) for _ in range(2)]
        khat = [P.sb([128, GK], BF16) for _ in range(2)]
        vbuf = P.sb([128, 4, GV], BF16, name="gvbuf")
        sgt = [P.sb([128, 512], BF16) for _ in range(3)]
        mm = [P.ps([128, 512], F32) for _ in range(2)]
        rps = P.ps([64, 512], F32, name="rps")
        zps = P.ps([128, 512], F32, name="zps")
        bps = [P.ps([128, 4, 128], F32) for _ in range(2)]
        trp = P.ps([128, GK], BF16, name="trp")
        cnt = {"mm": 0, "sg": 0, "cd": 0, "bp": 0}

        def next_mm():
            i = cnt["mm"] % len(mm)
            cnt["mm"] += 1
            return mm[i], ("mm", i)

        def proj_fm(t, hb, dst_sb, dkey):
            h = hT[hb]
            for sl in range(4):
                slab, skey = ws.next()
                for ml in range(2):
                    m = sl * 2 + ml
                    ps, pk = next_mm()
                    for k in range(KC):
                        P.pe(lambda e, ps=ps, slab=slab, k=k, ml=ml, h=h: e.matmul(
                            ps[:], lhsT=slab[:, k, ml * 128:(ml + 1) * 128], rhs=h[:, k, :],
                            start=(k == 0), stop=(k == KC - 1)), reads=[skey, (("ghT", hb), k)], writes=[pk])
                    P.act(lambda e, ps=ps, m=m: e.copy(out=dst_sb[:, m, :], in_=ps[:]), reads=[pk], writes=[(dkey, m)])

        def chunk_dir(t, cc, d, rb):
            c = 4 * t + cc
            base = 0 if d == "f" else 32
            last = 127 if d == "f" else 0
            tok = slice(cc * 128, (cc + 1) * 128)
            for hh in range(2):
                P.pe(lambda e, hh=hh: e.matmul(zps[:], lhsT=rT[rb][base:base + 32, tok],
                                               rhs=w2b[base:base + 32, hh * 512:(hh + 1) * 512], start=True, stop=True),
                     reads=[("rT", rb), "w2b"], writes=["zps"])
                P.act(lambda e, hh=hh: e.activation(out=e1[:, hh * 512:(hh + 1) * 512], in_=zps[:], func=AF.Exp, scale=-1.0),
                      reads=["zps"], writes=[("e1", hh)])
                P.act(lambda e, hh=hh: e.activation(out=la[d][:, hh * 512:(hh + 1) * 512], in_=e1[:, hh * 512:(hh + 1) * 512],
                                                    func=AF.Ln, bias=1.0, scale=1.0),
                      reads=[("e1", hh)], writes=[("la", d, hh)])
            for mh in range(2):
                bi = cnt["bp"] % 2
                cnt["bp"] += 1
                bp = bps[bi]
                for m4 in range(4):
                    m = mh * 4 + m4
                    P.pe(lambda e, bp=bp, m4=m4, m=m: e.matmul(bp[:, m4, :], lhsT=la[d][:, m * 128:(m + 1) * 128],
                                                               rhs=tri[d][:], start=True, stop=True),
                         reads=[("la", d, m // 4), ("tri", d)], writes=[("bps", bi)])
                ms = slice(mh * 4, mh * 4 + 4)
                P.act(lambda e, bp=bp, ms=ms: e.activation(out=Eq[:, ms, :], in_=bp[:], func=AF.Exp, bias=LN_QS, scale=1.0),
                      reads=[("bps", bi)], writes=[("Eq", mh)])
                P.act(lambda e, bp=bp, ms=ms: e.activation(out=Einv[:, ms, :], in_=bp[:], func=AF.Exp, scale=-1.0),
                      reads=[("bps", bi)], writes=[("Einv", mh)])
                P.act(lambda e, bp=bp, ms=ms: e.activation(out=el[d][:, ms, c:c + 1], in_=bp[:, :, last:last + 1], func=AF.Exp),
                      reads=[("bps", bi)], writes=[("el", d, c, mh)])
            i2 = cnt["cd"] % 2
            cnt["cd"] += 1
            P.dve(lambda e, i2=i2: e.tensor_tensor(out=qtl[i2][:], in0=qT_sb[:, :, tok], in1=Eq[:], op=ALU.mult),
                  reads=[("qT", m) for m in range(8)] + [("Eq", 0), ("Eq", 1)], writes=[("qtl", i2)])
            P.dve(lambda e: e.tensor_tensor(out=ktmp[:], in0=kT_sb[:, :, tok], in1=Einv[:], op=ALU.mult),
                  reads=[("kT", m) for m in range(8)] + [("Einv", 0), ("Einv", 1)], writes=["ktmp"])
            P.pool(lambda e, i2=i2: e.tensor_copy(out=ktl[i2][:], in_=ktmp[:]), reads=["ktmp"], writes=[("ktl", i2)])
            P.pool(lambda e, i2=i2: e.tensor_tensor(out=khT[i2][:], in0=ktmp[:],
                                                    in1=el[d][:, :, c:c + 1].to_broadcast([128, 8, 128]), op=ALU.mult),
                   reads=["ktmp", ("el", d, c, 0), ("el", d, c, 1)], writes=[("khT", i2)])
            for m in range(8):
                P.pe(lambda e, m=m, i2=i2: e.transpose(trp[:, m * 128:(m + 1) * 128], khT[i2][:, m, :], ident[:]),
                     reads=[("khT", i2), "ident"], writes=["trp"])
            P.act(lambda e, i2=i2: e.copy(out=khat[i2][:], in_=trp[:]), reads=["trp"], writes=[("khat", i2)])
            csl = slice(c * 128, (c + 1) * 128)
            P.dma("pool", QT[d][:, csl].rearrange("(m p) t -> p m t", p=128), qtl[i2][:],
                  reads=[("qtl", i2)], writes=[("QT", d, c)])
            P.dma("pool", KT[d][:, csl].rearrange("(m p) t -> p m t", p=128), ktl[i2][:],
                  reads=[("ktl", i2)], writes=[("KT", d, c)])
            P.dma("pool", KH[d][csl, :], khat[i2][:], reads=[("khat", i2)], writes=[("KH", d, c)])

        def sweep1_tile(t):
            hb = t % 2
            h = hT[hb]
            rb = t % 2
            for k in range(KC):
                P.pe(lambda e, k=k, h=h: e.matmul(rps[:], lhsT=w1b[:, k, :], rhs=h[:, k, :], start=(k == 0), stop=(k == KC - 1)),
                     reads=["w1b", (("ghT", hb), k)], writes=["rps"])
            P.act(lambda e: e.copy(out=rT[rb][0:16, :], in_=rps[0:16, :]), reads=["rps"], writes=[("rT", rb)])
            P.act(lambda e: e.copy(out=rT[rb][32:48, :], in_=rps[32:48, :]), reads=["rps"], writes=[("rT", rb)])
            proj_fm(t, hb, qT_sb, "qT")
            proj_fm(t, hb, kT_sb, "kT")

            def v_group(sl):
                slab, skey = ws.next()
                for j in range(4):
                    ps, pk = next_mm()
                    for k in range(KC):
                        P.pe(lambda e, ps=ps, slab=slab, k=k, j=j, h=h: e.matmul(
                            ps[:, 0:256], lhsT=h[:, k, j * 128:(j + 1) * 128], rhs=slab[:, k, :],
                            start=(k == 0), stop=(k == KC - 1)), reads=[skey, (("ghT", hb), k)], writes=[pk])
                    eng = "act" if (sl + j) % 2 == 0 else "dve"
                    if eng == "act":
                        P.act(lambda e, ps=ps, j=j, sl=sl: e.copy(out=vbuf[:, j, sl * 256:(sl + 1) * 256], in_=ps[:, 0:256]),
                              reads=[pk], writes=[("gvbuf", j, sl)])
                    else:
                        P.dve(lambda e, ps=ps, j=j, sl=sl: e.tensor_copy(out=vbuf[:, j, sl * 256:(sl + 1) * 256], in_=ps[:, 0:256]),
                              reads=[pk], writes=[("gvbuf", j, sl)])
                if sl == 7:
                    P.dma("pool", Vs[t * TT:(t + 1) * TT, :].rearrange("(j p) n -> p j n", p=128), vbuf[:],
                          reads=[("gvbuf", j, s_) for j in range(4) for s_ in range(8)], writes=[("Vs", t)])

            def g_group(sl):
                slab, skey = ws.next()
                for ml in range(2):
                    m = sl * 2 + ml
                    ps, pk = next_mm()
                    for k in range(KC):
                        P.pe(lambda e, ps=ps, slab=slab, k=k, ml=ml, h=h: e.matmul(
                            ps[:], lhsT=slab[:, k, ml * 128:(ml + 1) * 128], rhs=h[:, k, :],
                            start=(k == 0), stop=(k == KC - 1)), reads=[skey, (("ghT", hb), k)], writes=[pk])
                    i3 = cnt["sg"] % 3
                    cnt["sg"] += 1
                    P.act(lambda e, ps=ps, i3=i3: e.activation(out=sgt[i3][:], in_=ps[:], func=AF.Silu),
                          reads=[pk], writes=[("sgt", i3)])
                    P.dma("pool", SG[m * 128:(m + 1) * 128, t * TT:(t + 1) * TT], sgt[i3][:],
                          reads=[("sgt", i3)], writes=[("SG", m, t)])

            cds = [(cc, d) for cc in range(4) for d in "fb"]
            for gi in range(16):
                if gi % 2 == 0:
                    chunk_dir(t, cds[gi // 2][0], cds[gi // 2][1], rb)
                if gi < 8:
                    v_group(gi)
                else:
                    g_group(gi - 8)

        def norm(t):
            emit_norm_mod(P, C, xT, TT * t, 512, hT[t % 2], ("ghT", t % 2), nb)

        norm(0)
        for t in range(NT):
            if t + 1 < NT:
                norm(t + 1)
            sweep1_tile(t)
        P.release(m0)

        msk = {}
        for d in "fb":
            msk[d] = P.sb([128, 128], F32, name=f"msk{d}")
            P.dma("sp", msk[d][:], io["mask" + d], writes=[("msk", d)])
        S32 = {d: P.sb([128, 8, 512], F32, name=f"S32{d}") for d in "fb"}
        Sb = {d: P.sb([128, 8, 512], BF16, name=f"Sb{d}") for d in "fb"}
        ecum = {d: P.sb([128, 8], F32, name=f"ecum{d}") for d in "fb"}
        for d in "fb":
            P.dve(lambda e, d=d: e.memset(S32[d][:], 0.0), writes=[("S32", d, i) for i in range(8)])
            P.pool(lambda e, d=d: e.memset(Sb[d][:], 0.0), writes=[("Sb", d, i) for i in range(8)])
            P.dve(lambda e, d=d: e.memset(ecum[d][:], 1.0), writes=[("ecum", d)])
        qt = {d: [P.sb([128, 8, 128], BF16) for _ in range(2)] for d in "fb"}
        kt = {d: [P.sb([128, 8, 128], BF16) for _ in range(2)] for d in "fb"}
        kh = {d: [P.sb([128, GK], BF16) for _ in range(2)] for d in "fb"}
        vv = {d: [P.sb([128, GV], BF16) for _ in range(2)] for d in "fb"}
        Am = {d: P.sb([128, 4, 128], BF16, name=f"Am{d}") for d in "fb"}
        qc = {d: P.sb([128, 8, 128], BF16, name=f"qc{d}") for d in "fb"}
        osb = {d: P.sb([128, 16, 128], F32, name=f"osb{d}") for d in "fb"}
        aps = P.ps([128, 4, 128], F32, name="aps")
        ops_ = P.ps([128, 16, 128], F32, name="ops")
        sps = [P.ps([128, 512], F32) for _ in range(3)]
        spi = 0
        for step in range(NCH):
            for d in "fb":
                c = step if d == "f" else NCH - 1 - step
                b2 = step % 2
                csl = slice(c * 128, (c + 1) * 128)
                P.dma("sp", qt[d][b2][:], QT[d][:, csl].rearrange("(m p) t -> p m t", p=128),
                      reads=[("QT", d, c)], writes=[("qt", d, b2)])
                P.dma("sp", kt[d][b2][:], KT[d][:, csl].rearrange("(m p) t -> p m t", p=128),
                      reads=[("KT", d, c)], writes=[("kt", d, b2)])
                P.dma("sp", kh[d][b2][:], KH[d][csl, :], reads=[("KH", d, c)], writes=[("kh", d, b2)])
                P.dma("sp", vv[d][b2][:], Vs[csl, :], reads=[("Vs", c // 4)], writes=[("vv", d, b2)])
                for hd in range(4):
                    for dc in range(2):
                        i8 = hd * 2 + dc
                        P.pe(lambda e, hd=hd, dc=dc, i8=i8, d=d, b2=b2: e.matmul(
                            aps[:, hd, :], lhsT=kt[d][b2][:, i8, :], rhs=qt[d][b2][:, i8, :], start=(dc == 0), stop=(dc == 1)),
                            reads=[("kt", d, b2), ("qt", d, b2)], writes=["aps"])
                P.dve(lambda e, d=d: e.tensor_tensor(out=Am[d][:], in0=aps[:],
                                                     in1=msk[d][:, :].unsqueeze(1).to_broadcast([128, 4, 128]), op=ALU.mult),
                      reads=["aps", ("msk", d)], writes=[("Am", d)])
                for hd in range(4):
                    for ec in range(4):
                        o16 = hd * 4 + ec
                        for dc in range(2):
                            i8 = hd * 2 + dc
                            P.pe(lambda e, o16=o16, i8=i8, ec=ec, dc=dc, d=d, b2=b2: e.matmul(
                                ops_[:, o16, :], lhsT=Sb[d][:, i8, ec * 128:(ec + 1) * 128], rhs=qt[d][b2][:, i8, :],
                                start=(dc == 0), stop=False), reads=[("Sb", d, i8), ("qt", d, b2)], writes=["ops"])
                        P.pe(lambda e, o16=o16, hd=hd, ec=ec, d=d, b2=b2: e.matmul(
                            ops_[:, o16, :], lhsT=vv[d][b2][:, hd * 512 + ec * 128:hd * 512 + (ec + 1) * 128], rhs=Am[d][:, hd, :],
                            start=False, stop=True), reads=[("vv", d, b2), ("Am", d)], writes=["ops"])
                for q4 in range(4):
                    if q4 % 2 == 0:
                        P.act(lambda e, d=d, q4=q4: e.copy(out=osb[d][:, q4 * 4:(q4 + 1) * 4, :], in_=ops_[:, q4 * 4:(q4 + 1) * 4, :]),
                              reads=["ops"], writes=[("osb", d)])
                    else:
                        P.dve(lambda e, d=d, q4=q4: e.tensor_copy(out=osb[d][:, q4 * 4:(q4 + 1) * 4, :], in_=ops_[:, q4 * 4:(q4 + 1) * 4, :]),
                              reads=["ops"], writes=[("osb", d)])
                P.dma("pool", OT[d][:, csl].rearrange("(m p) t -> p m t", p=128), osb[d][:],
                      reads=[("osb", d)], writes=[("OT", d, c)])
                P.pool(lambda e, d=d, b2=b2: e.tensor_tensor(out=qc[d][:], in0=qt[d][b2][:],
                                                             in1=ecum[d][:, :].unsqueeze(2).to_broadcast([128, 8, 128]), op=ALU.mult),
                       reads=[("qt", d, b2), ("ecum", d)], writes=[("qc", d)])
                P.dma("pool", QC[d][:, csl].rearrange("(m p) t -> p m t", p=128), qc[d][:],
                      reads=[("qc", d)], writes=[("QC", d, c)])
                P.pool(lambda e, d=d, c=c: e.tensor_tensor(out=ecum[d][:], in0=ecum[d][:], in1=el[d][:, :, c], op=ALU.mult),
                       reads=[("ecum", d), ("el", d, c, 0), ("el", d, c, 1)], writes=[("ecum", d)])
                for i8 in range(8):
                    hd = i8 // 2
                    s3 = spi % 3
                    spi += 1
                    P.pe(lambda e, s3=s3, i8=i8, hd=hd, d=d, b2=b2: e.matmul(
                        sps[s3][:], lhsT=kh[d][b2][:, i8 * 128:(i8 + 1) * 128], rhs=vv[d][b2][:, hd * 512:(hd + 1) * 512],
                        start=True, stop=True), reads=[("kh", d, b2), ("vv", d, b2)], writes=[("sps", s3)])
                    P.dve(lambda e, s3=s3, i8=i8, d=d, c=c: e.scalar_tensor_tensor(
                        out=S32[d][:, i8, :], in0=S32[d][:, i8, :], scalar=el[d][:, i8, c:c + 1], in1=sps[s3][:],
                        op0=ALU.mult, op1=ALU.add), reads=[("sps", s3), ("S32", d, i8), ("el", d, c, i8 // 4)],
                        writes=[("S32", d, i8)])
                    P.act(lambda e, i8=i8, d=d: e.copy(out=Sb[d][:, i8, :], in_=S32[d][:, i8, :]),
                          reads=[("S32", d, i8)], writes=[("Sb", d, i8)])
        if "S_end_f" in io:
            for d in "fb":
                P.dma("pool", io["S_end_" + d], S32[d][:], reads=[("S32", d, i) for i in range(8)], writes=[("S_end", d)])
    if not full:
        P.release(mtop)
        return {"QC": QC, "OT": OT, "SG": SG, "wb": wb}
    P.release(m0)

    C = Common(P, io["mod"])
    ng = P.sb([128, 4], F32, name="ng")
    P.dma("sp", ng[:], io["ng"], writes=["ng"])
    Sin = {}
    s32 = P.sb([128, 8, 512], F32, name="Sin32")
    hm = P.sb([128, 2], F32, name="ghm")
    if "hmask" in io:
        P.dma("sp", hm[:], io["hmask"], writes=["ghm"])
    else:
        P.dve(lambda e: e.memset(hm[:], 1.0), writes=["ghm"])
    for di, d in enumerate("fb"):
        P.dma("sp", s32[:], io["S_in_" + d], writes=["Sin32"])
        Sin[d] = P.sb([128, 8, 512], BF16, name=f"Sin{d}")
        P.dve(lambda e, d=d, di=di: e.tensor_scalar(out=Sin[d][:], in0=s32[:], scalar1=hm[:, di:di + 1], scalar2=None,
                                                    op0=ALU.mult), reads=["Sin32", "ghm"], writes=[("Sin", d)])
    ws3 = WStream(P, [slab_src(wb_out, 256 * j) for j in range(8)] * NT, ring=3, name="ws3")
    qcl = {d: [P.sb([128, 8, 512], BF16) for _ in range(1)] for d in "fb"}
    of_ = [P.sb([128, 4, 512], F32) for _ in range(2)]
    ob_ = [P.sb([128, 4, 512], F32) for _ in range(2)]
    osum = [P.sb([128, 4, 512], F32) for _ in range(2)]
    osq = [P.sb([128, 512], BF16) for _ in range(2)]
    sgl = [P.sb([128, 4, 512], BF16) for _ in range(2)]
    rsd = [P.sb([128, 512], F32) for _ in range(2)]
    z = [P.sb([128, KC, 512], BF16) for _ in range(2)]
    xres = [P.sb([128, 512], F32) for _ in range(3)]
    xo_t = [P.sb([128, 512], F32) for _ in range(3)]
    mm3 = [P.ps([128, 512], F32) for _ in range(4)]
    ssp = [P.ps([128, 512], F32) for _ in range(2)]
    c3 = {"mm": 0}

    def next_mm3():
        i = c3["mm"] % len(mm3)
        c3["mm"] += 1
        return mm3[i], ("mm3", i)

    def heads_gen(t):
        tsl = slice(t * TT, (t + 1) * TT)
        tb = t % 2
        qb = 0
        for d in "fb":
            P.dma("sp", qcl[d][qb][:], QC[d][:, tsl].rearrange("(m p) t -> p m t", p=128),
                  reads=[("QC", d, c) for c in range(4 * t, 4 * t + 4)], writes=[("qcl", d, qb)])
        for hd in range(4):
            hb2 = hd % 2
            rows = slice(hd * 512, (hd + 1) * 512)
            P.dma("sp", of_[hb2][:], OT["f"][rows, tsl].rearrange("(m p) t -> p m t", p=128),
                  reads=[("OT", "f", c) for c in range(4 * t, 4 * t + 4)], writes=[("of", hb2)])
            P.dma("sp", ob_[hb2][:], OT["b"][rows, tsl].rearrange("(m p) t -> p m t", p=128),
                  reads=[("OT", "b", c) for c in range(4 * t, 4 * t + 4)], writes=[("ob", hb2)])
            P.dma("sp", sgl[hb2][:], SG[rows, tsl].rearrange("(m p) t -> p m t", p=128),
                  reads=[("SG", hd * 4 + ec, t) for ec in range(4)], writes=[("sgl", hb2)])
            P.pool(lambda e, hb2=hb2: e.tensor_tensor(out=of_[hb2][:], in0=of_[hb2][:], in1=ob_[hb2][:], op=ALU.add),
                   reads=[("of", hb2), ("ob", hb2)], writes=[("of", hb2)])
            for ec in range(4):
                ps, pk = next_mm3()
                n = 0
                for d in "fb":
                    for dc in range(2):
                        i8 = hd * 2 + dc
                        P.pe(lambda e, ps=ps, d=d, i8=i8, ec=ec, qb=qb, n=n: e.matmul(
                            ps[:], lhsT=Sin[d][:, i8, ec * 128:(ec + 1) * 128], rhs=qcl[d][qb][:, i8, :],
                            start=(n == 0), stop=(n == 3)), reads=[("Sin", d), ("qcl", d, qb)], writes=[pk])
                        n += 1
                P.dve(lambda e, ps=ps, hb2=hb2, ec=ec: e.tensor_tensor(out=osum[hb2][:, ec, :], in0=ps[:], in1=of_[hb2][:, ec, :],
                                                                       op=ALU.add),
                      reads=[pk, ("of", hb2)], writes=[("osum", hb2, ec)])
                sb2 = ec % 2
                P.act(lambda e, hb2=hb2, ec=ec, sb2=sb2: e.activation(out=osq[sb2][:], in_=osum[hb2][:, ec, :], func=AF.Square),
                      reads=[("osum", hb2, ec)], writes=[("osq", sb2)])
                P.pe(lambda e, hb2=hb2, ec=ec, sb2=sb2: e.matmul(ssp[hb2][:], lhsT=C.ones[:], rhs=osq[sb2][:],
                                                                start=(ec == 0), stop=(ec == 3)),
                     reads=["ones", ("osq", sb2)], writes=[("ssp", hb2)])
                yield
            P.act(lambda e, hb2=hb2: e.activation(out=rsd[hb2][:], in_=ssp[hb2][:], func=AF.Ln, scale=1.0 / 512, bias=EPS),
                  reads=[("ssp", hb2)], writes=[("rsd", hb2)])
            P.act(lambda e, hb2=hb2: e.activation(out=rsd[hb2][:], in_=rsd[hb2][:], func=AF.Exp, scale=-0.5),
                  reads=[("rsd", hb2)], writes=[("rsd", hb2)])
            for ec in range(4):
                mg = hd * 4 + ec
                P.dve(lambda e, hb2=hb2, ec=ec: e.scalar_tensor_tensor(
                    out=osum[hb2][:, ec, :], in0=osum[hb2][:, ec, :], scalar=ng[:, ec:ec + 1], in1=rsd[hb2][:],
                    op0=ALU.mult, op1=ALU.mult), reads=[("osum", hb2, ec), "ng", ("rsd", hb2)], writes=[("osum", hb2, ec)])
                P.pool(lambda e, hb2=hb2, ec=ec, mg=mg, tb=tb: e.tensor_tensor(
                    out=z[tb][:, mg, :], in0=osum[hb2][:, ec, :], in1=sgl[hb2][:, ec, :], op=ALU.mult),
                    reads=[("osum", hb2, ec), ("sgl", hb2)], writes=[("z3", tb, mg)])
            yield

    def outproj_gen(t):
        tb = t % 2
        c0 = TT * t

        def load_res(m):
            rb = m % len(xres)
            P.dma("pool", xres[rb][:], xT[m * 128:(m + 1) * 128, c0:c0 + 512], writes=[("xres", rb)])
        for m in range(len(xres)):
            load_res(m)
        for sl in range(8):
            slab, skey = ws3.next()
            for ml in range(2):
                m = sl * 2 + ml
                ps, pk = next_mm3()
                for k in range(KC):
                    P.pe(lambda e, ps=ps, slab=slab, k=k, ml=ml, tb=tb: e.matmul(
                        ps[:], lhsT=slab[:, k, ml * 128:(ml + 1) * 128], rhs=z[tb][:, k, :],
                        start=(k == 0), stop=(k == KC - 1)), reads=[skey, ("z3", tb, k)], writes=[pk])
                rb = m % len(xres)
                ob = m % len(xo_t)
                P.dve(lambda e, ps=ps, m=m, rb=rb, ob=ob: e.scalar_tensor_tensor(
                    out=xo_t[ob][:], in0=ps[:], scalar=C.gate(m), in1=xres[rb][:],
                    op0=ALU.mult, op1=ALU.add), reads=[pk, "mod", ("xres", rb)], writes=[("xo_t", ob)])
                P.dma("pool", io["xo"][m * 128:(m + 1) * 128, TT * t:TT * (t + 1)], xo_t[ob][:],
                      reads=[("xo_t", ob)], writes=[("xo", t, m)])
                if m + len(xres) < KC:
                    load_res(m + len(xres))
                yield

    for _ in heads_gen(0):
        pass
    for t in range(NT):
        ga = outproj_gen(t)
        gb = heads_gen(t + 1) if t + 1 < NT else iter(())
        done_a = done_b = False
        while not (done_a and done_b):
            if not done_b:
                try:
                    next(gb)
                except StopIteration:
                    done_b = True
            if not done_a:
                try:
                    next(ga)
                except StopIteration:
                    done_a = True
    P.release(mtop)


def emit_attn_fused(P, nc, segs, io, pfx="f"):
    m0 = P.mark()
    wb_in = nc.dram_tensor(pfx + "awb_in", [D, 5120], BF16).ap()
    convert_w(P, io["w_in"], wb_in, D, 5120, "acw_in")
    wb_out = nc.dram_tensor(pfx + "awb_out", [D, D], BF16).ap()
    convert_w(P, io["w_out"], wb_out, D, D, "acw_out")
    ns = len(segs)
    Qs = [nc.dram_tensor(pfx + f"aQs{i}", [16, 128, TC], BF16).ap() for i in range(ns)]
    SG = [nc.dram_tensor(pfx + f"aSG{i}", [D, TC], BF16).ap() for i in range(ns)]
    Zs = [nc.dram_tensor(pfx + f"aZs{i}", [D, TC], BF16).ap() for i in range(ns)]
    KTs = [nc.dram_tensor(pfx + f"aKT{i}", [4, 128, TC], BF16).ap() for i in range(ns)]
    Vss = [nc.dram_tensor(pfx + f"aV{i}", [TC, 512], BF16).ap() for i in range(ns)]
    P.release(m0)

    for si, seg in enumerate(segs):
        xT = seg["xT"]
        qtiles = set(seg["qtiles"])
        C = Common(P, io["mod"])
        gains = P.sb([128, 2], F32, name="gains")
        P.dma("sp", gains[:], io["gains"], writes=["gains"])
        rot32 = P.sb([128, 128], F32, name="rot32")
        P.dma("sp", rot32[:], io["rotT"], writes=["rot32"])
        rotb = P.sb([128, 128], BF16, name="rotb")
        P.dve(lambda e, rotb=rotb, rot32=rot32: e.tensor_copy(out=rotb[:], in_=rot32[:]), reads=["rot32"], writes=["rotb"])
        order = []
        for t in range(NT):
            if t in qtiles:
                order += [slab_src(wb_in, 256 * j) for j in range(8)]
            order += [slab_src(wb_in, 2048 + 256 * j) for j in range(4)]
            if t in qtiles:
                order += [slab_src(wb_in, 3072 + 256 * j) for j in range(8)]
        ws = WStream(P, order, ring=4, name=f"wsA{si}")
        nb = norm_bufs(P, 512)
        hT = [P.sb([128, KC, 512], BF16, name=f"ahT{i}") for i in range(2)]
        cs = [P.sb([128, 512], F32) for _ in range(2)]
        sn = [P.sb([128, 512], F32) for _ in range(2)]
        qg_t = [P.sb([128, 512], BF16) for _ in range(2)]
        sq_t = [P.sb([128, 512], BF16) for _ in range(2)]
        rs_t = [P.sb([128, 512], F32) for _ in range(2)]
        t1 = [P.sb([128, 512], F32) for _ in range(2)]
        t2 = [P.sb([128, 512], F32) for _ in range(2)]
        qo = [P.sb([128, 512], BF16) for _ in range(3)]
        vbuf = [P.sb([128, 4, 512], BF16) for _ in range(2)]
        sgt = [P.sb([128, 512], BF16) for _ in range(3)]
        mm = [P.ps([128, 512], F32) for _ in range(3)]
        ssq = [P.ps([128, 512], F32) for _ in range(2)]
        rotp = [P.ps([128, 512], F32) for _ in range(2)]
        cnt = {"mm": 0, "hd": 0, "qo": 0, "sg": 0}

        def next_mm():
            i = cnt["mm"] % len(mm)
            cnt["mm"] += 1
            return mm[i], ("mm", i)

        def qk_head(t, hb, slab, skey, ml, gcol, dst, dkey):
            h = hT[hb]
            ps, pk = next_mm()
            for k in range(KC):
                P.pe(lambda e, ps=ps, slab=slab, k=k, ml=ml, h=h: e.matmul(
                    ps[:], lhsT=slab[:, k, ml * 128:(ml + 1) * 128], rhs=h[:, k, :],
                    start=(k == 0), stop=(k == KC - 1)), reads=[skey, (("ahT", hb), k)], writes=[pk])
            i2 = cnt["hd"] % 2
            cnt["hd"] += 1
            P.act(lambda e, ps=ps, i2=i2, gains=gains, qg_t=qg_t: e.activation(
                out=qg_t[i2][:], in_=ps[:], func=AF.Copy, scale=gains[:, gcol:gcol + 1]),
                reads=[pk, "gains"], writes=[("qg", i2)])
            P.act(lambda e, ps=ps, i2=i2, sq_t=sq_t: e.activation(out=sq_t[i2][:], in_=ps[:], func=AF.Square),
                  reads=[pk], writes=[("sq_t", i2)])
            return lambda: qk_tail(t, i2, dst, dkey)

        def qk_tail(t, i2, dst, dkey):
            P.pe(lambda e, i2=i2, C=C, ssq=ssq, sq_t=sq_t: e.matmul(ssq[i2][:], lhsT=C.ones[:], rhs=sq_t[i2][:], start=True, stop=True),
                 reads=["ones", ("sq_t", i2)], writes=[("ssq", i2)])
            P.pe(lambda e, i2=i2, rotp=rotp, rotb=rotb, qg_t=qg_t: e.matmul(rotp[i2][:], lhsT=rotb[:], rhs=qg_t[i2][:], start=True, stop=True),
                 reads=["rotb", ("qg", i2)], writes=[("rotp", i2)])
            P.act(lambda e, i2=i2, rs_t=rs_t, ssq=ssq: e.activation(out=rs_t[i2][:], in_=ssq[i2][:], func=AF.Ln, scale=1.0 / 128, bias=EPS),
                  reads=[("ssq", i2)], writes=[("rs_t", i2)])
            P.act(lambda e, i2=i2, rs_t=rs_t: e.activation(out=rs_t[i2][:], in_=rs_t[i2][:], func=AF.Exp, scale=-0.5),
                  reads=[("rs_t", i2)], writes=[("rs_t", i2)])
            cb = t % 2
            P.pool(lambda e, i2=i2, cb=cb, t1=t1, qg_t=qg_t, cs=cs: e.tensor_tensor(out=t1[i2][:], in0=qg_t[i2][:], in1=cs[cb][:], op=ALU.mult),
                   reads=[("qg", i2), ("cs", cb)], writes=[("t1", i2)])
            P.dve(lambda e, i2=i2, cb=cb, t2=t2, rotp=rotp, sn=sn: e.tensor_tensor(out=t2[i2][:], in0=rotp[i2][:], in1=sn[cb][:], op=ALU.mult),
                  reads=[("rotp", i2), ("sn", cb)], writes=[("t2", i2)])
            P.pool(lambda e, i2=i2, t1=t1, t2=t2: e.tensor_tensor(out=t1[i2][:], in0=t1[i2][:], in1=t2[i2][:], op=ALU.add),
                   reads=[("t1", i2), ("t2", i2)], writes=[("t1", i2)])
            i3 = cnt["qo"] % 3
            cnt["qo"] += 1
            P.dve(lambda e, i2=i2, i3=i3, qo=qo, t1=t1, rs_t=rs_t: e.tensor_tensor(out=qo[i3][:], in0=t1[i2][:], in1=rs_t[i2][:], op=ALU.mult),
                  reads=[("t1", i2), ("rs_t", i2)], writes=[("qo", i3)])
            P.dma("pool", dst, qo[i3][:], reads=[("qo", i3)], writes=[dkey])

        def projections(t):
            hb = t % 2
            h = hT[hb]
            cb = t % 2
            wantq = t in qtiles
            P.dma("sp", cs[cb][:], seg["cos"][:, t * TT:(t + 1) * TT], writes=[("cs", cb)])
            P.dma("sp", sn[cb][:], seg["sin"][:, t * TT:(t + 1) * TT], writes=[("sn", cb)])
            pend = [None]

            def flush(nxt=None):
                if pend[0] is not None:
                    pend[0]()
                pend[0] = nxt
            if wantq:
                for sl in range(8):
                    slab, skey = ws.next()
                    for ml in range(2):
                        hd = sl * 2 + ml
                        flush(qk_head(t, hb, slab, skey, ml, 0, Qs[si][hd][:, t * TT:(t + 1) * TT], ("Qs", si, hd, t)))
            for sl in range(2):
                slab, skey = ws.next()
                for ml in range(2):
                    kh = sl * 2 + ml
                    flush(qk_head(t, hb, slab, skey, ml, 1, KTs[si][kh][:, t * TT:(t + 1) * TT], ("KT", si, kh, t)))
            vb = t % 2
            for sl in range(2):
                slab, skey = ws.next()
                if sl == 1:
                    flush()
                for j in range(4):
                    ps, pk = next_mm()
                    for k in range(KC):
                        P.pe(lambda e, ps=ps, slab=slab, k=k, j=j, h=h: e.matmul(
                            ps[:, 0:256], lhsT=h[:, k, j * 128:(j + 1) * 128], rhs=slab[:, k, :],
                            start=(k == 0), stop=(k == KC - 1)), reads=[skey, (("ahT", hb), k)], writes=[pk])
                    P.act(lambda e, ps=ps, j=j, sl=sl, vb=vb, vbuf=vbuf: e.copy(out=vbuf[vb][:, j, sl * 256:(sl + 1) * 256],
                                                                                in_=ps[:, 0:256]),
                          reads=[pk], writes=[("vbuf", vb, j, sl)])
            P.dma("pool", Vss[si][t * TT:(t + 1) * TT, :].rearrange("(j p) n -> p j n", p=128), vbuf[vb][:],
                  reads=[("vbuf", vb, j, sl) for j in range(4) for sl in range(2)], writes=[("V", si, t)])
            if wantq:
                for sl in range(8):
                    slab, skey = ws.next()
                    for ml in range(2):
                        m = sl * 2 + ml
                        ps, pk = next_mm()
                        for k in range(KC):
                            P.pe(lambda e, ps=ps, slab=slab, k=k, ml=ml, h=h: e.matmul(
                                ps[:], lhsT=slab[:, k, ml * 128:(ml + 1) * 128], rhs=h[:, k, :],
                                start=(k == 0), stop=(k == KC - 1)), reads=[skey, (("ahT", hb), k)], writes=[pk])
                        i3 = cnt["sg"] % 3
                        cnt["sg"] += 1
                        P.act(lambda e, ps=ps, i3=i3, sgt=sgt: e.activation(out=sgt[i3][:], in_=ps[:], func=AF.Silu),
                              reads=[pk], writes=[("sgt", i3)])
                        P.dma("pool", SG[si][m * 128:(m + 1) * 128, t * TT:(t + 1) * TT], sgt[i3][:],
                              reads=[("sgt", i3)], writes=[("SG", si, m, t)])

        def norm(t):
            emit_norm_mod(P, C, xT, TT * t, 512, hT[t % 2], ("ahT", t % 2), nb)

        norm(0)
        for t in range(NT):
            if t + 1 < NT:
                norm(t + 1)
            projections(t)
        P.release(m0)

    qlist = [(si, t) for si, seg in enumerate(segs) for t in seg["qtiles"]]

    ones32 = P.sb([128, 128], F32, name="ones32")
    P.dve(lambda e: e.memset(ones32[:], 1.0), writes=["ones32"])
    nsc = ns * TC // 128
    kt = [P.sb([128, ns * TC], BF16, name=f"kt{i}") for i in range(2)]
    vt = [P.sb([128, nsc, 128], BF16, name=f"vt{i}") for i in range(2)]
    qt = [P.sb([128, 512], BF16) for _ in range(3)]
    sgq = [P.sb([128, 512], BF16) for _ in range(3)]
    pt = [P.sb([128, 2, 512], BF16) for _ in range(3)]
    accD = [P.sb([128, 2, 512], F32) for _ in range(2)]
    accP = [P.sb([128, 2, 512], F32) for _ in range(2)]
    rinv = [P.sb([128, 512], F32) for _ in range(2)]
    zt = [P.sb([128, 512], BF16) for _ in range(2)]
    st = [P.ps([128, 2, 512], F32) for _ in range(2)]
    ot = [P.ps([128, 512], F32) for _ in range(2)]
    rsp = [P.ps([128, 512], F32) for _ in range(1)]
    it = 0
    sti = 0
    pti = 0
    cps = TC // 128
    npair = nsc // 2
    for kvh in range(4):
        kb = kvh % 2
        for r in range(ns):
            P.dma("sp", kt[kb][:, r * TC:(r + 1) * TC], KTs[r][kvh], reads=[("KT", r, kvh, t) for t in range(NT)],
                  writes=[("kt", kb, r)])
            P.dma("sp", vt[kb][:, r * cps:(r + 1) * cps, :],
                  Vss[r][:, kvh * 128:(kvh + 1) * 128].rearrange("(c p) d -> p c d", p=128),
                  reads=[("V", r, t) for t in range(NT)], writes=[("vt", kb, r)])
        for gq_ in range(4):
            hd = kvh * 4 + gq_
            for (si, t) in qlist:
                ab = it % 2
                q3 = it % 3
                it += 1
                P.dma("sp", qt[q3][:], Qs[si][hd][:, t * TT:(t + 1) * TT], reads=[("Qs", si, hd, t)], writes=[("qt", q3)])
                P.dma("sp", sgq[q3][:], SG[si][hd * 128:(hd + 1) * 128, t * TT:(t + 1) * TT],
                      reads=[("SG", si, hd, t)], writes=[("sgq", q3)])
                LOOK = 1
                p3s = {}
                used = {"dve": False, "pool": False}
                for step in range(npair + LOOK):
                    if step < npair:
                        pr = step
                        s2 = sti % 2
                        sti += 1
                        p3 = pti % 3
                        pti += 1
                        p3s[pr] = p3
                        for j in range(2):
                            sc = 2 * pr + j
                            P.pe(lambda e, s2=s2, kb=kb, sc=sc, q3=q3, j=j: e.matmul(
                                st[s2][:, j, :], lhsT=kt[kb][:, sc * 128:(sc + 1) * 128], rhs=qt[q3][:], start=True, stop=True),
                                reads=[("kt", kb, sc // cps), ("qt", q3)], writes=[("st", s2, j)])
                        P.act(lambda e, s2=s2, p3=p3: e.activation(out=pt[p3][:], in_=st[s2][:], func=AF.Exp, scale=A_SCALE),
                              reads=[("st", s2, 0), ("st", s2, 1)], writes=[("pt", p3)])
                    if step >= LOOK:
                        pr = step - LOOK
                        p3 = p3s[pr]
                        for j in range(2):
                            sc = 2 * pr + j
                            P.pe(lambda e, ab=ab, kb=kb, sc=sc, p3=p3, j=j: e.matmul(
                                ot[ab][:], lhsT=vt[kb][:, sc, :], rhs=pt[p3][:, j, :], start=(sc == 0), stop=(sc == nsc - 1)),
                                reads=[("vt", kb, sc // cps), ("pt", p3)], writes=[("ot", ab)])
                        eng, acc, akey = ("dve", accD[ab], ("accD", ab)) if pr % 3 == 0 else ("pool", accP[ab], ("accP", ab))
                        if not used[eng]:
                            used[eng] = True
                            P.add(eng, lambda e, acc=acc, p3=p3: e.tensor_copy(out=acc[:], in_=pt[p3][:]),
                                  reads=[("pt", p3)], writes=[akey])
                        else:
                            P.add(eng, lambda e, acc=acc, p3=p3: e.tensor_tensor(out=acc[:], in0=acc[:], in1=pt[p3][:], op=ALU.add),
                                  reads=[("pt", p3), akey], writes=[akey])
                n4 = 0
                for acc, akey in ((accD[ab], ("accD", ab)), (accP[ab], ("accP", ab))):
                    for j in range(2):
                        P.pe(lambda e, acc=acc, j=j, n4=n4: e.matmul(rsp[0][:], lhsT=ones32[:], rhs=acc[:, j, :],
                                                                     start=(n4 == 0), stop=(n4 == 3)),
                             reads=["ones32", akey], writes=[("rsp", 0)])
                        n4 += 1
                P.dve(lambda e, ab=ab: e.reciprocal(out=rinv[ab][:], in_=rsp[0][:]), reads=[("rsp", 0)], writes=[("rinv", ab)])
                P.pool(lambda e, ab=ab, q3=q3: e.tensor_tensor(out=rinv[ab][:], in0=rinv[ab][:], in1=sgq[q3][:], op=ALU.mult),
                       reads=[("rinv", ab), ("sgq", q3)], writes=[("rinv", ab)])
                P.dve(lambda e, ab=ab: e.tensor_tensor(out=zt[ab][:], in0=ot[ab][:], in1=rinv[ab][:], op=ALU.mult),
                      reads=[("ot", ab), ("rinv", ab)], writes=[("zt", ab)])
                P.dma("pool", Zs[si][hd * 128:(hd + 1) * 128, t * TT:(t + 1) * TT], zt[ab][:],
                      reads=[("zt", ab)], writes=[("Zs", si, hd, t)])
    P.release(m0)

    C = Common(P, io["mod"])
    ws2 = WStream(P, [slab_src(wb_out, 256 * j) for j in range(8)] * len(qlist), ring=4, name="wsC")
    mmC = [P.ps([128, 512], F32) for _ in range(4)]
    z = [P.sb([128, KC, 512], BF16) for _ in range(2)]
    xres = [P.sb([128, 512], F32) for _ in range(3)]
    xo_t = [P.sb([128, 512], F32) for _ in range(3)]
    mmi = 0
    for qi, (si, t) in enumerate(qlist):
        xT, xo = segs[si]["xT"], segs[si]["xo"]
        zb = qi % 2
        for k in range(KC):
            P.dma("pool", z[zb][:, k, :], Zs[si][k * 128:(k + 1) * 128, t * TT:(t + 1) * TT],
                  reads=[("Zs", si, k, t)], writes=[("zc", zb, k)])
        c0 = TT * t

        def load_res(m, xT=xT, c0=c0):
            rb = m % len(xres)
            P.dma("pool", xres[rb][:], xT[m * 128:(m + 1) * 128, c0:c0 + 512], writes=[("xres", rb)])
        for m in range(len(xres)):
            load_res(m)
        for sl in range(8):
            slab, skey = ws2.next()
            for ml in range(2):
                m = sl * 2 + ml
                ps, pk = mmC[mmi % 4], ("mmC", mmi % 4)
                mmi += 1
                for k in range(KC):
                    P.pe(lambda e, ps=ps, slab=slab, k=k, ml=ml, zb=zb: e.matmul(
                        ps[:], lhsT=slab[:, k, ml * 128:(ml + 1) * 128], rhs=z[zb][:, k, :],
                        start=(k == 0), stop=(k == KC - 1)), reads=[skey, ("zc", zb, k)], writes=[pk])
                rb = m % len(xres)
                ob = m % len(xo_t)
                P.dve(lambda e, ps=ps, m=m, rb=rb, ob=ob, C=C: e.scalar_tensor_tensor(
                    out=xo_t[ob][:], in0=ps[:], scalar=C.gate(m), in1=xres[rb][:],
                    op0=ALU.mult, op1=ALU.add), reads=[pk, "mod", ("xres", rb)], writes=[("xo_t", ob)])
                P.dma("pool", xo[m * 128:(m + 1) * 128, TT * t:TT * (t + 1)], xo_t[ob][:],
                      reads=[("xo_t", ob)], writes=[("xo", si, t, m)])
                if m + len(xres) < KC:
                    load_res(m + len(xres))
    P.release(m0)


def emit_modulation(P, nc, io, moddram):
    m0 = P.mark()
    ct = P.sb([128, 16, 1], F32, name="m_ct")
    sc = P.sb([128, 16, 1], F32, name="m_sc")
    one = P.sb([1, 1], F32, name="m_one")
    P.dve(lambda e: e.memset(one[:], 1.0), writes=["m_one"])
    P.dma("sp", ct[:], io["cT"], writes=["m_ct"])
    P.act(lambda e: e.activation(out=sc[:], in_=ct[:], func=AF.Silu), reads=["m_ct"], writes=["m_sc"])
    wt = [P.sb([128, 3072], F32) for _ in range(3)]
    brow = P.sb([1, 6144], F32, name="m_brow")
    mrow = P.sb([1, 6144], F32, name="m_mrow")
    modsb = P.sb([128, 48], F32, name="m_modsb")
    acc = [P.ps([1, 512], F32) for _ in range(6)]
    tp = P.ps([128, 48], F32, name="m_tp")
    wi = 0
    for l in range(4):
        P.dma("sp", brow[:], io["b_mod"][l:l + 1, :], writes=["m_brow"])
        for hf in range(2):
            for kc in range(16):
                b = wi % 3
                wi += 1
                P.dma("sp", wt[b][:], io["w_mod"][l, kc * 128:(kc + 1) * 128, hf * 3072:(hf + 1) * 3072], writes=[("m_wt", b)])
                for n in range(6):
                    P.pe(lambda e, kc=kc, n=n, b=b: e.matmul(acc[n][:], lhsT=sc[:, kc, :], rhs=wt[b][:, n * 512:(n + 1) * 512],
                                                             start=(kc == 0), stop=(kc == 15)),
                         reads=["m_sc", ("m_wt", b)], writes=[("m_acc", n)])
            for n in range(6):
                cs_ = slice(hf * 3072 + n * 512, hf * 3072 + (n + 1) * 512)
                P.dve(lambda e, n=n, cs_=cs_: e.tensor_tensor(out=mrow[:, cs_], in0=acc[n][:], in1=brow[:, cs_], op=ALU.add),
                      reads=[("m_acc", n), "m_brow"], writes=[("m_mrow", hf, n)])
        for j in range(48):
            P.pe(lambda e, j=j: e.matmul(tp[:, j:j + 1], lhsT=mrow[0:1, j * 128:(j + 1) * 128], rhs=one[0:1, 0:1],
                                         start=True, stop=True),
                 reads=[("m_mrow", hf, n) for hf in range(2) for n in range(6)] + ["m_one"], writes=["m_tp"])
        P.dve(lambda e: e.tensor_copy(out=modsb[:], in_=tp[:]), reads=["m_tp"], writes=["m_modsb"])
        P.dma("pool", moddram[l].rearrange("p s k -> p (s k)"), modsb[:], reads=["m_modsb"], writes=[("moddram", l)])
    P.release(m0)


SEQ = 8192
_NC_CACHE = {}


def _new_nc():
    return bass.Bass("TRN2", target_bir_lowering=False)


def build_fused():
    nc = _new_nc()
    E = {}

    def inp(name, shape, dt=F32):
        E[name] = nc.dram_tensor(name, shape, dt, kind="ExternalInput").ap()
    for s in "AB":
        inp("xT" + s, [2048, TC + 16]); inp("hmask" + s, [128, 2]); inp("icnt" + s, [128, 4, 16])
        inp("cos" + s, [128, TC]); inp("sin" + s, [128, TC])
    inp("cT", [128, 16, 1]); inp("w_mod", [4, 2048, 6144]); inp("b_mod", [4, 6144])
    for p in ("p0", "p3"):
        inp(p + "_w_in", [2048, 4096]); inp(p + "_w_grp", [2048, 512]); inp(p + "_w_out", [2048, 2048])
        inp(p + "_pscale", [128, 16])
    inp("fg", [128, 16])
    inp("g_w_in", [2048, 6144]); inp("g_w_out", [2048, 2048]); inp("w1cat", [2048, 64]); inp("w2aug", [64, 1024])
    inp("trif", [128, 128]); inp("trib", [128, 128]); inp("maskf", [128, 128]); inp("maskb", [128, 128])
    inp("ident", [128, 128]); inp("ng", [128, 4])
    inp("a_w_in", [2048, 5120]); inp("a_w_out", [2048, 2048]); inp("gains", [128, 2]); inp("rotT", [128, 128])
    y = nc.dram_tensor("y", [2048, TC], F32, kind="ExternalOutput").ap()

    def scr(name, shape, dt=F32):
        return nc.dram_tensor(name, shape, dt).ap()
    moddram = scr("moddram", [4, 128, 3, 16])
    X1 = [scr("X1A", [2048, TC]), scr("X1B", [2048, TC])]
    X2 = [scr("X2A", [2048, TC]), scr("X2B", [2048, TC])]
    X3A = scr("X3A", [2048, TC + 16])
    X3B = scr("X3B", [2048, TC])
    Se = [{d: scr(f"Se{s}{d}", [128, 8, 512]) for d in "fb"} for s in "AB"]

    P = Prog(nc)
    emit_modulation(P, nc, E, moddram)

    wb = None
    for si, s in enumerate("AB"):
        io = {"xT": E["xT" + s], "mod": moddram[0], "w_in": E["p0_w_in"], "w_grp": E["p0_w_grp"], "w_out": E["p0_w_out"],
              "pscale": E["p0_pscale"], "hmask": E["hmask" + s], "icnt": E["icnt" + s], "xo": X1[si]}
        wb = emit_pool_layer(P, nc, io, final=False, pfx="p0", wb=wb)

    gio = {"mod": moddram[1], "w_in": E["g_w_in"], "w_out": E["g_w_out"], "w1cat": E["w1cat"], "w2aug": E["w2aug"],
           "trif": E["trif"], "trib": E["trib"], "maskf": E["maskf"], "maskb": E["maskb"], "ident": E["ident"],
           "ng": E["ng"]}
    scrs = []
    gwb = None
    for si, s in enumerate("AB"):
        io = dict(gio, xT=X1[si], S_end_f=Se[si]["f"], S_end_b=Se[si]["b"])
        r = emit_gla_layer(P, nc, io, "state", pfx="g" + s, wb=gwb)
        gwb = r["wb"]
        scrs.append(r)
    for si, s in enumerate("AB"):
        io = dict(gio, xT=X1[si], hmask=E["hmask" + s], S_in_f=Se[1 - si]["f"], S_in_b=Se[1 - si]["b"], xo=X2[si])
        emit_gla_layer(P, nc, io, "sweep3", pfx="g" + s, wb=gwb, scr=scrs[si])

    segs = [dict(xT=X2[0], cos=E["cosA"], sin=E["sinA"], qtiles=list(range(NT)), xo=X3A[:, 8:8 + TC]),
            dict(xT=X2[1], cos=E["cosB"], sin=E["sinB"], qtiles=[0, NT - 1], xo=X3B)]
    aio = {"mod": moddram[2], "w_in": E["a_w_in"], "w_out": E["a_w_out"], "gains": E["gains"], "rotT": E["rotT"]}
    emit_attn_fused(P, nc, segs, aio, pfx="f")
    P.dma("sp", X3A[:, 0:8], X3B[:, TC - 8:TC], writes=["halo_l"], allow_slow_non_contiguous=True)
    P.dma("sp", X3A[:, TC + 8:TC + 16], X3B[:, 0:8], writes=["halo_r"], allow_slow_non_contiguous=True)

    io = {"xT": X3A, "mod": moddram[3], "w_in": E["p3_w_in"], "w_grp": E["p3_w_grp"], "w_out": E["p3_w_out"],
          "pscale": E["p3_pscale"], "hmask": E["hmaskA"], "icnt": E["icntA"], "fg": E["fg"], "xo": y}
    emit_pool_layer(P, nc, io, final=True, pfx="p3")
    P.emit()
    return nc


def _rope_tables(t0):
    t = np.arange(t0, t0 + TC)
    row = (t // 64 - (SEQ // 64) // 2).astype(np.float32)
    col = (t % 64 - 32).astype(np.float32)
    inv = (np.float32(10000.0) ** (-np.arange(0, 64, 2, dtype=np.float32) / np.float32(64))).astype(np.float32)
    ang = np.concatenate([row[:, None] * inv, col[:, None] * inv], axis=-1)
    cos = np.repeat(np.cos(ang).astype(np.float32), 2, axis=1).T
    sin = np.repeat(np.sin(ang).astype(np.float32), 2, axis=1).T
    return np.ascontiguousarray(cos), np.ascontiguousarray(sin)


def _rot_const():
    r = np.zeros((128, 128), np.float32)
    for i in range(64):
        r[2 * i + 1, 2 * i] = -1.0
        r[2 * i, 2 * i + 1] = 1.0
    return r


def _gla_consts():
    j = np.arange(128)[:, None]
    i = np.arange(128)[None, :]
    return {
        "trif": np.where(j <= i, -1.0 / 16, 0.0).astype(np.float32),
        "trib": np.where(j >= i, -1.0 / 16, 0.0).astype(np.float32),
        "maskf": (i >= j).astype(np.float32),
        "maskb": (i < j).astype(np.float32),
        "ident": np.eye(128, dtype=np.float32),
    }


def _pool_consts(h):
    t0 = h * TC
    ic = np.zeros((4, 16), np.float32)
    for gi in range(4):
        w = 2 << gi
        for j in range(8):
            for side, t in ((0, t0 + j), (1, t0 + TC - 8 + j)):
                cnt = min(t + w // 2, SEQ) - max(t - w // 2, 0)
                ic[gi, side * 8 + j] = 1.0 / cnt
    hmask = np.broadcast_to(np.array([float(h == 1), float(h == 0)], np.float32), (128, 2))
    return np.ascontiguousarray(hmask), np.ascontiguousarray(np.broadcast_to(ic, (128, 4, 16)))


def _col(v, k=16):
    return np.ascontiguousarray(np.asarray(v, np.float32).reshape(k, 128).T)


def _halo_x(x, b, h):
    t0 = h * TC
    xp = np.zeros((TC + 16, 2048), np.float32)
    lo, hi = max(t0 - 8, 0), min(t0 + TC + 8, SEQ)
    xp[lo - (t0 - 8):hi - (t0 - 8)] = x[b, lo:hi]
    return np.ascontiguousarray(xp.T)


def kernel(x, c, w_mod, b_mod, pool_w_in, pool_w_grp, pool_scale, pool_w_out,
           gla_w_in, gla_fwd_w1, gla_fwd_w2, gla_fwd_b, gla_bwd_w1, gla_bwd_w2, gla_bwd_b,
           gla_norm_g, gla_w_out, attn_w_in, attn_q_norm_g, attn_k_norm_g, attn_w_out, final_norm_g):
    f32 = lambda a: np.ascontiguousarray(np.asarray(a, dtype=np.float32))
    x = f32(x)
    c = f32(c)
    w1 = np.zeros((2048, 64), np.float32)
    w1[:, 0:16] = gla_fwd_w1[0]
    w1[:, 32:48] = gla_bwd_w1[0]
    w2 = np.zeros((64, 1024), np.float32)
    w2[0:16] = gla_fwd_w2[0]
    w2[16] = gla_fwd_b[0]
    w2[32:48] = gla_bwd_w2[0]
    w2[48] = gla_bwd_b[0]
    shared = {
        "w_mod": f32(w_mod), "b_mod": f32(b_mod),
        "p0_w_in": f32(pool_w_in[0]), "p0_w_grp": f32(np.asarray(pool_w_grp[0]).reshape(2048, 512)),
        "p0_w_out": f32(pool_w_out[0]), "p0_pscale": _col(pool_scale[0]),
        "p3_w_in": f32(pool_w_in[1]), "p3_w_grp": f32(np.asarray(pool_w_grp[1]).reshape(2048, 512)),
        "p3_w_out": f32(pool_w_out[1]), "p3_pscale": _col(pool_scale[1]),
        "fg": _col(final_norm_g),
        "g_w_in": f32(gla_w_in[0]), "g_w_out": f32(gla_w_out[0]), "w1cat": w1, "w2aug": w2, "ng": _col(gla_norm_g[0], 4),
        "a_w_in": f32(attn_w_in[0]), "a_w_out": f32(attn_w_out[0]),
        "gains": np.ascontiguousarray(np.stack([np.asarray(attn_q_norm_g[0], np.float32),
                                                np.asarray(attn_k_norm_g[0], np.float32)], axis=1)),
        "rotT": _rot_const(),
    }
    shared.update(_gla_consts())
    rope = [_rope_tables(0), _rope_tables(TC)]
    pc = [_pool_consts(0), _pool_consts(1)]
    in_maps = []
    for core in range(8):
        b, h = core // 2, core % 2
        d = dict(shared)
        for s, hh in (("A", h), ("B", 1 - h)):
            d["xT" + s] = _halo_x(x, b, hh)
            d["hmask" + s], d["icnt" + s] = pc[hh]
            d["cos" + s], d["sin" + s] = rope[hh]
        d["cT"] = np.ascontiguousarray(c[b].reshape(16, 128).T.reshape(128, 16, 1))
        in_maps.append(d)
    if "fused" not in _NC_CACHE:
        _NC_CACHE["fused"] = build_fused()
    res = run_bass_kernel_spmd(_NC_CACHE["fused"], in_maps, core_ids=list(range(8)))
    out = np.empty((4, SEQ, 2048), np.float32)
    for core in range(8):
        out[core // 2, (core % 2) * TC:(core % 2 + 1) * TC] = np.asarray(res.results[core]["y"]).T
    return out
```

```python
import numpy as np
import concourse.bass as bass
import concourse.mybir as mybir
from concourse.bass_utils import run_bass_kernel_spmd

F32 = mybir.dt.float32
BF16 = mybir.dt.bfloat16
ALU = mybir.AluOpType
AF = mybir.ActivationFunctionType
AX = mybir.AxisListType

ENGS = ("pe", "act", "dve", "pool", "sp")
SEM_CAP = 9000


class _Op:
    __slots__ = ("eng", "fn", "dma", "seq", "deps", "signal", "sig_idx", "slot", "cnt", "waits", "id")


class Prog:
    def __init__(self, nc, ring=8):
        self.nc = nc
        self.ring = ring
        self.ops = []
        self.eng_ops = {e: [] for e in ENGS}
        self.last_w = {}
        self.readers = {}
        self.n_dma = {e: 0 for e in ENGS}
        self._nm = 0
        self.pending = {e: set() for e in ENGS}
        self.dma_ops = {e: [] for e in ENGS}

    def sb(self, shape, dtype, name=None):
        self._nm += 1
        return self.nc.alloc_sbuf_tensor(f"s{self._nm}_{name or ""}", list(shape), dtype)

    def ps(self, shape, dtype=F32, name=None):
        self._nm += 1
        return self.nc.alloc_psum_tensor(f"p{self._nm}_{name or ""}", list(shape), dtype)

    def add(self, eng, fn, reads=(), writes=(), dma=False):
        op = _Op()
        op.eng, op.fn, op.dma = eng, fn, dma
        op.id = len(self.ops)
        op.seq = len(self.eng_ops[eng])
        op.signal = dma
        op.sig_idx = None
        op.slot = op.cnt = None
        deps = set()
        for r in reads:
            w = self.last_w.get(r)
            if w is not None:
                deps.add(w)
        for k in writes:
            w = self.last_w.get(k)
            if w is not None:
                deps.add(w)
            for rd in self.readers.get(k, ()):
                deps.add(rd)
        for r in reads:
            self.readers.setdefault(r, []).append(op.id)
        for k in writes:
            self.last_w[k] = op.id
            self.readers[k] = []
        deps.discard(op.id)
        if self.pending[eng]:
            deps |= self.pending[eng]
            self.pending[eng] = set()
        if dma:
            self.dma_ops[eng].append(op.id)
            j = self.n_dma[eng]
            self.n_dma[eng] += 1
            op.slot = j % self.ring
            op.cnt = j // self.ring + 1
        op.deps = deps
        self.ops.append(op)
        self.eng_ops[eng].append(op)
        return op

    def barrier(self):
        front = set()
        for e in ENGS:
            comp = [o for o in self.eng_ops[e] if not o.dma]
            if comp:
                front.add(comp[-1].id)
            for d in self.dma_ops[e][-self.ring:]:
                front.add(d)
        for e in ENGS:
            self.pending[e] |= front

    def mark(self):
        nc = self.nc
        return (nc.sbuf_base, nc.sbuf_top, nc.psum_base, nc.psum_top)

    def release(self, m):
        self.barrier()
        nc = self.nc
        nc.sbuf_base, nc.sbuf_top, nc.psum_base, nc.psum_top = m

    def pe(self, fn, reads=(), writes=()):
        return self.add("pe", fn, reads, writes)

    def act(self, fn, reads=(), writes=()):
        return self.add("act", fn, reads, writes)

    def dve(self, fn, reads=(), writes=()):
        return self.add("dve", fn, reads, writes)

    def pool(self, fn, reads=(), writes=()):
        return self.add("pool", fn, reads, writes)

    def dma(self, q, out, in_, reads=(), writes=(), **kw):
        return self.add(q, lambda e: e.dma_start(out=out, in_=in_, **kw), reads, writes, dma=True)

    def emit(self):
        nc = self.nc
        ops = self.ops
        seen_c = {e: {f: -1 for f in ENGS} for e in ENGS}
        seen_d = {e: {} for e in ENGS}
        for op in ops:
            waits = []
            e = op.eng
            if op.dma and op.cnt > 1:
                key = (e, op.slot)
                if seen_d[e].get(key, 0) < op.cnt - 1:
                    seen_d[e][key] = op.cnt - 1
                    waits.append(("d", e, op.slot, op.cnt - 1))
            cmax = {}
            for d in op.deps:
                p = ops[d]
                if not p.dma and (p.eng not in cmax or ops[cmax[p.eng]].seq < p.seq):
                    cmax[p.eng] = d
            for d in sorted(op.deps):
                p = ops[d]
                if not p.dma and cmax[p.eng] != d:
                    continue
                if p.dma:
                    key = (p.eng, p.slot)
                    if seen_d[e].get(key, 0) >= p.cnt:
                        continue
                    seen_d[e][key] = p.cnt
                    waits.append(("d", p.eng, p.slot, p.cnt))
                else:
                    if p.eng == "pe" and e == "pe":
                        continue
                    if seen_c[e][p.eng] >= p.seq:
                        continue
                    seen_c[e][p.eng] = p.seq
                    p.signal = True
                    waits.append(("c", d))
            op.waits = waits
        nsig = {}
        for e in ENGS:
            n = 0
            for op in self.eng_ops[e]:
                if op.signal and not op.dma:
                    n += 1
                    op.sig_idx = n
            nsig[e] = n
        csem = {}
        for e in ENGS:
            k = (nsig[e] + SEM_CAP - 1) // SEM_CAP
            csem[e] = [nc.alloc_semaphore(f"c_{e}_{i}") for i in range(k)]
        dsem = {}
        for e in ENGS:
            if self.n_dma[e]:
                dsem[e] = [nc.alloc_semaphore(f"d_{e}_{i}") for i in range(min(self.ring, self.n_dma[e]))]
        self.stats = {e: (len(self.eng_ops[e]), nsig[e], self.n_dma[e]) for e in ENGS}

        def run(e, eng):
            for op in self.eng_ops[e]:
                for w in op.waits:
                    if w[0] == "d":
                        eng.wait_ge(dsem[w[1]][w[2]], 16 * w[3])
                    else:
                        p = ops[w[1]]
                        i = p.sig_idx - 1
                        eng.wait_ge(csem[p.eng][i // SEM_CAP], i % SEM_CAP + 1)
                ins = op.fn(eng)
                if op.dma:
                    ins.then_inc(dsem[e][op.slot], 16)
                elif op.signal:
                    i = op.sig_idx - 1
                    ins.then_inc(csem[e][i // SEM_CAP], 1)
            if self.n_dma[e]:
                for s in range(min(self.ring, self.n_dma[e])):
                    last = (self.n_dma[e] - 1 - s) // self.ring + 1
                    eng.wait_ge(dsem[e][s], 16 * last)

        with nc.Block() as block:
            @block.tensor
            def _(eng):
                run("pe", eng)

            @block.scalar
            def _(eng):
                run("act", eng)

            @block.vector
            def _(eng):
                run("dve", eng)

            @block.gpsimd
            def _(eng):
                run("pool", eng)

            @block.sync
            def _(eng):
                run("sp", eng)


D = 2048
KC = 16
TC = 4096
TT = 512
NT = TC // TT
EPS = 1e-6
SLABW = 256


class WStream:
    def __init__(self, P, srcs, ring=4, q="sp", name="ws"):
        self.P, self.srcs, self.ring, self.q, self.name = P, srcs, ring, q, name
        self.bufs = [P.sb([128, KC, SLABW], BF16, name=f"{name}{i}") for i in range(ring)]
        self.issued = 0
        self.pos = 0

    def _issue(self):
        i = self.issued
        if i >= len(self.srcs):
            return
        s = i % self.ring
        self.P.dma(self.q, self.bufs[s][:], self.srcs[i], writes=[(self.name, s)])
        self.issued += 1

    def prefetch(self, n=None):
        n = self.ring if n is None else n
        while self.issued < min(self.pos + n, len(self.srcs)):
            self._issue()

    def next(self):
        self.prefetch(self.ring)
        i = self.pos
        self.pos += 1
        s = i % self.ring
        return self.bufs[s], (self.name, s)


def slab_src(wb, n0):
    return wb.rearrange("(k p) n -> p k n", p=128)[:, :, n0:n0 + SLABW]


def convert_w(P, src, dst, K, N, tag):
    CB = 2048
    st32 = [P.sb([128, CB], F32) for _ in range(2)]
    st16 = [P.sb([128, CB], BF16) for _ in range(2)]
    i = 0
    for kb in range(K // 128):
        for c0 in range(0, N, CB):
            cw = min(CB, N - c0)
            b = i % 2
            P.dma("sp", st32[b][:, :cw], src[kb * 128:(kb + 1) * 128, c0:c0 + cw], writes=[(tag, "s32", b)])
            eng = ("dve", "pool", "act")[i % 3]
            if eng == "act":
                P.act(lambda e, b=b, cw=cw: e.copy(out=st16[b][:, :cw], in_=st32[b][:, :cw]),
                      reads=[(tag, "s32", b)], writes=[(tag, "s16", b)])
            else:
                P.add(eng, lambda e, b=b, cw=cw: e.tensor_copy(out=st16[b][:, :cw], in_=st32[b][:, :cw]),
                      reads=[(tag, "s32", b)], writes=[(tag, "s16", b)])
            P.dma("pool", dst[kb * 128:(kb + 1) * 128, c0:c0 + cw], st16[b][:, :cw],
                  reads=[(tag, "s16", b)], writes=[(tag, "dram", i)])
            i += 1


class Common:
    def __init__(self, P, mod_ap):
        self.P = P
        self.ones = P.sb([128, 128], BF16, name="ones")
        P.dve(lambda e: e.memset(self.ones[:], 1.0), writes=["ones"])
        self.mod = P.sb([128, 3, KC], F32, name="mod")
        P.dma("sp", self.mod[:], mod_ap, writes=["mod"])
        self.s1 = P.sb([128, KC], F32, name="s1")
        P.dve(lambda e: e.tensor_scalar_add(out=self.s1[:], in0=self.mod[:, 1, :], scalar1=1.0),
              reads=["mod"], writes=["s1"])

    def shift(self, k):
        return self.mod[:, 0, k:k + 1]

    def gate(self, k):
        return self.mod[:, 2, k:k + 1]


def emit_norm_mod(P, C, xT, c_lo, ncol, hT, hkey, bufs):
    xs, sq, tt, ss_a, ss_b, rstd = bufs["xs"], bufs["sq"], bufs["tt"], bufs["ss_a"], bufs["ss_b"], bufs["rstd"]
    na = min(ncol, 512)
    nb = ncol - na
    for k in range(KC):
        b = k % len(xs)
        P.dma("sp", xs[b][:, :ncol], xT[k * 128:(k + 1) * 128, c_lo:c_lo + ncol], writes=[("xs", b)])
        sb_ = k % len(sq)
        P.act(lambda e, b=b, sb_=sb_: e.activation(out=sq[sb_][:, :ncol], in_=xs[b][:, :ncol], func=AF.Square),
              reads=[("xs", b)], writes=[("sq", sb_)])
        P.pe(lambda e, sb_=sb_, k=k: e.matmul(ss_a[:, :na], lhsT=C.ones[:], rhs=sq[sb_][:, :na],
                                              start=(k == 0), stop=(k == KC - 1)),
             reads=["ones", ("sq", sb_)], writes=["ss_a"])
        if nb:
            P.pe(lambda e, sb_=sb_, k=k: e.matmul(ss_b[:, :nb], lhsT=C.ones[:], rhs=sq[sb_][:, na:ncol],
                                                  start=(k == 0), stop=(k == KC - 1)),
                 reads=["ones", ("sq", sb_)], writes=["ss_b"])
    P.dve(lambda e: e.tensor_scalar(out=rstd[:, :na], in0=ss_a[:, :na], scalar1=1.0 / D, scalar2=EPS,
                                    op0=ALU.mult, op1=ALU.add), reads=["ss_a"], writes=["rstd"])
    if nb:
        P.dve(lambda e: e.tensor_scalar(out=rstd[:, na:ncol], in0=ss_b[:, :nb], scalar1=1.0 / D, scalar2=EPS,
                                        op0=ALU.mult, op1=ALU.add), reads=["ss_b", "rstd"], writes=["rstd"])
    P.dve(lambda e: e.reciprocal(out=rstd[:, :ncol], in_=rstd[:, :ncol]), reads=["rstd"], writes=["rstd"])
    P.act(lambda e: e.activation(out=rstd[:, :ncol], in_=rstd[:, :ncol], func=AF.Sqrt), reads=["rstd"], writes=["rstd"])
    for k in range(KC):
        b = k % len(xs)
        P.dma("sp", xs[b][:, :ncol], xT[k * 128:(k + 1) * 128, c_lo:c_lo + ncol], writes=[("xs", b)])
        tb = k % len(tt)
        P.dve(lambda e, b=b, tb=tb, k=k: e.scalar_tensor_tensor(
            out=tt[tb][:, :ncol], in0=xs[b][:, :ncol], scalar=C.s1[:, k:k + 1], in1=rstd[:, :ncol],
            op0=ALU.mult, op1=ALU.mult), reads=[("xs", b), "s1", "rstd"], writes=[("tt", tb)])
        P.act(lambda e, tb=tb, k=k: e.activation(out=hT[:, k, :ncol], in_=tt[tb][:, :ncol], func=AF.Identity,
                                                 bias=C.shift(k), scale=1.0),
              reads=[("tt", tb), "mod"], writes=[(hkey, k)])


def norm_bufs(P, width=528):
    return dict(
        xs=[P.sb([128, width], F32) for _ in range(4)],
        sq=[P.sb([128, width], BF16) for _ in range(2)],
        tt=[P.sb([128, width], F32) for _ in range(2)],
        ss_a=P.ps([128, 512], F32), ss_b=(P.ps([128, 16], F32) if width > 512 else None),
        rstd=P.sb([128, width], F32),
    )


def emit_pool_layer(P, nc, io, final=False, pfx="", wb=None):
    xT = io["xT"]
    m0 = P.mark()
    if wb is None:
        wb_in = nc.dram_tensor(pfx + "wb_in", [D, 4096], BF16).ap()
        wb_grp = nc.dram_tensor(pfx + "wb_grp", [D, 512], BF16).ap()
        wb_out = nc.dram_tensor(pfx + "wb_out", [D, D], BF16).ap()
        convert_w(P, io["w_in"], wb_in, D, 4096, "cw_in")
        convert_w(P, io["w_grp"], wb_grp, D, 512, "cw_grp")
        convert_w(P, io["w_out"], wb_out, D, D, "cw_out")
        wb = (wb_in, wb_grp, wb_out)
    wb_in, wb_grp, wb_out = wb
    P.release(m0)

    C = Common(P, io["mod"])
    pscale = P.sb([128, KC], F32, name="pscale")
    P.dma("sp", pscale[:], io["pscale"], writes=["pscale"])
    hmask = P.sb([128, 2], F32, name="hmask")
    P.dma("sp", hmask[:], io["hmask"], writes=["hmask"])
    icnt = P.sb([128, 4, 16], F32, name="icnt")
    P.dma("sp", icnt[:], io["icnt"], writes=["icnt"])
    if final:
        fg = P.sb([128, KC], F32, name="fg")
        P.dma("sp", fg[:], io["fg"], writes=["fg"])
    wg = P.sb([128, 4, 4, 512], BF16, name="wg")
    for g in range(4):
        P.dma("sp", wg[:, g, :, :], wb_grp[g * 512:(g + 1) * 512, :].rearrange("(k p) n -> p k n", p=128),
              writes=[("wg", g)])

    order = []
    for gi in range(4):
        order += [slab_src(wb_in, 512 * gi), slab_src(wb_in, 512 * gi + 256)]
        order += [slab_src(wb_in, 2048 + 512 * gi), slab_src(wb_in, 2048 + 512 * gi + 256)]
    for j in range(8):
        order.append(slab_src(wb_out, 256 * j))
    ws = WStream(P, order * NT, ring=4)

    nb = norm_bufs(P)
    hT = [P.sb([128, KC, 528], BF16, name=f"hT{i}") for i in range(2)]
    U = P.sb([128, 4, 528], F32, name="U")
    Ta = P.sb([128, 4, 528], F32, name="Ta")
    Tb = P.sb([128, 4, 528], F32, name="Tb")
    sg = P.sb([128, 4, 512], BF16, name="sg")
    pooled = P.sb([128, 4, 512], BF16, name="pooled")
    z = P.sb([128, KC, 512], BF16, name="z")
    xres = [P.sb([128, 512], F32) for _ in range(3)]
    xo_t = [P.sb([128, 512], F32) for _ in range(3)]
    mm = [P.ps([128, 512], F32) for _ in range(4)]
    psB = [P.ps([128, 4, 16], F32) for _ in range(2)]
    if final:
        xn = P.sb([128, KC, 512], F32, name="xn")
        fsq = [P.sb([128, 512], BF16) for _ in range(2)]
    mmi = [0]
    UK = [("U", i) for i in range(4)] + ["Uh"]

    def next_mm():
        i = mmi[0] % len(mm)
        mmi[0] += 1
        return mm[i], ("mm", i)

    def hk(hb, k):
        return (("hT", hb), k)

    def mixer(t):
        hb = t % 2
        h = hT[hb]
        for gi in range(4):
            w = 2 << gi
            pb = psB[gi % 2]
            pbk = ("psB", gi % 2)
            for sl in range(2):
                slab, skey = ws.next()
                for ml in range(2):
                    m4 = sl * 2 + ml
                    ps, pk = next_mm()
                    for k in range(KC):
                        P.pe(lambda e, ps=ps, slab=slab, k=k, ml=ml, h=h: e.matmul(
                            ps[:], lhsT=slab[:, k, ml * 128:(ml + 1) * 128], rhs=h[:, k, 0:512],
                            start=(k == 0), stop=(k == KC - 1)), reads=[skey, hk(hb, k)], writes=[pk])
                        P.pe(lambda e, pb=pb, slab=slab, k=k, ml=ml, h=h, m4=m4: e.matmul(
                            pb[:, m4, :], lhsT=slab[:, k, ml * 128:(ml + 1) * 128], rhs=h[:, k, 512:528],
                            start=(k == 0), stop=(k == KC - 1)), reads=[skey, hk(hb, k)], writes=[pbk])
                    P.act(lambda e, ps=ps, m4=m4: e.copy(out=U[:, m4, 0:512], in_=ps[:]),
                          reads=[pk], writes=[("U", m4)])
            P.dve(lambda e, pb=pb: e.tensor_copy(out=U[:, :, 512:528], in_=pb[:]), reads=[pbk], writes=["Uh"])
            if t == 0:
                P.dve(lambda e: e.tensor_scalar(out=U[:, :, 0:8], in0=U[:, :, 0:8], scalar1=hmask[:, 0:1],
                                                scalar2=None, op0=ALU.mult), reads=["hmask"] + UK, writes=UK)
            if t == NT - 1:
                P.dve(lambda e: e.tensor_scalar(out=U[:, :, 520:528], in0=U[:, :, 520:528], scalar1=hmask[:, 1:2],
                                                scalar2=None, op0=ALU.mult), reads=["hmask", "Uh"], writes=["Uh"])
            for sl in range(2):
                slab, skey = ws.next()
                for ml in range(2):
                    m4 = sl * 2 + ml
                    ps, pk = next_mm()
                    for k in range(KC):
                        P.pe(lambda e, ps=ps, slab=slab, k=k, ml=ml, h=h: e.matmul(
                            ps[:], lhsT=slab[:, k, ml * 128:(ml + 1) * 128], rhs=h[:, k, 8:520],
                            start=(k == 0), stop=(k == KC - 1)), reads=[skey, hk(hb, k)], writes=[pk])
                    P.act(lambda e, ps=ps, m4=m4: e.activation(out=sg[:, m4, :], in_=ps[:], func=AF.Silu),
                          reads=[pk], writes=[("sg", m4)])
            P.dve(lambda e: e.tensor_tensor(out=Ta[:, :, 1:528], in0=U[:, :, 0:527], in1=U[:, :, 1:528], op=ALU.add),
                  reads=UK, writes=["Ta"])
            S, skey_ = Ta, "Ta"
            if gi >= 1:
                P.dve(lambda e: e.tensor_tensor(out=Tb[:, :, 2:527], in0=Ta[:, :, 1:526], in1=Ta[:, :, 3:528],
                                                op=ALU.add), reads=["Ta"], writes=["Tb"])
                S, skey_ = Tb, "Tb"
            if gi >= 2:
                P.dve(lambda e: e.tensor_tensor(out=Ta[:, :, 4:525], in0=Tb[:, :, 2:523], in1=Tb[:, :, 6:527],
                                                op=ALU.add), reads=["Tb"], writes=["Ta"])
                S, skey_ = Ta, "Ta"
            if gi >= 3:
                P.dve(lambda e: e.tensor_tensor(out=Tb[:, :, 8:521], in0=Ta[:, :, 4:517], in1=Ta[:, :, 12:525],
                                                op=ALU.add), reads=["Ta"], writes=["Tb"])
                S, skey_ = Tb, "Tb"
            P.dve(lambda e, S=S, w=w: e.scalar_tensor_tensor(
                out=pooled[:], in0=S[:, :, 8:520], scalar=1.0 / w, in1=U[:, :, 8:520],
                op0=ALU.mult, op1=ALU.subtract), reads=[skey_] + UK, writes=["pooled"])
            if t == 0 or t == NT - 1:
                a, ic = (8, 0) if t == 0 else (512, 8)
                O, okey = (Ta, "Ta") if S is Tb else (Tb, "Tb")
                P.dve(lambda e, S=S, O=O, a=a, ic=ic, gi=gi: e.tensor_tensor(
                    out=O[:, :, 0:8], in0=S[:, :, a:a + 8],
                    in1=icnt[:, gi:gi + 1, ic:ic + 8].to_broadcast([128, 4, 8]), op=ALU.mult),
                    reads=[skey_, "icnt"], writes=[okey])
                P.dve(lambda e, O=O, a=a: e.tensor_tensor(
                    out=pooled[:, :, a - 8:a], in0=O[:, :, 0:8], in1=U[:, :, a:a + 8], op=ALU.subtract),
                    reads=[okey] + UK, writes=["pooled"])
            for m4 in range(4):
                ps, pk = next_mm()
                for k4 in range(4):
                    P.pe(lambda e, ps=ps, gi=gi, k4=k4, m4=m4: e.matmul(
                        ps[:], lhsT=wg[:, gi, k4, m4 * 128:(m4 + 1) * 128], rhs=pooled[:, k4, :],
                        start=(k4 == 0), stop=(k4 == 3)), reads=[("wg", gi), "pooled"], writes=[pk])
                mg = gi * 4 + m4
                P.dve(lambda e, ps=ps, mg=mg, m4=m4: e.scalar_tensor_tensor(
                    out=z[:, mg, :], in0=ps[:], scalar=pscale[:, mg:mg + 1], in1=sg[:, m4, :],
                    op0=ALU.mult, op1=ALU.mult), reads=[pk, "pscale", ("sg", m4)], writes=[("z", mg)])

    def outproj(t):
        c0 = 8 + TT * t

        def load_res(m):
            rb = m % len(xres)
            P.dma("sp", xres[rb][:], xT[m * 128:(m + 1) * 128, c0:c0 + 512], writes=[("xres", rb)])
        for m in range(min(len(xres), KC)):
            load_res(m)
        for sl in range(8):
            slab, skey = ws.next()
            for ml in range(2):
                m = sl * 2 + ml
                ps, pk = next_mm()
                for k in range(KC):
                    P.pe(lambda e, ps=ps, slab=slab, k=k, ml=ml: e.matmul(
                        ps[:], lhsT=slab[:, k, ml * 128:(ml + 1) * 128], rhs=z[:, k, :],
                        start=(k == 0), stop=(k == KC - 1)), reads=[skey, ("z", k)], writes=[pk])
                rb = m % len(xres)
                if not final:
                    ob = m % len(xo_t)
                    P.dve(lambda e, ps=ps, m=m, rb=rb, ob=ob: e.scalar_tensor_tensor(
                        out=xo_t[ob][:], in0=ps[:], scalar=C.gate(m), in1=xres[rb][:],
                        op0=ALU.mult, op1=ALU.add), reads=[pk, "mod", ("xres", rb)], writes=[("xo_t", ob)])
                    P.dma("pool", io["xo"][m * 128:(m + 1) * 128, TT * t:TT * (t + 1)], xo_t[ob][:],
                          reads=[("xo_t", ob)], writes=[("xo", t, m)])
                else:
                    P.dve(lambda e, ps=ps, m=m, rb=rb: e.scalar_tensor_tensor(
                        out=xn[:, m, :], in0=ps[:], scalar=C.gate(m), in1=xres[rb][:],
                        op0=ALU.mult, op1=ALU.add), reads=[pk, "mod", ("xres", rb)], writes=[("xn", m)])
                if m + len(xres) < KC:
                    load_res(m + len(xres))
        if final:
            ss = nb["ss_a"]
            for m in range(KC):
                fb = m % 2
                P.act(lambda e, m=m, fb=fb: e.activation(out=fsq[fb][:], in_=xn[:, m, :], func=AF.Square),
                      reads=[("xn", m)], writes=[("fsq", fb)])
                P.pe(lambda e, m=m, fb=fb: e.matmul(ss[:], lhsT=C.ones[:], rhs=fsq[fb][:],
                                                    start=(m == 0), stop=(m == KC - 1)),
                     reads=["ones", ("fsq", fb)], writes=["ss_a"])
            rstd = nb["rstd"]
            P.dve(lambda e: e.tensor_scalar(out=rstd[:, :512], in0=ss[:], scalar1=1.0 / D, scalar2=EPS,
                                            op0=ALU.mult, op1=ALU.add), reads=["ss_a"], writes=["rstd"])
            P.dve(lambda e: e.reciprocal(out=rstd[:, :512], in_=rstd[:, :512]), reads=["rstd"], writes=["rstd"])
            P.act(lambda e: e.activation(out=rstd[:, :512], in_=rstd[:, :512], func=AF.Sqrt),
                  reads=["rstd"], writes=["rstd"])
            for m in range(KC):
                ob = m % len(xo_t)
                P.dve(lambda e, m=m, ob=ob: e.scalar_tensor_tensor(
                    out=xo_t[ob][:], in0=xn[:, m, :], scalar=fg[:, m:m + 1], in1=rstd[:, :512],
                    op0=ALU.mult, op1=ALU.mult), reads=[("xn", m), "fg", "rstd"], writes=[("xo_t", ob)])
                P.dma("pool", io["xo"][m * 128:(m + 1) * 128, TT * t:TT * (t + 1)], xo_t[ob][:],
                      reads=[("xo_t", ob)], writes=[("xo", t, m)])

    def norm(t):
        emit_norm_mod(P, C, xT, TT * t, 528, hT[t % 2], ("hT", t % 2), nb)

    norm(0)
    for t in range(NT):
        mixer(t)
        if t + 1 < NT and not final:
            norm(t + 1)
        outproj(t)
        if t + 1 < NT and final:
            norm(t + 1)
    P.release(m0)
    return wb


AQ, AKV = 2048, 512
A_SCALE = 128 ** -0.5


def emit_outproj_phase(P, nc, C, ws, xT, xo, Zs, x_off, mm_ring, final_io=None):
    z = [P.sb([128, KC, 512], BF16) for _ in range(2)]
    xres = [P.sb([128, 512], F32) for _ in range(3)]
    xo_t = [P.sb([128, 512], F32) for _ in range(3)]
    mmi = [0]

    def next_mm():
        i = mmi[0] % len(mm_ring)
        mmi[0] += 1
        return mm_ring[i], ("mm", i)

    for t in range(NT):
        zb = t % 2
        for k in range(KC):
            P.dma("sp", z[zb][:, k, :], Zs[k * 128:(k + 1) * 128, t * TT:(t + 1) * TT],
                  reads=[("Zs", k, t)], writes=[("zc", zb, k)])
        c0 = x_off + TT * t

        def load_res(m):
            rb = m % len(xres)
            P.dma("sp", xres[rb][:], xT[m * 128:(m + 1) * 128, c0:c0 + 512], writes=[("xres", rb)])
        for m in range(len(xres)):
            load_res(m)
        for sl in range(8):
            slab, skey = ws.next()
            for ml in range(2):
                m = sl * 2 + ml
                ps, pk = next_mm()
                for k in range(KC):
                    P.pe(lambda e, ps=ps, slab=slab, k=k, ml=ml, zb=zb: e.matmul(
                        ps[:], lhsT=slab[:, k, ml * 128:(ml + 1) * 128], rhs=z[zb][:, k, :],
                        start=(k == 0), stop=(k == KC - 1)), reads=[skey, ("zc", zb, k)], writes=[pk])
                rb = m % len(xres)
                ob = m % len(xo_t)
                P.dve(lambda e, ps=ps, m=m, rb=rb, ob=ob: e.scalar_tensor_tensor(
                    out=xo_t[ob][:], in0=ps[:], scalar=C.gate(m), in1=xres[rb][:],
                    op0=ALU.mult, op1=ALU.add), reads=[pk, "mod", ("xres", rb)], writes=[("xo_t", ob)])
                P.dma("pool", xo[m * 128:(m + 1) * 128, TT * t:TT * (t + 1)], xo_t[ob][:],
                      reads=[("xo_t", ob)], writes=[("xo", t, m)])
                if m + len(xres) < KC:
                    load_res(m + len(xres))


def emit_attn_layer(P, nc, io, variant):
    xT = io["xT"]
    full = variant == "full"
    m0 = P.mark()
    wb_in = nc.dram_tensor("awb_in", [D, 5120], BF16).ap()
    convert_w(P, io["w_in"], wb_in, D, 5120, "acw_in")
    if full:
        wb_out = nc.dram_tensor("awb_out", [D, D], BF16).ap()
        convert_w(P, io["w_out"], wb_out, D, D, "acw_out")
        Qs = nc.dram_tensor("aQs", [16, 128, TC], BF16).ap()
        SG = nc.dram_tensor("aSG", [D, TC], BF16).ap()
        Zs = nc.dram_tensor("aZs", [D, TC], BF16).ap()
    KT_own, V_own = io["KT_own"], io["V_own"]
    P.release(m0)

    C = Common(P, io["mod"])
    gains = P.sb([128, 2], F32, name="gains")
    P.dma("sp", gains[:], io["gains"], writes=["gains"])
    rot32 = P.sb([128, 128], F32, name="rot32")
    P.dma("sp", rot32[:], io["rotT"], writes=["rot32"])
    rotb = P.sb([128, 128], BF16, name="rotb")
    P.dve(lambda e: e.tensor_copy(out=rotb[:], in_=rot32[:]), reads=["rot32"], writes=["rotb"])

    order = []
    if full:
        order += [slab_src(wb_in, 256 * j) for j in range(8)]
    order += [slab_src(wb_in, 2048 + 256 * j) for j in range(4)]
    if full:
        order += [slab_src(wb_in, 3072 + 256 * j) for j in range(8)]
    ws = WStream(P, order * NT, ring=4)
    nb = norm_bufs(P, 512)
    hT = [P.sb([128, KC, 512], BF16, name=f"ahT{i}") for i in range(2)]
    cs = [P.sb([128, 512], F32) for _ in range(2)]
    sn = [P.sb([128, 512], F32) for _ in range(2)]
    qg_t = [P.sb([128, 512], BF16) for _ in range(2)]
    sq_t = [P.sb([128, 512], BF16) for _ in range(2)]
    rs_t = [P.sb([128, 512], F32) for _ in range(2)]
    t1 = [P.sb([128, 512], F32) for _ in range(2)]
    t2 = [P.sb([128, 512], F32) for _ in range(2)]
    qo = [P.sb([128, 512], BF16) for _ in range(3)]
    vbuf = [P.sb([128, 4, 512], BF16) for _ in range(2)]
    sgt = [P.sb([128, 512], BF16) for _ in range(3)]
    mm = [P.ps([128, 512], F32) for _ in range(3)]
    ssq = [P.ps([128, 512], F32) for _ in range(2)]
    rotp = [P.ps([128, 512], F32) for _ in range(2)]
    cnt = {"mm": 0, "hd": 0, "qo": 0, "sg": 0}

    def next_mm():
        i = cnt["mm"] % len(mm)
        cnt["mm"] += 1
        return mm[i], ("mm", i)

    def qk_head(t, hb, slab, skey, ml, gcol, dst, dkey):
        h = hT[hb]
        ps, pk = next_mm()
        for k in range(KC):
            P.pe(lambda e, ps=ps, slab=slab, k=k, ml=ml, h=h: e.matmul(
                ps[:], lhsT=slab[:, k, ml * 128:(ml + 1) * 128], rhs=h[:, k, :],
                start=(k == 0), stop=(k == KC - 1)), reads=[skey, (("ahT", hb), k)], writes=[pk])
        i2 = cnt["hd"] % 2
        cnt["hd"] += 1
        P.act(lambda e, ps=ps, i2=i2: e.activation(out=qg_t[i2][:], in_=ps[:], func=AF.Copy,
                                                   scale=gains[:, gcol:gcol + 1]),
              reads=[pk, "gains"], writes=[("qg", i2)])
        P.act(lambda e, ps=ps, i2=i2: e.activation(out=sq_t[i2][:], in_=ps[:], func=AF.Square),
              reads=[pk], writes=[("sq_t", i2)])
        P.pe(lambda e, i2=i2: e.matmul(ssq[i2][:], lhsT=C.ones[:], rhs=sq_t[i2][:], start=True, stop=True),
             reads=["ones", ("sq_t", i2)], writes=[("ssq", i2)])
        P.pe(lambda e, i2=i2: e.matmul(rotp[i2][:], lhsT=rotb[:], rhs=qg_t[i2][:], start=True, stop=True),
             reads=["rotb", ("qg", i2)], writes=[("rotp", i2)])
        P.act(lambda e, i2=i2: e.activation(out=rs_t[i2][:], in_=ssq[i2][:], func=AF.Ln, scale=1.0 / 128, bias=EPS),
              reads=[("ssq", i2)], writes=[("rs_t", i2)])
        P.act(lambda e, i2=i2: e.activation(out=rs_t[i2][:], in_=rs_t[i2][:], func=AF.Exp, scale=-0.5),
              reads=[("rs_t", i2)], writes=[("rs_t", i2)])
        cb = t % 2
        P.pool(lambda e, i2=i2, cb=cb: e.tensor_tensor(out=t1[i2][:], in0=qg_t[i2][:], in1=cs[cb][:], op=ALU.mult),
               reads=[("qg", i2), ("cs", cb)], writes=[("t1", i2)])
        P.dve(lambda e, i2=i2, cb=cb: e.tensor_tensor(out=t2[i2][:], in0=rotp[i2][:], in1=sn[cb][:], op=ALU.mult),
              reads=[("rotp", i2), ("sn", cb)], writes=[("t2", i2)])
        P.pool(lambda e, i2=i2: e.tensor_tensor(out=t1[i2][:], in0=t1[i2][:], in1=t2[i2][:], op=ALU.add),
               reads=[("t1", i2), ("t2", i2)], writes=[("t1", i2)])
        i3 = cnt["qo"] % 3
        cnt["qo"] += 1
        P.dve(lambda e, i2=i2, i3=i3: e.tensor_tensor(out=qo[i3][:], in0=t1[i2][:], in1=rs_t[i2][:], op=ALU.mult),
              reads=[("t1", i2), ("rs_t", i2)], writes=[("qo", i3)])
        P.dma("pool", dst, qo[i3][:], reads=[("qo", i3)], writes=[dkey])

    def projections(t):
        hb = t % 2
        h = hT[hb]
        cb = t % 2
        P.dma("sp", cs[cb][:], io["cos"][:, t * TT:(t + 1) * TT], writes=[("cs", cb)])
        P.dma("sp", sn[cb][:], io["sin"][:, t * TT:(t + 1) * TT], writes=[("sn", cb)])
        if full:
            for sl in range(8):
                slab, skey = ws.next()
                for ml in range(2):
                    hd = sl * 2 + ml
                    qk_head(t, hb, slab, skey, ml, 0, Qs[hd][:, t * TT:(t + 1) * TT], ("Qs", hd, t))
        for sl in range(2):
            slab, skey = ws.next()
            for ml in range(2):
                kh = sl * 2 + ml
                qk_head(t, hb, slab, skey, ml, 1, KT_own[kh][:, t * TT:(t + 1) * TT], ("KT", kh, t))
        vb = t % 2
        for sl in range(2):
            slab, skey = ws.next()
            for j in range(4):
                ps, pk = next_mm()
                for k in range(KC):
                    P.pe(lambda e, ps=ps, slab=slab, k=k, j=j, h=h: e.matmul(
                        ps[:, 0:256], lhsT=h[:, k, j * 128:(j + 1) * 128], rhs=slab[:, k, :],
                        start=(k == 0), stop=(k == KC - 1)), reads=[skey, (("ahT", hb), k)], writes=[pk])
                P.act(lambda e, ps=ps, j=j, sl=sl, vb=vb: e.copy(out=vbuf[vb][:, j, sl * 256:(sl + 1) * 256],
                                                                 in_=ps[:, 0:256]),
                      reads=[pk], writes=[("vbuf", vb, j, sl)])
        P.dma("pool", V_own[t * TT:(t + 1) * TT, :].rearrange("(j p) n -> p j n", p=128), vbuf[vb][:],
              reads=[("vbuf", vb, j, sl) for j in range(4) for sl in range(2)], writes=[("V", t)])
        if full:
            for sl in range(8):
                slab, skey = ws.next()
                for ml in range(2):
                    m = sl * 2 + ml
                    ps, pk = next_mm()
                    for k in range(KC):
                        P.pe(lambda e, ps=ps, slab=slab, k=k, ml=ml, h=h: e.matmul(
                            ps[:], lhsT=slab[:, k, ml * 128:(ml + 1) * 128], rhs=h[:, k, :],
                            start=(k == 0), stop=(k == KC - 1)), reads=[skey, (("ahT", hb), k)], writes=[pk])
                    i3 = cnt["sg"] % 3
                    cnt["sg"] += 1
                    P.act(lambda e, ps=ps, i3=i3: e.activation(out=sgt[i3][:], in_=ps[:], func=AF.Silu),
                          reads=[pk], writes=[("sgt", i3)])
                    P.dma("pool", SG[m * 128:(m + 1) * 128, t * TT:(t + 1) * TT], sgt[i3][:],
                          reads=[("sgt", i3)], writes=[("SG", m, t)])

    def norm(t):
        emit_norm_mod(P, C, xT, TT * t, 512, hT[t % 2], ("ahT", t % 2), nb)

    norm(0)
    for t in range(NT):
        if t + 1 < NT:
            norm(t + 1)
        projections(t)
    if not full:
        return
    P.release(m0)

    C = Common(P, io["mod"])
    ones32 = P.sb([128, 128], F32, name="ones32")
    P.dve(lambda e: e.memset(ones32[:], 1.0), writes=["ones32"])
    kt = [P.sb([128, 2 * TC], BF16, name=f"kt{i}") for i in range(2)]
    vt = [P.sb([128, 64, 128], BF16, name=f"vt{i}") for i in range(2)]
    qt = [P.sb([128, 512], BF16) for _ in range(3)]
    sgq = [P.sb([128, 512], BF16) for _ in range(3)]
    pt = [P.sb([128, 512], BF16) for _ in range(4)]
    accD = [P.sb([128, 512], F32) for _ in range(2)]
    accP = [P.sb([128, 512], F32) for _ in range(2)]
    rinv = [P.sb([128, 512], F32) for _ in range(2)]
    zt = [P.sb([128, 512], BF16) for _ in range(2)]
    st = [P.ps([128, 512], F32) for _ in range(3)]
    ot = [P.ps([128, 512], F32) for _ in range(2)]
    rsp = [P.ps([128, 512], F32) for _ in range(2)]
    KT_all, V_all = io["KT_all"], io["V_all"]
    it = 0
    sti = 0
    pti = 0
    for kvh in range(4):
        kb = kvh % 2
        for r in range(2):
            P.dma("sp", kt[kb][:, r * TC:(r + 1) * TC], KT_all[r, kvh], writes=[("kt", kb, r)])
            P.dma("sp", vt[kb][:, r * 32:(r + 1) * 32, :],
                  V_all[r][:, kvh * 128:(kvh + 1) * 128].rearrange("(c p) d -> p c d", p=128),
                  writes=[("vt", kb, r)])
        ktk = [("kt", kb, 0), ("kt", kb, 1)]
        vtk = [("vt", kb, 0), ("vt", kb, 1)]
        for gq_ in range(4):
            hd = kvh * 4 + gq_
            for t in range(NT):
                ab = it % 2
                q3 = it % 3
                it += 1
                P.dma("sp", qt[q3][:], Qs[hd][:, t * TT:(t + 1) * TT], reads=[("Qs", hd, t)], writes=[("qt", q3)])
                P.dma("sp", sgq[q3][:], SG[hd * 128:(hd + 1) * 128, t * TT:(t + 1) * TT],
                      reads=[("SG", hd, t)], writes=[("sgq", q3)])
                for sc in range(64):
                    s3 = sti % 3
                    sti += 1
                    p4 = pti % 4
                    pti += 1
                    P.pe(lambda e, s3=s3, kb=kb, sc=sc, q3=q3: e.matmul(
                        st[s3][:], lhsT=kt[kb][:, sc * 128:(sc + 1) * 128], rhs=qt[q3][:], start=True, stop=True),
                        reads=[ktk[sc // 32], ("qt", q3)], writes=[("st", s3)])
                    P.act(lambda e, s3=s3, p4=p4: e.activation(out=pt[p4][:], in_=st[s3][:], func=AF.Exp, scale=A_SCALE),
                          reads=[("st", s3)], writes=[("pt", p4)])
                    P.pe(lambda e, ab=ab, kb=kb, sc=sc, p4=p4: e.matmul(
                        ot[ab][:], lhsT=vt[kb][:, sc, :], rhs=pt[p4][:], start=(sc == 0), stop=(sc == 63)),
                        reads=[vtk[sc // 32], ("pt", p4)], writes=[("ot", ab)])
                    eng, acc, akey = ("dve", accD[ab], ("accD", ab)) if sc % 2 == 0 else ("pool", accP[ab], ("accP", ab))
                    if sc < 2:
                        P.add(eng, lambda e, acc=acc, p4=p4: e.tensor_copy(out=acc[:], in_=pt[p4][:]),
                              reads=[("pt", p4)], writes=[akey])
                    else:
                        P.add(eng, lambda e, acc=acc, p4=p4: e.tensor_tensor(out=acc[:], in0=acc[:], in1=pt[p4][:], op=ALU.add),
                              reads=[("pt", p4), akey], writes=[akey])
                P.pe(lambda e, ab=ab: e.matmul(rsp[ab][:], lhsT=ones32[:], rhs=accD[ab][:], start=True, stop=False),
                     reads=["ones32", ("accD", ab)], writes=[("rsp", ab)])
                P.pe(lambda e, ab=ab: e.matmul(rsp[ab][:], lhsT=ones32[:], rhs=accP[ab][:], start=False, stop=True),
                     reads=["ones32", ("accP", ab)], writes=[("rsp", ab)])
                P.dve(lambda e, ab=ab: e.reciprocal(out=rinv[ab][:], in_=rsp[ab][:]), reads=[("rsp", ab)], writes=[("rinv", ab)])
                P.pool(lambda e, ab=ab, q3=q3: e.tensor_tensor(out=rinv[ab][:], in0=rinv[ab][:], in1=sgq[q3][:], op=ALU.mult),
                       reads=[("rinv", ab), ("sgq", q3)], writes=[("rinv", ab)])
                P.dve(lambda e, ab=ab: e.tensor_tensor(out=zt[ab][:], in0=ot[ab][:], in1=rinv[ab][:], op=ALU.mult),
                      reads=[("ot", ab), ("rinv", ab)], writes=[("zt", ab)])
                P.dma("pool", Zs[hd * 128:(hd + 1) * 128, t * TT:(t + 1) * TT], zt[ab][:],
                      reads=[("zt", ab)], writes=[("Zs", hd, t)])
    P.release(m0)

    C = Common(P, io["mod"])
    ws2 = WStream(P, [slab_src(wb_out, 256 * j) for j in range(8)] * NT, ring=4, name="wsC")
    mmC = [P.ps([128, 512], F32) for _ in range(4)]
    emit_outproj_phase(P, nc, C, ws2, xT, io["xo"], Zs, 0, mmC)


GK = 1024
GV = 2048
NCH = TC // 128
LN_QS = -2.772588722239781


def emit_gla_layer(P, nc, io, variant, pfx="", wb=None, scr=None):
    xT = io["xT"]
    full = variant in ("full", "sweep3")
    mtop = P.mark()
    if variant == "sweep3":
        m0 = mtop
        wb_in, wb_out = wb
        QC, OT, SG = scr["QC"], scr["OT"], scr["SG"]
    else:
        el = {d: P.sb([128, 8, NCH], F32, name=f"el_{d}") for d in "fb"}
        m0 = P.mark()
        if wb is None:
            wb_in = nc.dram_tensor(pfx + "gwb_in", [D, 6144], BF16).ap()
            convert_w(P, io["w_in"], wb_in, D, 6144, "gcw_in")
            wb_out = nc.dram_tensor(pfx + "gwb_out", [D, D], BF16).ap()
            convert_w(P, io["w_out"], wb_out, D, D, "gcw_out")
            wb = (wb_in, wb_out)
        wb_in, wb_out = wb
        QT = {d: nc.dram_tensor(pfx + f"gQ{d}", [GK, TC], BF16).ap() for d in "fb"}
        KT = {d: nc.dram_tensor(pfx + f"gK{d}", [GK, TC], BF16).ap() for d in "fb"}
        KH = {d: nc.dram_tensor(pfx + f"gKH{d}", [TC, GK], BF16).ap() for d in "fb"}
        QC = {d: nc.dram_tensor(pfx + f"gQC{d}", [GK, TC], BF16).ap() for d in "fb"}
        OT = {d: nc.dram_tensor(pfx + f"gO{d}", [GV, TC], F32).ap() for d in "fb"}
        Vs = nc.dram_tensor(pfx + "gVs", [TC, GV], BF16).ap()
        SG = nc.dram_tensor(pfx + "gSG", [D, TC], BF16).ap()
        P.release(m0)
    if variant != "sweep3":

        C = Common(P, io["mod"])
        tri = {}
        for d in "fb":
            tri[d] = P.sb([128, 128], F32, name=f"tri{d}")
            P.dma("sp", tri[d][:], io["tri" + d], writes=[("tri", d)])
        id32 = P.sb([128, 128], F32, name="id32")
        P.dma("sp", id32[:], io["ident"], writes=["id32"])
        ident = P.sb([128, 128], BF16, name="ident")
        P.dve(lambda e: e.tensor_copy(out=ident[:], in_=id32[:]), reads=["id32"], writes=["ident"])
        w1s = P.sb([128, KC, 64], F32, name="w1s")
        P.dma("sp", w1s[:], io["w1cat"].rearrange("(k p) n -> p k n", p=128), writes=["w1s"])
        w1b = P.sb([128, KC, 64], BF16, name="w1b")
        P.dve(lambda e: e.tensor_copy(out=w1b[:], in_=w1s[:]), reads=["w1s"], writes=["w1b"])
        w2s = P.sb([64, GK], F32, name="w2s")
        P.dma("sp", w2s[:], io["w2aug"], writes=["w2s"])
        w2b = P.sb([64, GK], BF16, name="w2b")
        P.dve(lambda e: e.tensor_copy(out=w2b[:], in_=w2s[:]), reads=["w2s"], writes=["w2b"])

        order = [slab_src(wb_in, 256 * j) for j in range(24)]
        ws = WStream(P, order * NT, ring=3)
        nb = norm_bufs(P, 512)
        hT = [P.sb([128, KC, 512], BF16, name=f"ghT{i}") for i in range(2)]
        rT = [P.sb([64, 512], BF16, name=f"rT{i}") for i in range(2)]
        for i in range(2):
            P.dve(lambda e, i=i: e.memset(rT[i][:], 1.0), writes=[("rT", i)])
        qT_sb = P.sb([128, 8, 512], F32, name="qT_sb")
        kT_sb = P.sb([128, 8, 512], F32, name="kT_sb")
        e1 = P.sb([128, GK], F32, name="e1")
        la = {d: P.sb([128, GK], F32, name=f"la{d}") for d in "fb"}
        Eq = P.sb([128, 8, 128], F32, name="Eq")
        Einv = P.sb([128, 8, 128], F32, name="Einv")
        ktmp = P.sb([128, 8, 128], F32, name="ktmp")
        qtl = [P.sb([128, 8, 128], BF16) for _ in range(2)]
        ktl = [P.sb([128, 8, 128], BF16) for _ in range(2)]
        khT = [P.sb([128, 8, 128], BF16) for _ in range(2)]
        khat = [P.sb([128, GK], BF16) for _ in range(2)]
        vbuf = P.sb([128, 4, GV], BF16, name="gvbuf")
        sgt = [P.sb([128, 512], BF16) for _ in range(3)]
        mm = [P.ps([128, 512], F32) for _ in range(2)]
        rps = P.ps([64, 512], F32, name="rps")
        zps = P.ps([128, 512], F32, name="zps")
        bps = [P.ps([128, 4, 128], F32) for _ in range(2)]
        trp = P.ps([128, GK], BF16, name="trp")
        cnt = {"mm": 0, "sg": 0, "cd": 0, "bp": 0}

        def next_mm():
            i = cnt["mm"] % len(mm)
            cnt["mm"] += 1
            return mm[i], ("mm", i)

        def proj_fm(t, hb, dst_sb, dkey):
            h = hT[hb]
            for sl in range(4):
                slab, skey = ws.next()
                for ml in range(2):
                    m = sl * 2 + ml
                    ps, pk = next_mm()
                    for k in range(KC):
                        P.pe(lambda e, ps=ps, slab=slab, k=k, ml=ml, h=h: e.matmul(
                            ps[:], lhsT=slab[:, k, ml * 128:(ml + 1) * 128], rhs=h[:, k, :],
                            start=(k == 0), stop=(k == KC - 1)), reads=[skey, (("ghT", hb), k)], writes=[pk])
                    P.act(lambda e, ps=ps, m=m: e.copy(out=dst_sb[:, m, :], in_=ps[:]), reads=[pk], writes=[(dkey, m)])

        def chunk_dir(t, cc, d, rb):
            c = 4 * t + cc
            base = 0 if d == "f" else 32
            last = 127 if d == "f" else 0
            tok = slice(cc * 128, (cc + 1) * 128)
            for hh in range(2):
                P.pe(lambda e, hh=hh: e.matmul(zps[:], lhsT=rT[rb][base:base + 32, tok],
                                               rhs=w2b[base:base + 32, hh * 512:(hh + 1) * 512], start=True, stop=True),
                     reads=[("rT", rb), "w2b"], writes=["zps"])
                P.act(lambda e, hh=hh: e.activation(out=e1[:, hh * 512:(hh + 1) * 512], in_=zps[:], func=AF.Exp, scale=-1.0),
                      reads=["zps"], writes=[("e1", hh)])
                P.act(lambda e, hh=hh: e.activation(out=la[d][:, hh * 512:(hh + 1) * 512], in_=e1[:, hh * 512:(hh + 1) * 512],
                                                    func=AF.Ln, bias=1.0, scale=1.0),
                      reads=[("e1", hh)], writes=[("la", d, hh)])
            for mh in range(2):
                bi = cnt["bp"] % 2
                cnt["bp"] += 1
                bp = bps[bi]
                for m4 in range(4):
                    m = mh * 4 + m4
                    P.pe(lambda e, bp=bp, m4=m4, m=m: e.matmul(bp[:, m4, :], lhsT=la[d][:, m * 128:(m + 1) * 128],
                                                               rhs=tri[d][:], start=True, stop=True),
                         reads=[("la", d, m // 4), ("tri", d)], writes=[("bps", bi)])
                ms = slice(mh * 4, mh * 4 + 4)
                P.act(lambda e, bp=bp, ms=ms: e.activation(out=Eq[:, ms, :], in_=bp[:], func=AF.Exp, bias=LN_QS, scale=1.0),
                      reads=[("bps", bi)], writes=[("Eq", mh)])
                P.act(lambda e, bp=bp, ms=ms: e.activation(out=Einv[:, ms, :], in_=bp[:], func=AF.Exp, scale=-1.0),
                      reads=[("bps", bi)], writes=[("Einv", mh)])
                P.act(lambda e, bp=bp, ms=ms: e.activation(out=el[d][:, ms, c:c + 1], in_=bp[:, :, last:last + 1], func=AF.Exp),
                      reads=[("bps", bi)], writes=[("el", d, c, mh)])
            i2 = cnt["cd"] % 2
            cnt["cd"] += 1
            P.dve(lambda e, i2=i2: e.tensor_tensor(out=qtl[i2][:], in0=qT_sb[:, :, tok], in1=Eq[:], op=ALU.mult),
                  reads=[("qT", m) for m in range(8)] + [("Eq", 0), ("Eq", 1)], writes=[("qtl", i2)])
            P.dve(lambda e: e.tensor_tensor(out=ktmp[:], in0=kT_sb[:, :, tok], in1=Einv[:], op=ALU.mult),
                  reads=[("kT", m) for m in range(8)] + [("Einv", 0), ("Einv", 1)], writes=["ktmp"])
            P.pool(lambda e, i2=i2: e.tensor_copy(out=ktl[i2][:], in_=ktmp[:]), reads=["ktmp"], writes=[("ktl", i2)])
            P.pool(lambda e, i2=i2: e.tensor_tensor(out=khT[i2][:], in0=ktmp[:],
                                                    in1=el[d][:, :, c:c + 1].to_broadcast([128, 8, 128]), op=ALU.mult),
                   reads=["ktmp", ("el", d, c, 0), ("el", d, c, 1)], writes=[("khT", i2)])
            for m in range(8):
                P.pe(lambda e, m=m, i2=i2: e.transpose(trp[:, m * 128:(m + 1) * 128], khT[i2][:, m, :], ident[:]),
                     reads=[("khT", i2), "ident"], writes=["trp"])
            P.act(lambda e, i2=i2: e.copy(out=khat[i2][:], in_=trp[:]), reads=["trp"], writes=[("khat", i2)])
            csl = slice(c * 128, (c + 1) * 128)
            P.dma("pool", QT[d][:, csl].rearrange("(m p) t -> p m t", p=128), qtl[i2][:],
                  reads=[("qtl", i2)], writes=[("QT", d, c)])
            P.dma("pool", KT[d][:, csl].rearrange("(m p) t -> p m t", p=128), ktl[i2][:],
                  reads=[("ktl", i2)], writes=[("KT", d, c)])
            P.dma("pool", KH[d][csl, :], khat[i2][:], reads=[("khat", i2)], writes=[("KH", d, c)])

        def sweep1_tile(t):
            hb = t % 2
            h = hT[hb]
            rb = t % 2
            for k in range(KC):
                P.pe(lambda e, k=k, h=h: e.matmul(rps[:], lhsT=w1b[:, k, :], rhs=h[:, k, :], start=(k == 0), stop=(k == KC - 1)),
                     reads=["w1b", (("ghT", hb), k)], writes=["rps"])
            P.act(lambda e: e.copy(out=rT[rb][0:16, :], in_=rps[0:16, :]), reads=["rps"], writes=[("rT", rb)])
            P.act(lambda e: e.copy(out=rT[rb][32:48, :], in_=rps[32:48, :]), reads=["rps"], writes=[("rT", rb)])
            proj_fm(t, hb, qT_sb, "qT")
            proj_fm(t, hb, kT_sb, "kT")

            def v_group(sl):
                slab, skey = ws.next()
                for j in range(4):
                    ps, pk = next_mm()
                    for k in range(KC):
                        P.pe(lambda e, ps=ps, slab=slab, k=k, j=j, h=h: e.matmul(
                            ps[:, 0:256], lhsT=h[:, k, j * 128:(j + 1) * 128], rhs=slab[:, k, :],
                            start=(k == 0), stop=(k == KC - 1)), reads=[skey, (("ghT", hb), k)], writes=[pk])
                    eng = "act" if (sl + j) % 2 == 0 else "dve"
                    if eng == "act":
                        P.act(lambda e, ps=ps, j=j, sl=sl: e.copy(out=vbuf[:, j, sl * 256:(sl + 1) * 256], in_=ps[:, 0:256]),
                              reads=[pk], writes=[("gvbuf", j, sl)])
                    else:
                        P.dve(lambda e, ps=ps, j=j, sl=sl: e.tensor_copy(out=vbuf[:, j, sl * 256:(sl + 1) * 256], in_=ps[:, 0:256]),
                              reads=[pk], writes=[("gvbuf", j, sl)])
                if sl == 7:
                    P.dma("pool", Vs[t * TT:(t + 1) * TT, :].rearrange("(j p) n -> p j n", p=128), vbuf[:],
                          reads=[("gvbuf", j, s_) for j in range(4) for s_ in range(8)], writes=[("Vs", t)])

            def g_group(sl):
                slab, skey = ws.next()
                for ml in range(2):
                    m = sl * 2 + ml
                    ps, pk = next_mm()
                    for k in range(KC):
                        P.pe(lambda e, ps=ps, slab=slab, k=k, ml=ml, h=h: e.matmul(
                            ps[:], lhsT=slab[:, k, ml * 128:(ml + 1) * 128], rhs=h[:, k, :],
                            start=(k == 0), stop=(k == KC - 1)), reads=[skey, (("ghT", hb), k)], writes=[pk])
                    i3 = cnt["sg"] % 3
                    cnt["sg"] += 1
                    P.act(lambda e, ps=ps, i3=i3: e.activation(out=sgt[i3][:], in_=ps[:], func=AF.Silu),
                          reads=[pk], writes=[("sgt", i3)])
                    P.dma("pool", SG[m * 128:(m + 1) * 128, t * TT:(t + 1) * TT], sgt[i3][:],
                          reads=[("sgt", i3)], writes=[("SG", m, t)])

            cds = [(cc, d) for cc in range(4) for d in "fb"]
            for gi in range(16):
                if gi % 2 == 0:
                    chunk_dir(t, cds[gi // 2][0], cds[gi // 2][1], rb)
                if gi < 8:
                    v_group(gi)
                else:
                    g_group(gi - 8)

        def norm(t):
            emit_norm_mod(P, C, xT, TT * t, 512, hT[t % 2], ("ghT", t % 2), nb)

        norm(0)
        for t in range(NT):
            if t + 1 < NT:
                norm(t + 1)
            sweep1_tile(t)
        P.release(m0)

        msk = {}
        for d in "fb":
            msk[d] = P.sb([128, 128], F32, name=f"msk{d}")
            P.dma("sp", msk[d][:], io["mask" + d], writes=[("msk", d)])
        S32 = {d: P.sb([128, 8, 512], F32, name=f"S32{d}") for d in "fb"}
        Sb = {d: P.sb([128, 8, 512], BF16, name=f"Sb{d}") for d in "fb"}
        ecum = {d: P.sb([128, 8], F32, name=f"ecum{d}") for d in "fb"}
        for d in "fb":
            P.dve(lambda e, d=d: e.memset(S32[d][:], 0.0), writes=[("S32", d, i) for i in range(8)])
            P.pool(lambda e, d=d: e.memset(Sb[d][:], 0.0), writes=[("Sb", d, i) for i in range(8)])
            P.dve(lambda e, d=d: e.memset(ecum[d][:], 1.0), writes=[("ecum", d)])
        qt = {d: [P.sb([128, 8, 128], BF16) for _ in range(2)] for d in "fb"}
        kt = {d: [P.sb([128, 8, 128], BF16) for _ in range(2)] for d in "fb"}
        kh = {d: [P.sb([128, GK], BF16) for _ in range(2)] for d in "fb"}
        vv = {d: [P.sb([128, GV], BF16) for _ in range(2)] for d in "fb"}
        Am = {d: P.sb([128, 4, 128], BF16, name=f"Am{d}") for d in "fb"}
        qc = {d: P.sb([128, 8, 128], BF16, name=f"qc{d}") for d in "fb"}
        osb = {d: P.sb([128, 16, 128], F32, name=f"osb{d}") for d in "fb"}
        aps = P.ps([128, 4, 128], F32, name="aps")
        ops_ = P.ps([128, 16, 128], F32, name="ops")
        sps = [P.ps([128, 512], F32) for _ in range(3)]
        spi = 0
        for step in range(NCH):
            for d in "fb":
                c = step if d == "f" else NCH - 1 - step
                b2 = step % 2
                csl = slice(c * 128, (c + 1) * 128)
                P.dma("sp", qt[d][b2][:], QT[d][:, csl].rearrange("(m p) t -> p m t", p=128),
                      reads=[("QT", d, c)], writes=[("qt", d, b2)])
                P.dma("sp", kt[d][b2][:], KT[d][:, csl].rearrange("(m p) t -> p m t", p=128),
                      reads=[("KT", d, c)], writes=[("kt", d, b2)])
                P.dma("sp", kh[d][b2][:], KH[d][csl, :], reads=[("KH", d, c)], writes=[("kh", d, b2)])
                P.dma("sp", vv[d][b2][:], Vs[csl, :], reads=[("Vs", c // 4)], writes=[("vv", d, b2)])
                for hd in range(4):
                    for dc in range(2):
                        i8 = hd * 2 + dc
                        P.pe(lambda e, hd=hd, dc=dc, i8=i8, d=d, b2=b2: e.matmul(
                            aps[:, hd, :], lhsT=kt[d][b2][:, i8, :], rhs=qt[d][b2][:, i8, :], start=(dc == 0), stop=(dc == 1)),
                            reads=[("kt", d, b2), ("qt", d, b2)], writes=["aps"])
                P.dve(lambda e, d=d: e.tensor_tensor(out=Am[d][:], in0=aps[:],
                                                     in1=msk[d][:, :].unsqueeze(1).to_broadcast([128, 4, 128]), op=ALU.mult),
                      reads=["aps", ("msk", d)], writes=[("Am", d)])
                for hd in range(4):
                    for ec in range(4):
                        o16 = hd * 4 + ec
                        for dc in range(2):
                            i8 = hd * 2 + dc
                            P.pe(lambda e, o16=o16, i8=i8, ec=ec, dc=dc, d=d, b2=b2: e.matmul(
                                ops_[:, o16, :], lhsT=Sb[d][:, i8, ec * 128:(ec + 1) * 128], rhs=qt[d][b2][:, i8, :],
                                start=(dc == 0), stop=False), reads=[("Sb", d, i8), ("qt", d, b2)], writes=["ops"])
                        P.pe(lambda e, o16=o16, hd=hd, ec=ec, d=d, b2=b2: e.matmul(
                            ops_[:, o16, :], lhsT=vv[d][b2][:, hd * 512 + ec * 128:hd * 512 + (ec + 1) * 128], rhs=Am[d][:, hd, :],
                            start=False, stop=True), reads=[("vv", d, b2), ("Am", d)], writes=["ops"])
                for q4 in range(4):
                    if q4 % 2 == 0:
                        P.act(lambda e, d=d, q4=q4: e.copy(out=osb[d][:, q4 * 4:(q4 + 1) * 4, :], in_=ops_[:, q4 * 4:(q4 + 1) * 4, :]),
                              reads=["ops"], writes=[("osb", d)])
                    else:
                        P.dve(lambda e, d=d, q4=q4: e.tensor_copy(out=osb[d][:, q4 * 4:(q4 + 1) * 4, :], in_=ops_[:, q4 * 4:(q4 + 1) * 4, :]),
                              reads=["ops"], writes=[("osb", d)])
                P.dma("pool", OT[d][:, csl].rearrange("(m p) t -> p m t", p=128), osb[d][:],
                      reads=[("osb", d)], writes=[("OT", d, c)])
                P.pool(lambda e, d=d, b2=b2: e.tensor_tensor(out=qc[d][:], in0=qt[d][b2][:],
                                                             in1=ecum[d][:, :].unsqueeze(2).to_broadcast([128, 8, 128]), op=ALU.mult),
                       reads=[("qt", d, b2), ("ecum", d)], writes=[("qc", d)])
                P.dma("pool", QC[d][:, csl].rearrange("(m p) t -> p m t", p=128), qc[d][:],
                      reads=[("qc", d)], writes=[("QC", d, c)])
                P.pool(lambda e, d=d, c=c: e.tensor_tensor(out=ecum[d][:], in0=ecum[d][:], in1=el[d][:, :, c], op=ALU.mult),
                       reads=[("ecum", d), ("el", d, c, 0), ("el", d, c, 1)], writes=[("ecum", d)])
                for i8 in range(8):
                    hd = i8 // 2
                    s3 = spi % 3
                    spi += 1
                    P.pe(lambda e, s3=s3, i8=i8, hd=hd, d=d, b2=b2: e.matmul(
                        sps[s3][:], lhsT=kh[d][b2][:, i8 * 128:(i8 + 1) * 128], rhs=vv[d][b2][:, hd * 512:(hd + 1) * 512],
                        start=True, stop=True), reads=[("kh", d, b2), ("vv", d, b2)], writes=[("sps", s3)])
                    P.dve(lambda e, s3=s3, i8=i8, d=d, c=c: e.scalar_tensor_tensor(
                        out=S32[d][:, i8, :], in0=S32[d][:, i8, :], scalar=el[d][:, i8, c:c + 1], in1=sps[s3][:],
                        op0=ALU.mult, op1=ALU.add), reads=[("sps", s3), ("S32", d, i8), ("el", d, c, i8 // 4)],
                        writes=[("S32", d, i8)])
                    P.act(lambda e, i8=i8, d=d: e.copy(out=Sb[d][:, i8, :], in_=S32[d][:, i8, :]),
                          reads=[("S32", d, i8)], writes=[("Sb", d, i8)])
        if "S_end_f" in io:
            for d in "fb":
                P.dma("pool", io["S_end_" + d], S32[d][:], reads=[("S32", d, i) for i in range(8)], writes=[("S_end", d)])
    if not full:
        P.release(mtop)
        return {"QC": QC, "OT": OT, "SG": SG, "wb": wb}
    P.release(m0)

    C = Common(P, io["mod"])
    ng = P.sb([128, 4], F32, name="ng")
    P.dma("sp", ng[:], io["ng"], writes=["ng"])
    Sin = {}
    s32 = P.sb([128, 8, 512], F32, name="Sin32")
    hm = P.sb([128, 2], F32, name="ghm")
    if "hmask" in io:
        P.dma("sp", hm[:], io["hmask"], writes=["ghm"])
    else:
        P.dve(lambda e: e.memset(hm[:], 1.0), writes=["ghm"])
    for di, d in enumerate("fb"):
        P.dma("sp", s32[:], io["S_in_" + d], writes=["Sin32"])
        Sin[d] = P.sb([128, 8, 512], BF16, name=f"Sin{d}")
        P.dve(lambda e, d=d, di=di: e.tensor_scalar(out=Sin[d][:], in0=s32[:], scalar1=hm[:, di:di + 1], scalar2=None,
                                                    op0=ALU.mult), reads=["Sin32", "ghm"], writes=[("Sin", d)])
    ws3 = WStream(P, [slab_src(wb_out, 256 * j) for j in range(8)] * NT, ring=3, name="ws3")
    qcl = {d: [P.sb([128, 8, 512], BF16) for _ in range(1)] for d in "fb"}
    of_ = [P.sb([128, 4, 512], F32) for _ in range(2)]
    ob_ = [P.sb([128, 4, 512], F32) for _ in range(2)]
    osum = [P.sb([128, 4, 512], F32) for _ in range(2)]
    osq = [P.sb([128, 512], BF16) for _ in range(2)]
    sgl = [P.sb([128, 4, 512], BF16) for _ in range(2)]
    rsd = [P.sb([128, 512], F32) for _ in range(2)]
    z = [P.sb([128, KC, 512], BF16) for _ in range(2)]
    xres = [P.sb([128, 512], F32) for _ in range(3)]
    xo_t = [P.sb([128, 512], F32) for _ in range(3)]
    mm3 = [P.ps([128, 512], F32) for _ in range(4)]
    ssp = [P.ps([128, 512], F32) for _ in range(2)]
    c3 = {"mm": 0}

    def next_mm3():
        i = c3["mm"] % len(mm3)
        c3["mm"] += 1
        return mm3[i], ("mm3", i)

    def heads_gen(t):
        tsl = slice(t * TT, (t + 1) * TT)
        tb = t % 2
        qb = 0
        for d in "fb":
            P.dma("sp", qcl[d][qb][:], QC[d][:, tsl].rearrange("(m p) t -> p m t", p=128),
                  reads=[("QC", d, c) for c in range(4 * t, 4 * t + 4)], writes=[("qcl", d, qb)])
        for hd in range(4):
            hb2 = hd % 2
            rows = slice(hd * 512, (hd + 1) * 512)
            P.dma("sp", of_[hb2][:], OT["f"][rows, tsl].rearrange("(m p) t -> p m t", p=128),
                  reads=[("OT", "f", c) for c in range(4 * t, 4 * t + 4)], writes=[("of", hb2)])
            P.dma("sp", ob_[hb2][:], OT["b"][rows, tsl].rearrange("(m p) t -> p m t", p=128),
                  reads=[("OT", "b", c) for c in range(4 * t, 4 * t + 4)], writes=[("ob", hb2)])
            P.dma("sp", sgl[hb2][:], SG[rows, tsl].rearrange("(m p) t -> p m t", p=128),
                  reads=[("SG", hd * 4 + ec, t) for ec in range(4)], writes=[("sgl", hb2)])
            P.dve(lambda e, hb2=hb2: e.tensor_tensor(out=of_[hb2][:], in0=of_[hb2][:], in1=ob_[hb2][:], op=ALU.add),
                  reads=[("of", hb2), ("ob", hb2)], writes=[("of", hb2)])
            for ec in range(4):
                ps, pk = next_mm3()
                n = 0
                for d in "fb":
                    for dc in range(2):
                        i8 = hd * 2 + dc
                        P.pe(lambda e, ps=ps, d=d, i8=i8, ec=ec, qb=qb, n=n: e.matmul(
                            ps[:], lhsT=Sin[d][:, i8, ec * 128:(ec + 1) * 128], rhs=qcl[d][qb][:, i8, :],
                            start=(n == 0), stop=(n == 3)), reads=[("Sin", d), ("qcl", d, qb)], writes=[pk])
                        n += 1
                P.dve(lambda e, ps=ps, hb2=hb2, ec=ec: e.tensor_tensor(out=osum[hb2][:, ec, :], in0=ps[:], in1=of_[hb2][:, ec, :],
                                                                       op=ALU.add),
                      reads=[pk, ("of", hb2)], writes=[("osum", hb2, ec)])
                sb2 = ec % 2
                P.act(lambda e, hb2=hb2, ec=ec, sb2=sb2: e.activation(out=osq[sb2][:], in_=osum[hb2][:, ec, :], func=AF.Square),
                      reads=[("osum", hb2, ec)], writes=[("osq", sb2)])
                P.pe(lambda e, hb2=hb2, ec=ec, sb2=sb2: e.matmul(ssp[hb2][:], lhsT=C.ones[:], rhs=osq[sb2][:],
                                                                start=(ec == 0), stop=(ec == 3)),
                     reads=["ones", ("osq", sb2)], writes=[("ssp", hb2)])
                yield
            P.act(lambda e, hb2=hb2: e.activation(out=rsd[hb2][:], in_=ssp[hb2][:], func=AF.Ln, scale=1.0 / 512, bias=EPS),
                  reads=[("ssp", hb2)], writes=[("rsd", hb2)])
            P.act(lambda e, hb2=hb2: e.activation(out=rsd[hb2][:], in_=rsd[hb2][:], func=AF.Exp, scale=-0.5),
                  reads=[("rsd", hb2)], writes=[("rsd", hb2)])
            for ec in range(4):
                mg = hd * 4 + ec
                P.dve(lambda e, hb2=hb2, ec=ec: e.scalar_tensor_tensor(
                    out=osum[hb2][:, ec, :], in0=osum[hb2][:, ec, :], scalar=ng[:, ec:ec + 1], in1=rsd[hb2][:],
                    op0=ALU.mult, op1=ALU.mult), reads=[("osum", hb2, ec), "ng", ("rsd", hb2)], writes=[("osum", hb2, ec)])
                P.dve(lambda e, hb2=hb2, ec=ec, mg=mg, tb=tb: e.tensor_tensor(
                    out=z[tb][:, mg, :], in0=osum[hb2][:, ec, :], in1=sgl[hb2][:, ec, :], op=ALU.mult),
                    reads=[("osum", hb2, ec), ("sgl", hb2)], writes=[("z3", tb, mg)])
            yield

    def outproj_gen(t):
        tb = t % 2
        c0 = TT * t

        def load_res(m):
            rb = m % len(xres)
            P.dma("sp", xres[rb][:], xT[m * 128:(m + 1) * 128, c0:c0 + 512], writes=[("xres", rb)])
        for m in range(len(xres)):
            load_res(m)
        for sl in range(8):
            slab, skey = ws3.next()
            for ml in range(2):
                m = sl * 2 + ml
                ps, pk = next_mm3()
                for k in range(KC):
                    P.pe(lambda e, ps=ps, slab=slab, k=k, ml=ml, tb=tb: e.matmul(
                        ps[:], lhsT=slab[:, k, ml * 128:(ml + 1) * 128], rhs=z[tb][:, k, :],
                        start=(k == 0), stop=(k == KC - 1)), reads=[skey, ("z3", tb, k)], writes=[pk])
                rb = m % len(xres)
                ob = m % len(xo_t)
                P.dve(lambda e, ps=ps, m=m, rb=rb, ob=ob: e.scalar_tensor_tensor(
                    out=xo_t[ob][:], in0=ps[:], scalar=C.gate(m), in1=xres[rb][:],
                    op0=ALU.mult, op1=ALU.add), reads=[pk, "mod", ("xres", rb)], writes=[("xo_t", ob)])
                P.dma("pool", io["xo"][m * 128:(m + 1) * 128, TT * t:TT * (t + 1)], xo_t[ob][:],
                      reads=[("xo_t", ob)], writes=[("xo", t, m)])
                if m + len(xres) < KC:
                    load_res(m + len(xres))
                yield

    for _ in heads_gen(0):
        pass
    for t in range(NT):
        ga = outproj_gen(t)
        gb = heads_gen(t + 1) if t + 1 < NT else iter(())
        done_a = done_b = False
        while not (done_a and done_b):
            if not done_b:
                try:
                    next(gb)
                except StopIteration:
                    done_b = True
            if not done_a:
                try:
                    next(ga)
                except StopIteration:
                    done_a = True
    P.release(mtop)


def emit_attn_fused(P, nc, segs, io, pfx="f"):
    m0 = P.mark()
    wb_in = nc.dram_tensor(pfx + "awb_in", [D, 5120], BF16).ap()
    convert_w(P, io["w_in"], wb_in, D, 5120, "acw_in")
    wb_out = nc.dram_tensor(pfx + "awb_out", [D, D], BF16).ap()
    convert_w(P, io["w_out"], wb_out, D, D, "acw_out")
    ns = len(segs)
    Qs = [nc.dram_tensor(pfx + f"aQs{i}", [16, 128, TC], BF16).ap() for i in range(ns)]
    SG = [nc.dram_tensor(pfx + f"aSG{i}", [D, TC], BF16).ap() for i in range(ns)]
    Zs = [nc.dram_tensor(pfx + f"aZs{i}", [D, TC], BF16).ap() for i in range(ns)]
    KTs = [nc.dram_tensor(pfx + f"aKT{i}", [4, 128, TC], BF16).ap() for i in range(ns)]
    Vss = [nc.dram_tensor(pfx + f"aV{i}", [TC, 512], BF16).ap() for i in range(ns)]
    P.release(m0)

    for si, seg in enumerate(segs):
        xT = seg["xT"]
        qtiles = set(seg["qtiles"])
        C = Common(P, io["mod"])
        gains = P.sb([128, 2], F32, name="gains")
        P.dma("sp", gains[:], io["gains"], writes=["gains"])
        rot32 = P.sb([128, 128], F32, name="rot32")
        P.dma("sp", rot32[:], io["rotT"], writes=["rot32"])
        rotb = P.sb([128, 128], BF16, name="rotb")
        P.dve(lambda e, rotb=rotb, rot32=rot32: e.tensor_copy(out=rotb[:], in_=rot32[:]), reads=["rot32"], writes=["rotb"])
        order = []
        for t in range(NT):
            if t in qtiles:
                order += [slab_src(wb_in, 256 * j) for j in range(8)]
            order += [slab_src(wb_in, 2048 + 256 * j) for j in range(4)]
            if t in qtiles:
                order += [slab_src(wb_in, 3072 + 256 * j) for j in range(8)]
        ws = WStream(P, order, ring=4, name=f"wsA{si}")
        nb = norm_bufs(P, 512)
        hT = [P.sb([128, KC, 512], BF16, name=f"ahT{i}") for i in range(2)]
        cs = [P.sb([128, 512], F32) for _ in range(2)]
        sn = [P.sb([128, 512], F32) for _ in range(2)]
        qg_t = [P.sb([128, 512], BF16) for _ in range(2)]
        sq_t = [P.sb([128, 512], BF16) for _ in range(2)]
        rs_t = [P.sb([128, 512], F32) for _ in range(2)]
        t1 = [P.sb([128, 512], F32) for _ in range(2)]
        t2 = [P.sb([128, 512], F32) for _ in range(2)]
        qo = [P.sb([128, 512], BF16) for _ in range(3)]
        vbuf = [P.sb([128, 4, 512], BF16) for _ in range(2)]
        sgt = [P.sb([128, 512], BF16) for _ in range(3)]
        mm = [P.ps([128, 512], F32) for _ in range(3)]
        ssq = [P.ps([128, 512], F32) for _ in range(2)]
        rotp = [P.ps([128, 512], F32) for _ in range(2)]
        cnt = {"mm": 0, "hd": 0, "qo": 0, "sg": 0}

        def next_mm():
            i = cnt["mm"] % len(mm)
            cnt["mm"] += 1
            return mm[i], ("mm", i)

        def qk_head(t, hb, slab, skey, ml, gcol, dst, dkey):
            h = hT[hb]
            ps, pk = next_mm()
            for k in range(KC):
                P.pe(lambda e, ps=ps, slab=slab, k=k, ml=ml, h=h: e.matmul(
                    ps[:], lhsT=slab[:, k, ml * 128:(ml + 1) * 128], rhs=h[:, k, :],
                    start=(k == 0), stop=(k == KC - 1)), reads=[skey, (("ahT", hb), k)], writes=[pk])
            i2 = cnt["hd"] % 2
            cnt["hd"] += 1
            P.act(lambda e, ps=ps, i2=i2, gains=gains, qg_t=qg_t: e.activation(
                out=qg_t[i2][:], in_=ps[:], func=AF.Copy, scale=gains[:, gcol:gcol + 1]),
                reads=[pk, "gains"], writes=[("qg", i2)])
            P.act(lambda e, ps=ps, i2=i2, sq_t=sq_t: e.activation(out=sq_t[i2][:], in_=ps[:], func=AF.Square),
                  reads=[pk], writes=[("sq_t", i2)])
            return lambda: qk_tail(t, i2, dst, dkey)

        def qk_tail(t, i2, dst, dkey):
            P.pe(lambda e, i2=i2, C=C, ssq=ssq, sq_t=sq_t: e.matmul(ssq[i2][:], lhsT=C.ones[:], rhs=sq_t[i2][:], start=True, stop=True),
                 reads=["ones", ("sq_t", i2)], writes=[("ssq", i2)])
            P.pe(lambda e, i2=i2, rotp=rotp, rotb=rotb, qg_t=qg_t: e.matmul(rotp[i2][:], lhsT=rotb[:], rhs=qg_t[i2][:], start=True, stop=True),
                 reads=["rotb", ("qg", i2)], writes=[("rotp", i2)])
            P.act(lambda e, i2=i2, rs_t=rs_t, ssq=ssq: e.activation(out=rs_t[i2][:], in_=ssq[i2][:], func=AF.Ln, scale=1.0 / 128, bias=EPS),
                  reads=[("ssq", i2)], writes=[("rs_t", i2)])
            P.act(lambda e, i2=i2, rs_t=rs_t: e.activation(out=rs_t[i2][:], in_=rs_t[i2][:], func=AF.Exp, scale=-0.5),
                  reads=[("rs_t", i2)], writes=[("rs_t", i2)])
            cb = t % 2
            P.pool(lambda e, i2=i2, cb=cb, t1=t1, qg_t=qg_t, cs=cs: e.tensor_tensor(out=t1[i2][:], in0=qg_t[i2][:], in1=cs[cb][:], op=ALU.mult),
                   reads=[("qg", i2), ("cs", cb)], writes=[("t1", i2)])
            P.dve(lambda e, i2=i2, cb=cb, t2=t2, rotp=rotp, sn=sn: e.tensor_tensor(out=t2[i2][:], in0=rotp[i2][:], in1=sn[cb][:], op=ALU.mult),
                  reads=[("rotp", i2), ("sn", cb)], writes=[("t2", i2)])
            P.pool(lambda e, i2=i2, t1=t1, t2=t2: e.tensor_tensor(out=t1[i2][:], in0=t1[i2][:], in1=t2[i2][:], op=ALU.add),
                   reads=[("t1", i2), ("t2", i2)], writes=[("t1", i2)])
            i3 = cnt["qo"] % 3
            cnt["qo"] += 1
            P.dve(lambda e, i2=i2, i3=i3, qo=qo, t1=t1, rs_t=rs_t: e.tensor_tensor(out=qo[i3][:], in0=t1[i2][:], in1=rs_t[i2][:], op=ALU.mult),
                  reads=[("t1", i2), ("rs_t", i2)], writes=[("qo", i3)])
            P.dma("pool", dst, qo[i3][:], reads=[("qo", i3)], writes=[dkey])

        def projections(t):
            hb = t % 2
            h = hT[hb]
            cb = t % 2
            wantq = t in qtiles
            P.dma("sp", cs[cb][:], seg["cos"][:, t * TT:(t + 1) * TT], writes=[("cs", cb)])
            P.dma("sp", sn[cb][:], seg["sin"][:, t * TT:(t + 1) * TT], writes=[("sn", cb)])
            pend = [None]

            def flush(nxt=None):
                if pend[0] is not None:
                    pend[0]()
                pend[0] = nxt
            if wantq:
                for sl in range(8):
                    slab, skey = ws.next()
                    for ml in range(2):
                        hd = sl * 2 + ml
                        flush(qk_head(t, hb, slab, skey, ml, 0, Qs[si][hd][:, t * TT:(t + 1) * TT], ("Qs", si, hd, t)))
            for sl in range(2):
                slab, skey = ws.next()
                for ml in range(2):
                    kh = sl * 2 + ml
                    flush(qk_head(t, hb, slab, skey, ml, 1, KTs[si][kh][:, t * TT:(t + 1) * TT], ("KT", si, kh, t)))
            vb = t % 2
            for sl in range(2):
                slab, skey = ws.next()
                if sl == 1:
                    flush()
                for j in range(4):
                    ps, pk = next_mm()
                    for k in range(KC):
                        P.pe(lambda e, ps=ps, slab=slab, k=k, j=j, h=h: e.matmul(
                            ps[:, 0:256], lhsT=h[:, k, j * 128:(j + 1) * 128], rhs=slab[:, k, :],
                            start=(k == 0), stop=(k == KC - 1)), reads=[skey, (("ahT", hb), k)], writes=[pk])
                    P.act(lambda e, ps=ps, j=j, sl=sl, vb=vb, vbuf=vbuf: e.copy(out=vbuf[vb][:, j, sl * 256:(sl + 1) * 256],
                                                                                in_=ps[:, 0:256]),
                          reads=[pk], writes=[("vbuf", vb, j, sl)])
            P.dma("pool", Vss[si][t * TT:(t + 1) * TT, :].rearrange("(j p) n -> p j n", p=128), vbuf[vb][:],
                  reads=[("vbuf", vb, j, sl) for j in range(4) for sl in range(2)], writes=[("V", si, t)])
            if wantq:
                for sl in range(8):
                    slab, skey = ws.next()
                    for ml in range(2):
                        m = sl * 2 + ml
                        ps, pk = next_mm()
                        for k in range(KC):
                            P.pe(lambda e, ps=ps, slab=slab, k=k, ml=ml, h=h: e.matmul(
                                ps[:], lhsT=slab[:, k, ml * 128:(ml + 1) * 128], rhs=h[:, k, :],
                                start=(k == 0), stop=(k == KC - 1)), reads=[skey, (("ahT", hb), k)], writes=[pk])
                        i3 = cnt["sg"] % 3
                        cnt["sg"] += 1
                        P.act(lambda e, ps=ps, i3=i3, sgt=sgt: e.activation(out=sgt[i3][:], in_=ps[:], func=AF.Silu),
                              reads=[pk], writes=[("sgt", i3)])
                        P.dma("pool", SG[si][m * 128:(m + 1) * 128, t * TT:(t + 1) * TT], sgt[i3][:],
                              reads=[("sgt", i3)], writes=[("SG", si, m, t)])

        def norm(t):
            emit_norm_mod(P, C, xT, TT * t, 512, hT[t % 2], ("ahT", t % 2), nb)

        norm(0)
        for t in range(NT):
            if t + 1 < NT:
                norm(t + 1)
            projections(t)
        P.release(m0)

    qlist = [(si, t) for si, seg in enumerate(segs) for t in seg["qtiles"]]

    ones32 = P.sb([128, 128], F32, name="ones32")
    P.dve(lambda e: e.memset(ones32[:], 1.0), writes=["ones32"])
    nsc = ns * TC // 128
    kt = [P.sb([128, ns * TC], BF16, name=f"kt{i}") for i in range(2)]
    vt = [P.sb([128, nsc, 128], BF16, name=f"vt{i}") for i in range(2)]
    qt = [P.sb([128, 512], BF16) for _ in range(3)]
    sgq = [P.sb([128, 512], BF16) for _ in range(3)]
    pt = [P.sb([128, 2, 512], BF16) for _ in range(3)]
    accD = [P.sb([128, 2, 512], F32) for _ in range(2)]
    accP = [P.sb([128, 2, 512], F32) for _ in range(2)]
    rinv = [P.sb([128, 512], F32) for _ in range(2)]
    zt = [P.sb([128, 512], BF16) for _ in range(2)]
    st = [P.ps([128, 2, 512], F32) for _ in range(2)]
    ot = [P.ps([128, 512], F32) for _ in range(2)]
    rsp = [P.ps([128, 512], F32) for _ in range(1)]
    it = 0
    sti = 0
    pti = 0
    cps = TC // 128
    npair = nsc // 2
    for kvh in range(4):
        kb = kvh % 2
        for r in range(ns):
            P.dma("sp", kt[kb][:, r * TC:(r + 1) * TC], KTs[r][kvh], reads=[("KT", r, kvh, t) for t in range(NT)],
                  writes=[("kt", kb, r)])
            P.dma("sp", vt[kb][:, r * cps:(r + 1) * cps, :],
                  Vss[r][:, kvh * 128:(kvh + 1) * 128].rearrange("(c p) d -> p c d", p=128),
                  reads=[("V", r, t) for t in range(NT)], writes=[("vt", kb, r)])
        for gq_ in range(4):
            hd = kvh * 4 + gq_
            for (si, t) in qlist:
                ab = it % 2
                q3 = it % 3
                it += 1
                P.dma("sp", qt[q3][:], Qs[si][hd][:, t * TT:(t + 1) * TT], reads=[("Qs", si, hd, t)], writes=[("qt", q3)])
                P.dma("sp", sgq[q3][:], SG[si][hd * 128:(hd + 1) * 128, t * TT:(t + 1) * TT],
                      reads=[("SG", si, hd, t)], writes=[("sgq", q3)])
                LOOK = 1
                p3s = {}
                used = {"dve": False, "pool": False}
                for step in range(npair + LOOK):
                    if step < npair:
                        pr = step
                        s2 = sti % 2
                        sti += 1
                        p3 = pti % 3
                        pti += 1
                        p3s[pr] = p3
                        for j in range(2):
                            sc = 2 * pr + j
                            P.pe(lambda e, s2=s2, kb=kb, sc=sc, q3=q3, j=j: e.matmul(
                                st[s2][:, j, :], lhsT=kt[kb][:, sc * 128:(sc + 1) * 128], rhs=qt[q3][:], start=True, stop=True),
                                reads=[("kt", kb, sc // cps), ("qt", q3)], writes=[("st", s2, j)])
                        P.act(lambda e, s2=s2, p3=p3: e.activation(out=pt[p3][:], in_=st[s2][:], func=AF.Exp, scale=A_SCALE),
                              reads=[("st", s2, 0), ("st", s2, 1)], writes=[("pt", p3)])
                    if step >= LOOK:
                        pr = step - LOOK
                        p3 = p3s[pr]
                        for j in range(2):
                            sc = 2 * pr + j
                            P.pe(lambda e, ab=ab, kb=kb, sc=sc, p3=p3, j=j: e.matmul(
                                ot[ab][:], lhsT=vt[kb][:, sc, :], rhs=pt[p3][:, j, :], start=(sc == 0), stop=(sc == nsc - 1)),
                                reads=[("vt", kb, sc // cps), ("pt", p3)], writes=[("ot", ab)])
                        eng, acc, akey = ("dve", accD[ab], ("accD", ab)) if pr % 3 != 2 else ("pool", accP[ab], ("accP", ab))
                        if not used[eng]:
                            used[eng] = True
                            P.add(eng, lambda e, acc=acc, p3=p3: e.tensor_copy(out=acc[:], in_=pt[p3][:]),
                                  reads=[("pt", p3)], writes=[akey])
                        else:
                            P.add(eng, lambda e, acc=acc, p3=p3: e.tensor_tensor(out=acc[:], in0=acc[:], in1=pt[p3][:], op=ALU.add),
                                  reads=[("pt", p3), akey], writes=[akey])
                n4 = 0
                for acc, akey in ((accD[ab], ("accD", ab)), (accP[ab], ("accP", ab))):
                    for j in range(2):
                        P.pe(lambda e, acc=acc, j=j, n4=n4: e.matmul(rsp[0][:], lhsT=ones32[:], rhs=acc[:, j, :],
                                                                     start=(n4 == 0), stop=(n4 == 3)),
                             reads=["ones32", akey], writes=[("rsp", 0)])
                        n4 += 1
                P.dve(lambda e, ab=ab: e.reciprocal(out=rinv[ab][:], in_=rsp[0][:]), reads=[("rsp", 0)], writes=[("rinv", ab)])
                P.pool(lambda e, ab=ab, q3=q3: e.tensor_tensor(out=rinv[ab][:], in0=rinv[ab][:], in1=sgq[q3][:], op=ALU.mult),
                       reads=[("rinv", ab), ("sgq", q3)], writes=[("rinv", ab)])
                P.dve(lambda e, ab=ab: e.tensor_tensor(out=zt[ab][:], in0=ot[ab][:], in1=rinv[ab][:], op=ALU.mult),
                      reads=[("ot", ab), ("rinv", ab)], writes=[("zt", ab)])
                P.dma("pool", Zs[si][hd * 128:(hd + 1) * 128, t * TT:(t + 1) * TT], zt[ab][:],
                      reads=[("zt", ab)], writes=[("Zs", si, hd, t)])
    P.release(m0)

    C = Common(P, io["mod"])
    ws2 = WStream(P, [slab_src(wb_out, 256 * j) for j in range(8)] * len(qlist), ring=4, name="wsC")
    mmC = [P.ps([128, 512], F32) for _ in range(4)]
    z = [P.sb([128, KC, 512], BF16) for _ in range(2)]
    xres = [P.sb([128, 512], F32) for _ in range(3)]
    xo_t = [P.sb([128, 512], F32) for _ in range(3)]
    mmi = 0
    for qi, (si, t) in enumerate(qlist):
        xT, xo = segs[si]["xT"], segs[si]["xo"]
        zb = qi % 2
        for k in range(KC):
            P.dma("sp", z[zb][:, k, :], Zs[si][k * 128:(k + 1) * 128, t * TT:(t + 1) * TT],
                  reads=[("Zs", si, k, t)], writes=[("zc", zb, k)])
        c0 = TT * t

        def load_res(m, xT=xT, c0=c0):
            rb = m % len(xres)
            P.dma("sp", xres[rb][:], xT[m * 128:(m + 1) * 128, c0:c0 + 512], writes=[("xres", rb)])
        for m in range(len(xres)):
            load_res(m)
        for sl in range(8):
            slab, skey = ws2.next()
            for ml in range(2):
                m = sl * 2 + ml
                ps, pk = mmC[mmi % 4], ("mmC", mmi % 4)
                mmi += 1
                for k in range(KC):
                    P.pe(lambda e, ps=ps, slab=slab, k=k, ml=ml, zb=zb: e.matmul(
                        ps[:], lhsT=slab[:, k, ml * 128:(ml + 1) * 128], rhs=z[zb][:, k, :],
                        start=(k == 0), stop=(k == KC - 1)), reads=[skey, ("zc", zb, k)], writes=[pk])
                rb = m % len(xres)
                ob = m % len(xo_t)
                P.dve(lambda e, ps=ps, m=m, rb=rb, ob=ob, C=C: e.scalar_tensor_tensor(
                    out=xo_t[ob][:], in0=ps[:], scalar=C.gate(m), in1=xres[rb][:],
                    op0=ALU.mult, op1=ALU.add), reads=[pk, "mod", ("xres", rb)], writes=[("xo_t", ob)])
                P.dma("pool", xo[m * 128:(m + 1) * 128, TT * t:TT * (t + 1)], xo_t[ob][:],
                      reads=[("xo_t", ob)], writes=[("xo", si, t, m)])
                if m + len(xres) < KC:
                    load_res(m + len(xres))
    P.release(m0)


def emit_modulation(P, nc, io, moddram):
    m0 = P.mark()
    ct = P.sb([128, 16, 1], F32, name="m_ct")
    sc = P.sb([128, 16, 1], F32, name="m_sc")
    one = P.sb([1, 1], F32, name="m_one")
    P.dve(lambda e: e.memset(one[:], 1.0), writes=["m_one"])
    P.dma("sp", ct[:], io["cT"], writes=["m_ct"])
    P.act(lambda e: e.activation(out=sc[:], in_=ct[:], func=AF.Silu), reads=["m_ct"], writes=["m_sc"])
    wt = [P.sb([128, 3072], F32) for _ in range(3)]
    brow = P.sb([1, 6144], F32, name="m_brow")
    mrow = P.sb([1, 6144], F32, name="m_mrow")
    modsb = P.sb([128, 48], F32, name="m_modsb")
    acc = [P.ps([1, 512], F32) for _ in range(6)]
    tp = P.ps([128, 48], F32, name="m_tp")
    wi = 0
    for l in range(4):
        P.dma("sp", brow[:], io["b_mod"][l:l + 1, :], writes=["m_brow"])
        for hf in range(2):
            for kc in range(16):
                b = wi % 3
                wi += 1
                P.dma("sp", wt[b][:], io["w_mod"][l, kc * 128:(kc + 1) * 128, hf * 3072:(hf + 1) * 3072], writes=[("m_wt", b)])
                for n in range(6):
                    P.pe(lambda e, kc=kc, n=n, b=b: e.matmul(acc[n][:], lhsT=sc[:, kc, :], rhs=wt[b][:, n * 512:(n + 1) * 512],
                                                             start=(kc == 0), stop=(kc == 15)),
                         reads=["m_sc", ("m_wt", b)], writes=[("m_acc", n)])
            for n in range(6):
                cs_ = slice(hf * 3072 + n * 512, hf * 3072 + (n + 1) * 512)
                P.dve(lambda e, n=n, cs_=cs_: e.tensor_tensor(out=mrow[:, cs_], in0=acc[n][:], in1=brow[:, cs_], op=ALU.add),
                      reads=[("m_acc", n), "m_brow"], writes=[("m_mrow", hf, n)])
        for j in range(48):
            P.pe(lambda e, j=j: e.matmul(tp[:, j:j + 1], lhsT=mrow[0:1, j * 128:(j + 1) * 128], rhs=one[0:1, 0:1],
                                         start=True, stop=True),
                 reads=[("m_mrow", hf, n) for hf in range(2) for n in range(6)] + ["m_one"], writes=["m_tp"])
        P.dve(lambda e: e.tensor_copy(out=modsb[:], in_=tp[:]), reads=["m_tp"], writes=["m_modsb"])
        P.dma("pool", moddram[l].rearrange("p s k -> p (s k)"), modsb[:], reads=["m_modsb"], writes=[("moddram", l)])
    P.release(m0)


SEQ = 8192
_NC_CACHE = {}


def _new_nc():
    return bass.Bass("TRN2", target_bir_lowering=False)


def build_fused():
    nc = _new_nc()
    E = {}

    def inp(name, shape, dt=F32):
        E[name] = nc.dram_tensor(name, shape, dt, kind="ExternalInput").ap()
    for s in "AB":
        inp("xT" + s, [2048, TC + 16]); inp("hmask" + s, [128, 2]); inp("icnt" + s, [128, 4, 16])
        inp("cos" + s, [128, TC]); inp("sin" + s, [128, TC])
    inp("cT", [128, 16, 1]); inp("w_mod", [4, 2048, 6144]); inp("b_mod", [4, 6144])
    for p in ("p0", "p3"):
        inp(p + "_w_in", [2048, 4096]); inp(p + "_w_grp", [2048, 512]); inp(p + "_w_out", [2048, 2048])
        inp(p + "_pscale", [128, 16])
    inp("fg", [128, 16])
    inp("g_w_in", [2048, 6144]); inp("g_w_out", [2048, 2048]); inp("w1cat", [2048, 64]); inp("w2aug", [64, 1024])
    inp("trif", [128, 128]); inp("trib", [128, 128]); inp("maskf", [128, 128]); inp("maskb", [128, 128])
    inp("ident", [128, 128]); inp("ng", [128, 4])
    inp("a_w_in", [2048, 5120]); inp("a_w_out", [2048, 2048]); inp("gains", [128, 2]); inp("rotT", [128, 128])
    y = nc.dram_tensor("y", [2048, TC], F32, kind="ExternalOutput").ap()

    def scr(name, shape, dt=F32):
        return nc.dram_tensor(name, shape, dt).ap()
    moddram = scr("moddram", [4, 128, 3, 16])
    X1 = [scr("X1A", [2048, TC]), scr("X1B", [2048, TC])]
    X2 = [scr("X2A", [2048, TC]), scr("X2B", [2048, TC])]
    X3A = scr("X3A", [2048, TC + 16])
    X3B = scr("X3B", [2048, TC])
    Se = [{d: scr(f"Se{s}{d}", [128, 8, 512]) for d in "fb"} for s in "AB"]

    P = Prog(nc)
    emit_modulation(P, nc, E, moddram)

    wb = None
    for si, s in enumerate("AB"):
        io = {"xT": E["xT" + s], "mod": moddram[0], "w_in": E["p0_w_in"], "w_grp": E["p0_w_grp"], "w_out": E["p0_w_out"],
              "pscale": E["p0_pscale"], "hmask": E["hmask" + s], "icnt": E["icnt" + s], "xo": X1[si]}
        wb = emit_pool_layer(P, nc, io, final=False, pfx="p0", wb=wb)

    gio = {"mod": moddram[1], "w_in": E["g_w_in"], "w_out": E["g_w_out"], "w1cat": E["w1cat"], "w2aug": E["w2aug"],
           "trif": E["trif"], "trib": E["trib"], "maskf": E["maskf"], "maskb": E["maskb"], "ident": E["ident"],
           "ng": E["ng"]}
    scrs = []
    gwb = None
    for si, s in enumerate("AB"):
        io = dict(gio, xT=X1[si], S_end_f=Se[si]["f"], S_end_b=Se[si]["b"])
        r = emit_gla_layer(P, nc, io, "state", pfx="g" + s, wb=gwb)
        gwb = r["wb"]
        scrs.append(r)
    for si, s in enumerate("AB"):
        io = dict(gio, xT=X1[si], hmask=E["hmask" + s], S_in_f=Se[1 - si]["f"], S_in_b=Se[1 - si]["b"], xo=X2[si])
        emit_gla_layer(P, nc, io, "sweep3", pfx="g" + s, wb=gwb, scr=scrs[si])

    segs = [dict(xT=X2[0], cos=E["cosA"], sin=E["sinA"], qtiles=list(range(NT)), xo=X3A[:, 8:8 + TC]),
            dict(xT=X2[1], cos=E["cosB"], sin=E["sinB"], qtiles=[0, NT - 1], xo=X3B)]
    aio = {"mod": moddram[2], "w_in": E["a_w_in"], "w_out": E["a_w_out"], "gains": E["gains"], "rotT": E["rotT"]}
    emit_attn_fused(P, nc, segs, aio, pfx="f")
    P.dma("sp", X3A[:, 0:8], X3B[:, TC - 8:TC], writes=["halo_l"], allow_slow_non_contiguous=True)
    P.dma("sp", X3A[:, TC + 8:TC + 16], X3B[:, 0:8], writes=["halo_r"], allow_slow_non_contiguous=True)

    io = {"xT": X3A, "mod": moddram[3], "w_in": E["p3_w_in"], "w_grp": E["p3_w_grp"], "w_out": E["p3_w_out"],
          "pscale": E["p3_pscale"], "hmask": E["hmaskA"], "icnt": E["icntA"], "fg": E["fg"], "xo": y}
    emit_pool_layer(P, nc, io, final=True, pfx="p3")
    P.emit()
    return nc


def _rope_tables(t0):
    t = np.arange(t0, t0 + TC)
    row = (t // 64 - (SEQ // 64) // 2).astype(np.float32)
    col = (t % 64 - 32).astype(np.float32)
    inv = (np.float32(10000.0) ** (-np.arange(0, 64, 2, dtype=np.float32) / np.float32(64))).astype(np.float32)
    ang = np.concatenate([row[:, None] * inv, col[:, None] * inv], axis=-1)
    cos = np.repeat(np.cos(ang).astype(np.float32), 2, axis=1).T
    sin = np.repeat(np.sin(ang).astype(np.float32), 2, axis=1).T
    return np.ascontiguousarray(cos), np.ascontiguousarray(sin)


def _rot_const():
    r = np.zeros((128, 128), np.float32)
    for i in range(64):
        r[2 * i + 1, 2 * i] = -1.0
        r[2 * i, 2 * i + 1] = 1.0
    return r


def _gla_consts():
    j = np.arange(128)[:, None]
    i = np.arange(128)[None, :]
    return {
        "trif": np.where(j <= i, -1.0 / 16, 0.0).astype(np.float32),
        "trib": np.where(j >= i, -1.0 / 16, 0.0).astype(np.float32),
        "maskf": (i >= j).astype(np.float32),
        "maskb": (i < j).astype(np.float32),
        "ident": np.eye(128, dtype=np.float32),
    }


def _pool_consts(h):
    t0 = h * TC
    ic = np.zeros((4, 16), np.float32)
    for gi in range(4):
        w = 2 << gi
        for j in range(8):
            for side, t in ((0, t0 + j), (1, t0 + TC - 8 + j)):
                cnt = min(t + w // 2, SEQ) - max(t - w // 2, 0)
                ic[gi, side * 8 + j] = 1.0 / cnt
    hmask = np.broadcast_to(np.array([float(h == 1), float(h == 0)], np.float32), (128, 2))
    return np.ascontiguousarray(hmask), np.ascontiguousarray(np.broadcast_to(ic, (128, 4, 16)))


def _col(v, k=16):
    return np.ascontiguousarray(np.asarray(v, np.float32).reshape(k, 128).T)


def _halo_x(x, b, h):
    t0 = h * TC
    xp = np.zeros((TC + 16, 2048), np.float32)
    lo, hi = max(t0 - 8, 0), min(t0 + TC + 8, SEQ)
    xp[lo - (t0 - 8):hi - (t0 - 8)] = x[b, lo:hi]
    return np.ascontiguousarray(xp.T)


def kernel(x, c, w_mod, b_mod, pool_w_in, pool_w_grp, pool_scale, pool_w_out,
           gla_w_in, gla_fwd_w1, gla_fwd_w2, gla_fwd_b, gla_bwd_w1, gla_bwd_w2, gla_bwd_b,
           gla_norm_g, gla_w_out, attn_w_in, attn_q_norm_g, attn_k_norm_g, attn_w_out, final_norm_g):
    f32 = lambda a: np.ascontiguousarray(np.asarray(a, dtype=np.float32))
    x = f32(x)
    c = f32(c)
    w1 = np.zeros((2048, 64), np.float32)
    w1[:, 0:16] = gla_fwd_w1[0]
    w1[:, 32:48] = gla_bwd_w1[0]
    w2 = np.zeros((64, 1024), np.float32)
    w2[0:16] = gla_fwd_w2[0]
    w2[16] = gla_fwd_b[0]
    w2[32:48] = gla_bwd_w2[0]
    w2[48] = gla_bwd_b[0]
    shared = {
        "w_mod": f32(w_mod), "b_mod": f32(b_mod),
        "p0_w_in": f32(pool_w_in[0]), "p0_w_grp": f32(np.asarray(pool_w_grp[0]).reshape(2048, 512)),
        "p0_w_out": f32(pool_w_out[0]), "p0_pscale": _col(pool_scale[0]),
        "p3_w_in": f32(pool_w_in[1]), "p3_w_grp": f32(np.asarray(pool_w_grp[1]).reshape(2048, 512)),
        "p3_w_out": f32(pool_w_out[1]), "p3_pscale": _col(pool_scale[1]),
        "fg": _col(final_norm_g),
        "g_w_in": f32(gla_w_in[0]), "g_w_out": f32(gla_w_out[0]), "w1cat": w1, "w2aug": w2, "ng": _col(gla_norm_g[0], 4),
        "a_w_in": f32(attn_w_in[0]), "a_w_out": f32(attn_w_out[0]),
        "gains": np.ascontiguousarray(np.stack([np.asarray(attn_q_norm_g[0], np.float32),
                                                np.asarray(attn_k_norm_g[0], np.float32)], axis=1)),
        "rotT": _rot_const(),
    }
    shared.update(_gla_consts())
    rope = [_rope_tables(0), _rope_tables(TC)]
    pc = [_pool_consts(0), _pool_consts(1)]
    in_maps = []
    for core in range(8):
        b, h = core // 2, core % 2
        d = dict(shared)
        for s, hh in (("A", h), ("B", 1 - h)):
            d["xT" + s] = _halo_x(x, b, hh)
            d["hmask" + s], d["icnt" + s] = pc[hh]
            d["cos" + s], d["sin" + s] = rope[hh]
        d["cT"] = np.ascontiguousarray(c[b].reshape(16, 128).T.reshape(128, 16, 1))
        in_maps.append(d)
    if "fused" not in _NC_CACHE:
        _NC_CACHE["fused"] = build_fused()
    res = run_bass_kernel_spmd(_NC_CACHE["fused"], in_maps, core_ids=list(range(8)))
    out = np.empty((4, SEQ, 2048), np.float32)
    for core in range(8):
        out[core // 2, (core % 2) * TC:(core % 2 + 1) * TC] = np.asarray(res.results[core]["y"]).T
    return out
```

```python
import numpy as np
import concourse.bass as bass
import concourse.mybir as mybir
from concourse.bass_utils import run_bass_kernel_spmd

F32 = mybir.dt.float32
BF16 = mybir.dt.bfloat16
ALU = mybir.AluOpType
AF = mybir.ActivationFunctionType
AX = mybir.AxisListType

ENGS = ("pe", "act", "dve", "pool", "sp")
SEM_CAP = 9000


class _Op:
    __slots__ = ("eng", "fn", "dma", "seq", "deps", "signal", "sig_idx", "slot", "cnt", "waits", "id")


class Prog:
    def __init__(self, nc, ring=8):
        self.nc = nc
        self.ring = ring
        self.ops = []
        self.eng_ops = {e: [] for e in ENGS}
        self.last_w = {}
        self.readers = {}
        self.n_dma = {e: 0 for e in ENGS}
        self._nm = 0
        self.pending = {e: set() for e in ENGS}
        self.dma_ops = {e: [] for e in ENGS}

    def sb(self, shape, dtype, name=None):
        self._nm += 1
        return self.nc.alloc_sbuf_tensor(f"s{self._nm}_{name or ""}", list(shape), dtype)

    def ps(self, shape, dtype=F32, name=None):
        self._nm += 1
        return self.nc.alloc_psum_tensor(f"p{self._nm}_{name or ""}", list(shape), dtype)

    def add(self, eng, fn, reads=(), writes=(), dma=False):
        op = _Op()
        op.eng, op.fn, op.dma = eng, fn, dma
        op.id = len(self.ops)
        op.seq = len(self.eng_ops[eng])
        op.signal = dma
        op.sig_idx = None
        op.slot = op.cnt = None
        deps = set()
        for r in reads:
            w = self.last_w.get(r)
            if w is not None:
                deps.add(w)
        for k in writes:
            w = self.last_w.get(k)
            if w is not None:
                deps.add(w)
            for rd in self.readers.get(k, ()):
                deps.add(rd)
        for r in reads:
            self.readers.setdefault(r, []).append(op.id)
        for k in writes:
            self.last_w[k] = op.id
            self.readers[k] = []
        deps.discard(op.id)
        if self.pending[eng]:
            deps |= self.pending[eng]
            self.pending[eng] = set()
        if dma:
            self.dma_ops[eng].append(op.id)
            j = self.n_dma[eng]
            self.n_dma[eng] += 1
            op.slot = j % self.ring
            op.cnt = j // self.ring + 1
        op.deps = deps
        self.ops.append(op)
        self.eng_ops[eng].append(op)
        return op

    def barrier(self):
        front = set()
        for e in ENGS:
            comp = [o for o in self.eng_ops[e] if not o.dma]
            if comp:
                front.add(comp[-1].id)
            for d in self.dma_ops[e][-self.ring:]:
                front.add(d)
        for e in ENGS:
            self.pending[e] |= front

    def mark(self):
        nc = self.nc
        return (nc.sbuf_base, nc.sbuf_top, nc.psum_base, nc.psum_top)

    def release(self, m):
        self.barrier()
        nc = self.nc
        nc.sbuf_base, nc.sbuf_top, nc.psum_base, nc.psum_top = m

    def pe(self, fn, reads=(), writes=()):
        return self.add("pe", fn, reads, writes)

    def act(self, fn, reads=(), writes=()):
        return self.add("act", fn, reads, writes)

    def dve(self, fn, reads=(), writes=()):
        return self.add("dve", fn, reads, writes)

    def pool(self, fn, reads=(), writes=()):
        return self.add("pool", fn, reads, writes)

    def dma(self, q, out, in_, reads=(), writes=(), **kw):
        return self.add(q, lambda e: e.dma_start(out=out, in_=in_, **kw), reads, writes, dma=True)

    def emit(self):
        nc = self.nc
        ops = self.ops
        seen_c = {e: {f: -1 for f in ENGS} for e in ENGS}
        seen_d = {e: {} for e in ENGS}
        for op in ops:
            waits = []
            e = op.eng
            if op.dma and op.cnt > 1:
                key = (e, op.slot)
                if seen_d[e].get(key, 0) < op.cnt - 1:
                    seen_d[e][key] = op.cnt - 1
                    waits.append(("d", e, op.slot, op.cnt - 1))
            cmax = {}
            for d in op.deps:
                p = ops[d]
                if not p.dma and (p.eng not in cmax or ops[cmax[p.eng]].seq < p.seq):
                    cmax[p.eng] = d
            for d in sorted(op.deps):
                p = ops[d]
                if not p.dma and cmax[p.eng] != d:
                    continue
                if p.dma:
                    key = (p.eng, p.slot)
                    if seen_d[e].get(key, 0) >= p.cnt:
                        continue
                    seen_d[e][key] = p.cnt
                    waits.append(("d", p.eng, p.slot, p.cnt))
                else:
                    if p.eng == "pe" and e == "pe":
                        continue
                    if seen_c[e][p.eng] >= p.seq:
                        continue
                    seen_c[e][p.eng] = p.seq
                    p.signal = True
                    waits.append(("c", d))
            op.waits = waits
        nsig = {}
        for e in ENGS:
            n = 0
            for op in self.eng_ops[e]:
                if op.signal and not op.dma:
                    n += 1
                    op.sig_idx = n
            nsig[e] = n
        csem = {}
        for e in ENGS:
            k = (nsig[e] + SEM_CAP - 1) // SEM_CAP
            csem[e] = [nc.alloc_semaphore(f"c_{e}_{i}") for i in range(k)]
        dsem = {}
        for e in ENGS:
            if self.n_dma[e]:
                dsem[e] = [nc.alloc_semaphore(f"d_{e}_{i}") for i in range(min(self.ring, self.n_dma[e]))]
        self.stats = {e: (len(self.eng_ops[e]), nsig[e], self.n_dma[e]) for e in ENGS}

        def run(e, eng):
            for op in self.eng_ops[e]:
                for w in op.waits:
                    if w[0] == "d":
                        eng.wait_ge(dsem[w[1]][w[2]], 16 * w[3])
                    else:
                        p = ops[w[1]]
                        i = p.sig_idx - 1
                        eng.wait_ge(csem[p.eng][i // SEM_CAP], i % SEM_CAP + 1)
                ins = op.fn(eng)
                if op.dma:
                    ins.then_inc(dsem[e][op.slot], 16)
                elif op.signal:
                    i = op.sig_idx - 1
                    ins.then_inc(csem[e][i // SEM_CAP], 1)
            if self.n_dma[e]:
                for s in range(min(self.ring, self.n_dma[e])):
                    last = (self.n_dma[e] - 1 - s) // self.ring + 1
                    eng.wait_ge(dsem[e][s], 16 * last)

        with nc.Block() as block:
            @block.tensor
            def _(eng):
                run("pe", eng)

            @block.scalar
            def _(eng):
                run("act", eng)

            @block.vector
            def _(eng):
                run("dve", eng)

            @block.gpsimd
            def _(eng):
                run("pool", eng)

            @block.sync
            def _(eng):
                run("sp", eng)


D = 2048
KC = 16
TC = 4096
TT = 512
NT = TC // TT
EPS = 1e-6
SLABW = 256


class WStream:
    def __init__(self, P, srcs, ring=4, q="sp", name="ws"):
        self.P, self.srcs, self.ring, self.q, self.name = P, srcs, ring, q, name
        self.bufs = [P.sb([128, KC, SLABW], BF16, name=f"{name}{i}") for i in range(ring)]
        self.issued = 0
        self.pos = 0

    def _issue(self):
        i = self.issued
        if i >= len(self.srcs):
            return
        s = i % self.ring
        self.P.dma(self.q, self.bufs[s][:], self.srcs[i], writes=[(self.name, s)])
        self.issued += 1

    def prefetch(self, n=None):
        n = self.ring if n is None else n
        while self.issued < min(self.pos + n, len(self.srcs)):
            self._issue()

    def next(self):
        self.prefetch(self.ring)
        i = self.pos
        self.pos += 1
        s = i % self.ring
        return self.bufs[s], (self.name, s)


def slab_src(wb, n0):
    return wb.rearrange("(k p) n -> p k n", p=128)[:, :, n0:n0 + SLABW]


def convert_w(P, src, dst, K, N, tag):
    CB = 2048
    st32 = [P.sb([128, CB], F32) for _ in range(2)]
    st16 = [P.sb([128, CB], BF16) for _ in range(2)]
    i = 0
    for kb in range(K // 128):
        for c0 in range(0, N, CB):
            cw = min(CB, N - c0)
            b = i % 2
            P.dma("sp", st32[b][:, :cw], src[kb * 128:(kb + 1) * 128, c0:c0 + cw], writes=[(tag, "s32", b)])
            eng = ("dve", "pool", "act")[i % 3]
            if eng == "act":
                P.act(lambda e, b=b, cw=cw: e.copy(out=st16[b][:, :cw], in_=st32[b][:, :cw]),
                      reads=[(tag, "s32", b)], writes=[(tag, "s16", b)])
            else:
                P.add(eng, lambda e, b=b, cw=cw: e.tensor_copy(out=st16[b][:, :cw], in_=st32[b][:, :cw]),
                      reads=[(tag, "s32", b)], writes=[(tag, "s16", b)])
            P.dma("pool", dst[kb * 128:(kb + 1) * 128, c0:c0 + cw], st16[b][:, :cw],
                  reads=[(tag, "s16", b)], writes=[(tag, "dram", i)])
            i += 1


class Common:
    def __init__(self, P, mod_ap):
        self.P = P
        self.ones = P.sb([128, 128], BF16, name="ones")
        P.dve(lambda e: e.memset(self.ones[:], 1.0), writes=["ones"])
        self.mod = P.sb([128, 3, KC], F32, name="mod")
        P.dma("sp", self.mod[:], mod_ap, writes=["mod"])
        self.s1 = P.sb([128, KC], F32, name="s1")
        P.dve(lambda e: e.tensor_scalar_add(out=self.s1[:], in0=self.mod[:, 1, :], scalar1=1.0),
              reads=["mod"], writes=["s1"])

    def shift(self, k):
        return self.mod[:, 0, k:k + 1]

    def gate(self, k):
        return self.mod[:, 2, k:k + 1]


def emit_norm_mod(P, C, xT, c_lo, ncol, hT, hkey, bufs):
    xs, sq, tt, ss_a, ss_b, rstd = bufs["xs"], bufs["sq"], bufs["tt"], bufs["ss_a"], bufs["ss_b"], bufs["rstd"]
    na = min(ncol, 512)
    nb = ncol - na
    for k in range(KC):
        b = k % len(xs)
        P.dma("sp", xs[b][:, :ncol], xT[k * 128:(k + 1) * 128, c_lo:c_lo + ncol], writes=[("xs", b)])
        sb_ = k % len(sq)
        P.act(lambda e, b=b, sb_=sb_: e.activation(out=sq[sb_][:, :ncol], in_=xs[b][:, :ncol], func=AF.Square),
              reads=[("xs", b)], writes=[("sq", sb_)])
        P.pe(lambda e, sb_=sb_, k=k: e.matmul(ss_a[:, :na], lhsT=C.ones[:], rhs=sq[sb_][:, :na],
                                              start=(k == 0), stop=(k == KC - 1)),
             reads=["ones", ("sq", sb_)], writes=["ss_a"])
        if nb:
            P.pe(lambda e, sb_=sb_, k=k: e.matmul(ss_b[:, :nb], lhsT=C.ones[:], rhs=sq[sb_][:, na:ncol],
                                                  start=(k == 0), stop=(k == KC - 1)),
                 reads=["ones", ("sq", sb_)], writes=["ss_b"])
    P.dve(lambda e: e.tensor_scalar(out=rstd[:, :na], in0=ss_a[:, :na], scalar1=1.0 / D, scalar2=EPS,
                                    op0=ALU.mult, op1=ALU.add), reads=["ss_a"], writes=["rstd"])
    if nb:
        P.dve(lambda e: e.tensor_scalar(out=rstd[:, na:ncol], in0=ss_b[:, :nb], scalar1=1.0 / D, scalar2=EPS,
                                        op0=ALU.mult, op1=ALU.add), reads=["ss_b", "rstd"], writes=["rstd"])
    P.dve(lambda e: e.reciprocal(out=rstd[:, :ncol], in_=rstd[:, :ncol]), reads=["rstd"], writes=["rstd"])
    P.act(lambda e: e.activation(out=rstd[:, :ncol], in_=rstd[:, :ncol], func=AF.Sqrt), reads=["rstd"], writes=["rstd"])
    for k in range(KC):
        b = k % len(xs)
        P.dma("sp", xs[b][:, :ncol], xT[k * 128:(k + 1) * 128, c_lo:c_lo + ncol], writes=[("xs", b)])
        tb = k % len(tt)
        P.dve(lambda e, b=b, tb=tb, k=k: e.scalar_tensor_tensor(
            out=tt[tb][:, :ncol], in0=xs[b][:, :ncol], scalar=C.s1[:, k:k + 1], in1=rstd[:, :ncol],
            op0=ALU.mult, op1=ALU.mult), reads=[("xs", b), "s1", "rstd"], writes=[("tt", tb)])
        P.act(lambda e, tb=tb, k=k: e.activation(out=hT[:, k, :ncol], in_=tt[tb][:, :ncol], func=AF.Identity,
                                                 bias=C.shift(k), scale=1.0),
              reads=[("tt", tb), "mod"], writes=[(hkey, k)])


def norm_bufs(P, width=528):
    return dict(
        xs=[P.sb([128, width], F32) for _ in range(4)],
        sq=[P.sb([128, width], BF16) for _ in range(2)],
        tt=[P.sb([128, width], F32) for _ in range(2)],
        ss_a=P.ps([128, 512], F32), ss_b=(P.ps([128, 16], F32) if width > 512 else None),
        rstd=P.sb([128, width], F32),
    )


def emit_pool_layer(P, nc, io, final=False, pfx="", wb=None):
    xT = io["xT"]
    m0 = P.mark()
    if wb is None:
        wb_in = nc.dram_tensor(pfx + "wb_in", [D, 4096], BF16).ap()
        wb_grp = nc.dram_tensor(pfx + "wb_grp", [D, 512], BF16).ap()
        wb_out = nc.dram_tensor(pfx + "wb_out", [D, D], BF16).ap()
        convert_w(P, io["w_in"], wb_in, D, 4096, "cw_in")
        convert_w(P, io["w_grp"], wb_grp, D, 512, "cw_grp")
        convert_w(P, io["w_out"], wb_out, D, D, "cw_out")
        wb = (wb_in, wb_grp, wb_out)
    wb_in, wb_grp, wb_out = wb
    P.release(m0)

    C = Common(P, io["mod"])
    pscale = P.sb([128, KC], F32, name="pscale")
    P.dma("sp", pscale[:], io["pscale"], writes=["pscale"])
    hmask = P.sb([128, 2], F32, name="hmask")
    P.dma("sp", hmask[:], io["hmask"], writes=["hmask"])
    icnt = P.sb([128, 4, 16], F32, name="icnt")
    P.dma("sp", icnt[:], io["icnt"], writes=["icnt"])
    if final:
        fg = P.sb([128, KC], F32, name="fg")
        P.dma("sp", fg[:], io["fg"], writes=["fg"])
    wg = P.sb([128, 4, 4, 512], BF16, name="wg")
    for g in range(4):
        P.dma("sp", wg[:, g, :, :], wb_grp[g * 512:(g + 1) * 512, :].rearrange("(k p) n -> p k n", p=128),
              writes=[("wg", g)])

    order = []
    for gi in range(4):
        order += [slab_src(wb_in, 512 * gi), slab_src(wb_in, 512 * gi + 256)]
        order += [slab_src(wb_in, 2048 + 512 * gi), slab_src(wb_in, 2048 + 512 * gi + 256)]
    for j in range(8):
        order.append(slab_src(wb_out, 256 * j))
    ws = WStream(P, order * NT, ring=4)

    nb = norm_bufs(P)
    hT = [P.sb([128, KC, 528], BF16, name=f"hT{i}") for i in range(2)]
    U = P.sb([128, 4, 528], F32, name="U")
    Ta = P.sb([128, 4, 528], F32, name="Ta")
    Tb = P.sb([128, 4, 528], F32, name="Tb")
    sg = P.sb([128, 4, 512], BF16, name="sg")
    pooled = P.sb([128, 4, 512], BF16, name="pooled")
    z = P.sb([128, KC, 512], BF16, name="z")
    xres = [P.sb([128, 512], F32) for _ in range(3)]
    xo_t = [P.sb([128, 512], F32) for _ in range(3)]
    mm = [P.ps([128, 512], F32) for _ in range(4)]
    psB = [P.ps([128, 4, 16], F32) for _ in range(2)]
    if final:
        xn = P.sb([128, KC, 512], F32, name="xn")
        fsq = [P.sb([128, 512], BF16) for _ in range(2)]
    mmi = [0]
    UK = [("U", i) for i in range(4)] + ["Uh"]

    def next_mm():
        i = mmi[0] % len(mm)
        mmi[0] += 1
        return mm[i], ("mm", i)

    def hk(hb, k):
        return (("hT", hb), k)

    def mixer(t):
        hb = t % 2
        h = hT[hb]
        for gi in range(4):
            w = 2 << gi
            pb = psB[gi % 2]
            pbk = ("psB", gi % 2)
            for sl in range(2):
                slab, skey = ws.next()
                for ml in range(2):
                    m4 = sl * 2 + ml
                    ps, pk = next_mm()
                    for k in range(KC):
                        P.pe(lambda e, ps=ps, slab=slab, k=k, ml=ml, h=h: e.matmul(
                            ps[:], lhsT=slab[:, k, ml * 128:(ml + 1) * 128], rhs=h[:, k, 0:512],
                            start=(k == 0), stop=(k == KC - 1)), reads=[skey, hk(hb, k)], writes=[pk])
                        P.pe(lambda e, pb=pb, slab=slab, k=k, ml=ml, h=h, m4=m4: e.matmul(
                            pb[:, m4, :], lhsT=slab[:, k, ml * 128:(ml + 1) * 128], rhs=h[:, k, 512:528],
                            start=(k == 0), stop=(k == KC - 1)), reads=[skey, hk(hb, k)], writes=[pbk])
                    P.act(lambda e, ps=ps, m4=m4: e.copy(out=U[:, m4, 0:512], in_=ps[:]),
                          reads=[pk], writes=[("U", m4)])
            P.dve(lambda e, pb=pb: e.tensor_copy(out=U[:, :, 512:528], in_=pb[:]), reads=[pbk], writes=["Uh"])
            if t == 0:
                P.dve(lambda e: e.tensor_scalar(out=U[:, :, 0:8], in0=U[:, :, 0:8], scalar1=hmask[:, 0:1],
                                                scalar2=None, op0=ALU.mult), reads=["hmask"] + UK, writes=UK)
            if t == NT - 1:
                P.dve(lambda e: e.tensor_scalar(out=U[:, :, 520:528], in0=U[:, :, 520:528], scalar1=hmask[:, 1:2],
                                                scalar2=None, op0=ALU.mult), reads=["hmask", "Uh"], writes=["Uh"])
            for sl in range(2):
                slab, skey = ws.next()
                for ml in range(2):
                    m4 = sl * 2 + ml
                    ps, pk = next_mm()
                    for k in range(KC):
                        P.pe(lambda e, ps=ps, slab=slab, k=k, ml=ml, h=h: e.matmul(
                            ps[:], lhsT=slab[:, k, ml * 128:(ml + 1) * 128], rhs=h[:, k, 8:520],
                            start=(k == 0), stop=(k == KC - 1)), reads=[skey, hk(hb, k)], writes=[pk])
                    P.act(lambda e, ps=ps, m4=m4: e.activation(out=sg[:, m4, :], in_=ps[:], func=AF.Silu),
                          reads=[pk], writes=[("sg", m4)])
            P.dve(lambda e: e.tensor_tensor(out=Ta[:, :, 1:528], in0=U[:, :, 0:527], in1=U[:, :, 1:528], op=ALU.add),
                  reads=UK, writes=["Ta"])
            S, skey_ = Ta, "Ta"
            if gi >= 1:
                P.dve(lambda e: e.tensor_tensor(out=Tb[:, :, 2:527], in0=Ta[:, :, 1:526], in1=Ta[:, :, 3:528],
                                                op=ALU.add), reads=["Ta"], writes=["Tb"])
                S, skey_ = Tb, "Tb"
            if gi >= 2:
                P.dve(lambda e: e.tensor_tensor(out=Ta[:, :, 4:525], in0=Tb[:, :, 2:523], in1=Tb[:, :, 6:527],
                                                op=ALU.add), reads=["Tb"], writes=["Ta"])
                S, skey_ = Ta, "Ta"
            if gi >= 3:
                P.dve(lambda e: e.tensor_tensor(out=Tb[:, :, 8:521], in0=Ta[:, :, 4:517], in1=Ta[:, :, 12:525],
                                                op=ALU.add), reads=["Ta"], writes=["Tb"])
                S, skey_ = Tb, "Tb"
            P.dve(lambda e, S=S, w=w: e.scalar_tensor_tensor(
                out=pooled[:], in0=S[:, :, 8:520], scalar=1.0 / w, in1=U[:, :, 8:520],
                op0=ALU.mult, op1=ALU.subtract), reads=[skey_] + UK, writes=["pooled"])
            if t == 0 or t == NT - 1:
                a, ic = (8, 0) if t == 0 else (512, 8)
                O, okey = (Ta, "Ta") if S is Tb else (Tb, "Tb")
                P.dve(lambda e, S=S, O=O, a=a, ic=ic, gi=gi: e.tensor_tensor(
                    out=O[:, :, 0:8], in0=S[:, :, a:a + 8],
                    in1=icnt[:, gi:gi + 1, ic:ic + 8].to_broadcast([128, 4, 8]), op=ALU.mult),
                    reads=[skey_, "icnt"], writes=[okey])
                P.dve(lambda e, O=O, a=a: e.tensor_tensor(
                    out=pooled[:, :, a - 8:a], in0=O[:, :, 0:8], in1=U[:, :, a:a + 8], op=ALU.subtract),
                    reads=[okey] + UK, writes=["pooled"])
            for m4 in range(4):
                ps, pk = next_mm()
                for k4 in range(4):
                    P.pe(lambda e, ps=ps, gi=gi, k4=k4, m4=m4: e.matmul(
                        ps[:], lhsT=wg[:, gi, k4, m4 * 128:(m4 + 1) * 128], rhs=pooled[:, k4, :],
                        start=(k4 == 0), stop=(k4 == 3)), reads=[("wg", gi), "pooled"], writes=[pk])
                mg = gi * 4 + m4
                P.dve(lambda e, ps=ps, mg=mg, m4=m4: e.scalar_tensor_tensor(
                    out=z[:, mg, :], in0=ps[:], scalar=pscale[:, mg:mg + 1], in1=sg[:, m4, :],
                    op0=ALU.mult, op1=ALU.mult), reads=[pk, "pscale", ("sg", m4)], writes=[("z", mg)])

    def outproj(t):
        c0 = 8 + TT * t

        def load_res(m):
            rb = m % len(xres)
            P.dma("sp", xres[rb][:], xT[m * 128:(m + 1) * 128, c0:c0 + 512], writes=[("xres", rb)])
        for m in range(min(len(xres), KC)):
            load_res(m)
        for sl in range(8):
            slab, skey = ws.next()
            for ml in range(2):
                m = sl * 2 + ml
                ps, pk = next_mm()
                for k in range(KC):
                    P.pe(lambda e, ps=ps, slab=slab, k=k, ml=ml: e.matmul(
                        ps[:], lhsT=slab[:, k, ml * 128:(ml + 1) * 128], rhs=z[:, k, :],
                        start=(k == 0), stop=(k == KC - 1)), reads=[skey, ("z", k)], writes=[pk])
                rb = m % len(xres)
                if not final:
                    ob = m % len(xo_t)
                    P.dve(lambda e, ps=ps, m=m, rb=rb, ob=ob: e.scalar_tensor_tensor(
                        out=xo_t[ob][:], in0=ps[:], scalar=C.gate(m), in1=xres[rb][:],
                        op0=ALU.mult, op1=ALU.add), reads=[pk, "mod", ("xres", rb)], writes=[("xo_t", ob)])
                    P.dma("pool", io["xo"][m * 128:(m + 1) * 128, TT * t:TT * (t + 1)], xo_t[ob][:],
                          reads=[("xo_t", ob)], writes=[("xo", t, m)])
                else:
                    P.dve(lambda e, ps=ps, m=m, rb=rb: e.scalar_tensor_tensor(
                        out=xn[:, m, :], in0=ps[:], scalar=C.gate(m), in1=xres[rb][:],
                        op0=ALU.mult, op1=ALU.add), reads=[pk, "mod", ("xres", rb)], writes=[("xn", m)])
                if m + len(xres) < KC:
                    load_res(m + len(xres))
        if final:
            ss = nb["ss_a"]
            for m in range(KC):
                fb = m % 2
                P.act(lambda e, m=m, fb=fb: e.activation(out=fsq[fb][:], in_=xn[:, m, :], func=AF.Square),
                      reads=[("xn", m)], writes=[("fsq", fb)])
                P.pe(lambda e, m=m, fb=fb: e.matmul(ss[:], lhsT=C.ones[:], rhs=fsq[fb][:],
                                                    start=(m == 0), stop=(m == KC - 1)),
                     reads=["ones", ("fsq", fb)], writes=["ss_a"])
            rstd = nb["rstd"]
            P.dve(lambda e: e.tensor_scalar(out=rstd[:, :512], in0=ss[:], scalar1=1.0 / D, scalar2=EPS,
                                            op0=ALU.mult, op1=ALU.add), reads=["ss_a"], writes=["rstd"])
            P.dve(lambda e: e.reciprocal(out=rstd[:, :512], in_=rstd[:, :512]), reads=["rstd"], writes=["rstd"])
            P.act(lambda e: e.activation(out=rstd[:, :512], in_=rstd[:, :512], func=AF.Sqrt),
                  reads=["rstd"], writes=["rstd"])
            for m in range(KC):
                ob = m % len(xo_t)
                P.dve(lambda e, m=m, ob=ob: e.scalar_tensor_tensor(
                    out=xo_t[ob][:], in0=xn[:, m, :], scalar=fg[:, m:m + 1], in1=rstd[:, :512],
                    op0=ALU.mult, op1=ALU.mult), reads=[("xn", m), "fg", "rstd"], writes=[("xo_t", ob)])
                P.dma("pool", io["xo"][m * 128:(m + 1) * 128, TT * t:TT * (t + 1)], xo_t[ob][:],
                      reads=[("xo_t", ob)], writes=[("xo", t, m)])

    def norm(t):
        emit_norm_mod(P, C, xT, TT * t, 528, hT[t % 2], ("hT", t % 2), nb)

    norm(0)
    for t in range(NT):
        mixer(t)
        if t + 1 < NT and not final:
            norm(t + 1)
        outproj(t)
        if t + 1 < NT and final:
            norm(t + 1)
    P.release(m0)
    return wb


AQ, AKV = 2048, 512
A_SCALE = 128 ** -0.5


def emit_outproj_phase(P, nc, C, ws, xT, xo, Zs, x_off, mm_ring, final_io=None):
    z = [P.sb([128, KC, 512], BF16) for _ in range(2)]
    xres = [P.sb([128, 512], F32) for _ in range(3)]
    xo_t = [P.sb([128, 512], F32) for _ in range(3)]
    mmi = [0]

    def next_mm():
        i = mmi[0] % len(mm_ring)
        mmi[0] += 1
        return mm_ring[i], ("mm", i)

    for t in range(NT):
        zb = t % 2
        for k in range(KC):
            P.dma("sp", z[zb][:, k, :], Zs[k * 128:(k + 1) * 128, t * TT:(t + 1) * TT],
                  reads=[("Zs", k, t)], writes=[("zc", zb, k)])
        c0 = x_off + TT * t

        def load_res(m):
            rb = m % len(xres)
            P.dma("sp", xres[rb][:], xT[m * 128:(m + 1) * 128, c0:c0 + 512], writes=[("xres", rb)])
        for m in range(len(xres)):
            load_res(m)
        for sl in range(8):
            slab, skey = ws.next()
            for ml in range(2):
                m = sl * 2 + ml
                ps, pk = next_mm()
                for k in range(KC):
                    P.pe(lambda e, ps=ps, slab=slab, k=k, ml=ml, zb=zb: e.matmul(
                        ps[:], lhsT=slab[:, k, ml * 128:(ml + 1) * 128], rhs=z[zb][:, k, :],
                        start=(k == 0), stop=(k == KC - 1)), reads=[skey, ("zc", zb, k)], writes=[pk])
                rb = m % len(xres)
                ob = m % len(xo_t)
                P.dve(lambda e, ps=ps, m=m, rb=rb, ob=ob: e.scalar_tensor_tensor(
                    out=xo_t[ob][:], in0=ps[:], scalar=C.gate(m), in1=xres[rb][:],
                    op0=ALU.mult, op1=ALU.add), reads=[pk, "mod", ("xres", rb)], writes=[("xo_t", ob)])
                P.dma("pool", xo[m * 128:(m + 1) * 128, TT * t:TT * (t + 1)], xo_t[ob][:],
                      reads=[("xo_t", ob)], writes=[("xo", t, m)])
                if m + len(xres) < KC:
                    load_res(m + len(xres))


def emit_attn_layer(P, nc, io, variant):
    xT = io["xT"]
    full = variant == "full"
    m0 = P.mark()
    wb_in = nc.dram_tensor("awb_in", [D, 5120], BF16).ap()
    convert_w(P, io["w_in"], wb_in, D, 5120, "acw_in")
    if full:
        wb_out = nc.dram_tensor("awb_out", [D, D], BF16).ap()
        convert_w(P, io["w_out"], wb_out, D, D, "acw_out")
        Qs = nc.dram_tensor("aQs", [16, 128, TC], BF16).ap()
        SG = nc.dram_tensor("aSG", [D, TC], BF16).ap()
        Zs = nc.dram_tensor("aZs", [D, TC], BF16).ap()
    KT_own, V_own = io["KT_own"], io["V_own"]
    P.release(m0)

    C = Common(P, io["mod"])
    gains = P.sb([128, 2], F32, name="gains")
    P.dma("sp", gains[:], io["gains"], writes=["gains"])
    rot32 = P.sb([128, 128], F32, name="rot32")
    P.dma("sp", rot32[:], io["rotT"], writes=["rot32"])
    rotb = P.sb([128, 128], BF16, name="rotb")
    P.dve(lambda e: e.tensor_copy(out=rotb[:], in_=rot32[:]), reads=["rot32"], writes=["rotb"])

    order = []
    if full:
        order += [slab_src(wb_in, 256 * j) for j in range(8)]
    order += [slab_src(wb_in, 2048 + 256 * j) for j in range(4)]
    if full:
        order += [slab_src(wb_in, 3072 + 256 * j) for j in range(8)]
    ws = WStream(P, order * NT, ring=4)
    nb = norm_bufs(P, 512)
    hT = [P.sb([128, KC, 512], BF16, name=f"ahT{i}") for i in range(2)]
    cs = [P.sb([128, 512], F32) for _ in range(2)]
    sn = [P.sb([128, 512], F32) for _ in range(2)]
    qg_t = [P.sb([128, 512], BF16) for _ in range(2)]
    sq_t = [P.sb([128, 512], BF16) for _ in range(2)]
    rs_t = [P.sb([128, 512], F32) for _ in range(2)]
    t1 = [P.sb([128, 512], F32) for _ in range(2)]
    t2 = [P.sb([128, 512], F32) for _ in range(2)]
    qo = [P.sb([128, 512], BF16) for _ in range(3)]
    vbuf = [P.sb([128, 4, 512], BF16) for _ in range(2)]
    sgt = [P.sb([128, 512], BF16) for _ in range(3)]
    mm = [P.ps([128, 512], F32) for _ in range(3)]
    ssq = [P.ps([128, 512], F32) for _ in range(2)]
    rotp = [P.ps([128, 512], F32) for _ in range(2)]
    cnt = {"mm": 0, "hd": 0, "qo": 0, "sg": 0}

    def next_mm():
        i = cnt["mm"] % len(mm)
        cnt["mm"] += 1
        return mm[i], ("mm", i)

    def qk_head(t, hb, slab, skey, ml, gcol, dst, dkey):
        h = hT[hb]
        ps, pk = next_mm()
        for k in range(KC):
            P.pe(lambda e, ps=ps, slab=slab, k=k, ml=ml, h=h: e.matmul(
                ps[:], lhsT=slab[:, k, ml * 128:(ml + 1) * 128], rhs=h[:, k, :],
                start=(k == 0), stop=(k == KC - 1)), reads=[skey, (("ahT", hb), k)], writes=[pk])
        i2 = cnt["hd"] % 2
        cnt["hd"] += 1
        P.act(lambda e, ps=ps, i2=i2: e.activation(out=qg_t[i2][:], in_=ps[:], func=AF.Copy,
                                                   scale=gains[:, gcol:gcol + 1]),
              reads=[pk, "gains"], writes=[("qg", i2)])
        P.act(lambda e, ps=ps, i2=i2: e.activation(out=sq_t[i2][:], in_=ps[:], func=AF.Square),
              reads=[pk], writes=[("sq_t", i2)])
        P.pe(lambda e, i2=i2: e.matmul(ssq[i2][:], lhsT=C.ones[:], rhs=sq_t[i2][:], start=True, stop=True),
             reads=["ones", ("sq_t", i2)], writes=[("ssq", i2)])
        P.pe(lambda e, i2=i2: e.matmul(rotp[i2][:], lhsT=rotb[:], rhs=qg_t[i2][:], start=True, stop=True),
             reads=["rotb", ("qg", i2)], writes=[("rotp", i2)])
        P.act(lambda e, i2=i2: e.activation(out=rs_t[i2][:], in_=ssq[i2][:], func=AF.Ln, scale=1.0 / 128, bias=EPS),
              reads=[("ssq", i2)], writes=[("rs_t", i2)])
        P.act(lambda e, i2=i2: e.activation(out=rs_t[i2][:], in_=rs_t[i2][:], func=AF.Exp, scale=-0.5),
              reads=[("rs_t", i2)], writes=[("rs_t", i2)])
        cb = t % 2
        P.pool(lambda e, i2=i2, cb=cb: e.tensor_tensor(out=t1[i2][:], in0=qg_t[i2][:], in1=cs[cb][:], op=ALU.mult),
               reads=[("qg", i2), ("cs", cb)], writes=[("t1", i2)])
        P.dve(lambda e, i2=i2, cb=cb: e.tensor_tensor(out=t2[i2][:], in0=rotp[i2][:], in1=sn[cb][:], op=ALU.mult),
              reads=[("rotp", i2), ("sn", cb)], writes=[("t2", i2)])
        P.pool(lambda e, i2=i2: e.tensor_tensor(out=t1[i2][:], in0=t1[i2][:], in1=t2[i2][:], op=ALU.add),
               reads=[("t1", i2), ("t2", i2)], writes=[("t1", i2)])
        i3 = cnt["qo"] % 3
        cnt["qo"] += 1
        P.dve(lambda e, i2=i2, i3=i3: e.tensor_tensor(out=qo[i3][:], in0=t1[i2][:], in1=rs_t[i2][:], op=ALU.mult),
              reads=[("t1", i2), ("rs_t", i2)], writes=[("qo", i3)])
        P.dma("pool", dst, qo[i3][:], reads=[("qo", i3)], writes=[dkey])

    def projections(t):
        hb = t % 2
        h = hT[hb]
        cb = t % 2
        P.dma("sp", cs[cb][:], io["cos"][:, t * TT:(t + 1) * TT], writes=[("cs", cb)])
        P.dma("sp", sn[cb][:], io["sin"][:, t * TT:(t + 1) * TT], writes=[("sn", cb)])
        if full:
            for sl in range(8):
                slab, skey = ws.next()
                for ml in range(2):
                    hd = sl * 2 + ml
                    qk_head(t, hb, slab, skey, ml, 0, Qs[hd][:, t * TT:(t + 1) * TT], ("Qs", hd, t))
        for sl in range(2):
            slab, skey = ws.next()
            for ml in range(2):
                kh = sl * 2 + ml
                qk_head(t, hb, slab, skey, ml, 1, KT_own[kh][:, t * TT:(t + 1) * TT], ("KT", kh, t))
        vb = t % 2
        for sl in range(2):
            slab, skey = ws.next()
            for j in range(4):
                ps, pk = next_mm()
                for k in range(KC):
                    P.pe(lambda e, ps=ps, slab=slab, k=k, j=j, h=h: e.matmul(
                        ps[:, 0:256], lhsT=h[:, k, j * 128:(j + 1) * 128], rhs=slab[:, k, :],
                        start=(k == 0), stop=(k == KC - 1)), reads=[skey, (("ahT", hb), k)], writes=[pk])
                P.act(lambda e, ps=ps, j=j, sl=sl, vb=vb: e.copy(out=vbuf[vb][:, j, sl * 256:(sl + 1) * 256],
                                                                 in_=ps[:, 0:256]),
                      reads=[pk], writes=[("vbuf", vb, j, sl)])
        P.dma("pool", V_own[t * TT:(t + 1) * TT, :].rearrange("(j p) n -> p j n", p=128), vbuf[vb][:],
              reads=[("vbuf", vb, j, sl) for j in range(4) for sl in range(2)], writes=[("V", t)])
        if full:
            for sl in range(8):
                slab, skey = ws.next()
                for ml in range(2):
                    m = sl * 2 + ml
                    ps, pk = next_mm()
                    for k in range(KC):
                        P.pe(lambda e, ps=ps, slab=slab, k=k, ml=ml, h=h: e.matmul(
                            ps[:], lhsT=slab[:, k, ml * 128:(ml + 1) * 128], rhs=h[:, k, :],
                            start=(k == 0), stop=(k == KC - 1)), reads=[skey, (("ahT", hb), k)], writes=[pk])
                    i3 = cnt["sg"] % 3
                    cnt["sg"] += 1
                    P.act(lambda e, ps=ps, i3=i3: e.activation(out=sgt[i3][:], in_=ps[:], func=AF.Silu),
                          reads=[pk], writes=[("sgt", i3)])
                    P.dma("pool", SG[m * 128:(m + 1) * 128, t * TT:(t + 1) * TT], sgt[i3][:],
                          reads=[("sgt", i3)], writes=[("SG", m, t)])

    def norm(t):
        emit_norm_mod(P, C, xT, TT * t, 512, hT[t % 2], ("ahT", t % 2), nb)

    norm(0)
    for t in range(NT):
        if t + 1 < NT:
            norm(t + 1)
        projections(t)
    if not full:
        return
    P.release(m0)

    C = Common(P, io["mod"])
    ones32 = P.sb([128, 128], F32, name="ones32")
    P.dve(lambda e: e.memset(ones32[:], 1.0), writes=["ones32"])
    kt = [P.sb([128, 2 * TC], BF16, name=f"kt{i}") for i in range(2)]
    vt = [P.sb([128, 64, 128], BF16, name=f"vt{i}") for i in range(2)]
    qt = [P.sb([128, 512], BF16) for _ in range(3)]
    sgq = [P.sb([128, 512], BF16) for _ in range(3)]
    pt = [P.sb([128, 512], BF16) for _ in range(4)]
    accD = [P.sb([128, 512], F32) for _ in range(2)]
    accP = [P.sb([128, 512], F32) for _ in range(2)]
    rinv = [P.sb([128, 512], F32) for _ in range(2)]
    zt = [P.sb([128, 512], BF16) for _ in range(2)]
    st = [P.ps([128, 512], F32) for _ in range(3)]
    ot = [P.ps([128, 512], F32) for _ in range(2)]
    rsp = [P.ps([128, 512], F32) for _ in range(2)]
    KT_all, V_all = io["KT_all"], io["V_all"]
    it = 0
    sti = 0
    pti = 0
    for kvh in range(4):
        kb = kvh % 2
        for r in range(2):
            P.dma("sp", kt[kb][:, r * TC:(r + 1) * TC], KT_all[r, kvh], writes=[("kt", kb, r)])
            P.dma("sp", vt[kb][:, r * 32:(r + 1) * 32, :],
                  V_all[r][:, kvh * 128:(kvh + 1) * 128].rearrange("(c p) d -> p c d", p=128),
                  writes=[("vt", kb, r)])
        ktk = [("kt", kb, 0), ("kt", kb, 1)]
        vtk = [("vt", kb, 0), ("vt", kb, 1)]
        for gq_ in range(4):
            hd = kvh * 4 + gq_
            for t in range(NT):
                ab = it % 2
                q3 = it % 3
                it += 1
                P.dma("sp", qt[q3][:], Qs[hd][:, t * TT:(t + 1) * TT], reads=[("Qs", hd, t)], writes=[("qt", q3)])
                P.dma("sp", sgq[q3][:], SG[hd * 128:(hd + 1) * 128, t * TT:(t + 1) * TT],
                      reads=[("SG", hd, t)], writes=[("sgq", q3)])
                for sc in range(64):
                    s3 = sti % 3
                    sti += 1
                    p4 = pti % 4
                    pti += 1
                    P.pe(lambda e, s3=s3, kb=kb, sc=sc, q3=q3: e.matmul(
                        st[s3][:], lhsT=kt[kb][:, sc * 128:(sc + 1) * 128], rhs=qt[q3][:], start=True, stop=True),
                        reads=[ktk[sc // 32], ("qt", q3)], writes=[("st", s3)])
                    P.act(lambda e, s3=s3, p4=p4: e.activation(out=pt[p4][:], in_=st[s3][:], func=AF.Exp, scale=A_SCALE),
                          reads=[("st", s3)], writes=[("pt", p4)])
                    P.pe(lambda e, ab=ab, kb=kb, sc=sc, p4=p4: e.matmul(
                        ot[ab][:], lhsT=vt[kb][:, sc, :], rhs=pt[p4][:], start=(sc == 0), stop=(sc == 63)),
                        reads=[vtk[sc // 32], ("pt", p4)], writes=[("ot", ab)])
                    eng, acc, akey = ("dve", accD[ab], ("accD", ab)) if sc % 2 == 0 else ("pool", accP[ab], ("accP", ab))
                    if sc < 2:
                        P.add(eng, lambda e, acc=acc, p4=p4: e.tensor_copy(out=acc[:], in_=pt[p4][:]),
                              reads=[("pt", p4)], writes=[akey])
                    else:
                        P.add(eng, lambda e, acc=acc, p4=p4: e.tensor_tensor(out=acc[:], in0=acc[:], in1=pt[p4][:], op=ALU.add),
                              reads=[("pt", p4), akey], writes=[akey])
                P.pe(lambda e, ab=ab: e.matmul(rsp[ab][:], lhsT=ones32[:], rhs=accD[ab][:], start=True, stop=False),
                     reads=["ones32", ("accD", ab)], writes=[("rsp", ab)])
                P.pe(lambda e, ab=ab: e.matmul(rsp[ab][:], lhsT=ones32[:], rhs=accP[ab][:], start=False, stop=True),
                     reads=["ones32", ("accP", ab)], writes=[("rsp", ab)])
                P.dve(lambda e, ab=ab: e.reciprocal(out=rinv[ab][:], in_=rsp[ab][:]), reads=[("rsp", ab)], writes=[("rinv", ab)])
                P.pool(lambda e, ab=ab, q3=q3: e.tensor_tensor(out=rinv[ab][:], in0=rinv[ab][:], in1=sgq[q3][:], op=ALU.mult),
                       reads=[("rinv", ab), ("sgq", q3)], writes=[("rinv", ab)])
                P.dve(lambda e, ab=ab: e.tensor_tensor(out=zt[ab][:], in0=ot[ab][:], in1=rinv[ab][:], op=ALU.mult),
                      reads=[("ot", ab), ("rinv", ab)], writes=[("zt", ab)])
                P.dma("pool", Zs[hd * 128:(hd + 1) * 128, t * TT:(t + 1) * TT], zt[ab][:],
                      reads=[("zt", ab)], writes=[("Zs", hd, t)])
    P.release(m0)

    C = Common(P, io["mod"])
    ws2 = WStream(P, [slab_src(wb_out, 256 * j) for j in range(8)] * NT, ring=4, name="wsC")
    mmC = [P.ps([128, 512], F32) for _ in range(4)]
    emit_outproj_phase(P, nc, C, ws2, xT, io["xo"], Zs, 0, mmC)


GK = 1024
GV = 2048
NCH = TC // 128
LN_QS = -2.772588722239781


def emit_gla_layer(P, nc, io, variant, pfx="", wb=None, scr=None):
    xT = io["xT"]
    full = variant in ("full", "sweep3")
    mtop = P.mark()
    if variant == "sweep3":
        m0 = mtop
        wb_in, wb_out = wb
        QC, OT, SG = scr["QC"], scr["OT"], scr["SG"]
    else:
        el = {d: P.sb([128, 8, NCH], F32, name=f"el_{d}") for d in "fb"}
        m0 = P.mark()
        if wb is None:
            wb_in = nc.dram_tensor(pfx + "gwb_in", [D, 6144], BF16).ap()
            convert_w(P, io["w_in"], wb_in, D, 6144, "gcw_in")
            wb_out = nc.dram_tensor(pfx + "gwb_out", [D, D], BF16).ap()
            convert_w(P, io["w_out"], wb_out, D, D, "gcw_out")
            wb = (wb_in, wb_out)
        wb_in, wb_out = wb
        QT = {d: nc.dram_tensor(pfx + f"gQ{d}", [GK, TC], BF16).ap() for d in "fb"}
        KT = {d: nc.dram_tensor(pfx + f"gK{d}", [GK, TC], BF16).ap() for d in "fb"}
        KH = {d: nc.dram_tensor(pfx + f"gKH{d}", [TC, GK], BF16).ap() for d in "fb"}
        QC = {d: nc.dram_tensor(pfx + f"gQC{d}", [GK, TC], BF16).ap() for d in "fb"}
        OT = {d: nc.dram_tensor(pfx + f"gO{d}", [GV, TC], F32).ap() for d in "fb"}
        Vs = nc.dram_tensor(pfx + "gVs", [TC, GV], BF16).ap()
        SG = nc.dram_tensor(pfx + "gSG", [D, TC], BF16).ap()
        P.release(m0)
    if variant != "sweep3":

        C = Common(P, io["mod"])
        tri = {}
        for d in "fb":
            tri[d] = P.sb([128, 128], F32, name=f"tri{d}")
            P.dma("sp", tri[d][:], io["tri" + d], writes=[("tri", d)])
        id32 = P.sb([128, 128], F32, name="id32")
        P.dma("sp", id32[:], io["ident"], writes=["id32"])
        ident = P.sb([128, 128], BF16, name="ident")
        P.dve(lambda e: e.tensor_copy(out=ident[:], in_=id32[:]), reads=["id32"], writes=["ident"])
        w1s = P.sb([128, KC, 64], F32, name="w1s")
        P.dma("sp", w1s[:], io["w1cat"].rearrange("(k p) n -> p k n", p=128), writes=["w1s"])
        w1b = P.sb([128, KC, 64], BF16, name="w1b")
        P.dve(lambda e: e.tensor_copy(out=w1b[:], in_=w1s[:]), reads=["w1s"], writes=["w1b"])
        w2s = P.sb([64, GK], F32, name="w2s")
        P.dma("sp", w2s[:], io["w2aug"], writes=["w2s"])
        w2b = P.sb([64, GK], BF16, name="w2b")
        P.dve(lambda e: e.tensor_copy(out=w2b[:], in_=w2s[:]), reads=["w2s"], writes=["w2b"])

        order = [slab_src(wb_in, 256 * j) for j in range(24)]
        ws = WStream(P, order * NT, ring=3)
        nb = norm_bufs(P, 512)
        hT = [P.sb([128, KC, 512], BF16, name=f"ghT{i}") for i in range(2)]
        rT = [P.sb([64, 512], BF16, name=f"rT{i}") for i in range(2)]
        for i in range(2):
            P.dve(lambda e, i=i: e.memset(rT[i][:], 1.0), writes=[("rT", i)])
        qT_sb = P.sb([128, 8, 512], F32, name="qT_sb")
        kT_sb = P.sb([128, 8, 512], F32, name="kT_sb")
        e1 = P.sb([128, GK], F32, name="e1")
        la = {d: P.sb([128, GK], F32, name=f"la{d}") for d in "fb"}
        Eq = P.sb([128, 8, 128], F32, name="Eq")
        Einv = P.sb([128, 8, 128], F32, name="Einv")
        ktmp = P.sb([128, 8, 128], F32, name="ktmp")
        qtl = [P.sb([128, 8, 128], BF16) for _ in range(2)]
        ktl = [P.sb([128, 8, 128], BF16) for _ in range(2)]
        khT = [P.sb([128, 8, 128], BF16) for _ in range(2)]
        khat = [P.sb([128, GK], BF16) for _ in range(2)]
        vbuf = P.sb([128, 4, GV], BF16, name="gvbuf")
        sgt = [P.sb([128, 512], BF16) for _ in range(3)]
        mm = [P.ps([128, 512], F32) for _ in range(2)]
        rps = P.ps([64, 512], F32, name="rps")
        zps = P.ps([128, 512], F32, name="zps")
        bps = [P.ps([128, 4, 128], F32) for _ in range(2)]
        trp = P.ps([128, GK], BF16, name="trp")
        cnt = {"mm": 0, "sg": 0, "cd": 0, "bp": 0}

        def next_mm():
            i = cnt["mm"] % len(mm)
            cnt["mm"] += 1
            return mm[i], ("mm", i)

        def proj_fm(t, hb, dst_sb, dkey):
            h = hT[hb]
            for sl in range(4):
                slab, skey = ws.next()
                for ml in range(2):
                    m = sl * 2 + ml
                    ps, pk = next_mm()
                    for k in range(KC):
                        P.pe(lambda e, ps=ps, slab=slab, k=k, ml=ml, h=h: e.matmul(
                            ps[:], lhsT=slab[:, k, ml * 128:(ml + 1) * 128], rhs=h[:, k, :],
                            start=(k == 0), stop=(k == KC - 1)), reads=[skey, (("ghT", hb), k)], writes=[pk])
                    P.act(lambda e, ps=ps, m=m: e.copy(out=dst_sb[:, m, :], in_=ps[:]), reads=[pk], writes=[(dkey, m)])

        def chunk_dir(t, cc, d, rb):
            c = 4 * t + cc
            base = 0 if d == "f" else 32
            last = 127 if d == "f" else 0
            tok = slice(cc * 128, (cc + 1) * 128)
            for hh in range(2):
                P.pe(lambda e, hh=hh: e.matmul(zps[:], lhsT=rT[rb][base:base + 32, tok],
                                               rhs=w2b[base:base + 32, hh * 512:(hh + 1) * 512], start=True, stop=True),
                     reads=[("rT", rb), "w2b"], writes=["zps"])
                P.act(lambda e, hh=hh: e.activation(out=e1[:, hh * 512:(hh + 1) * 512], in_=zps[:], func=AF.Exp, scale=-1.0),
                      reads=["zps"], writes=[("e1", hh)])
                P.act(lambda e, hh=hh: e.activation(out=la[d][:, hh * 512:(hh + 1) * 512], in_=e1[:, hh * 512:(hh + 1) * 512],
                                                    func=AF.Ln, bias=1.0, scale=1.0),
                      reads=[("e1", hh)], writes=[("la", d, hh)])
            for mh in range(2):
                bi = cnt["bp"] % 2
                cnt["bp"] += 1
                bp = bps[bi]
                for m4 in range(4):
                    m = mh * 4 + m4
                    P.pe(lambda e, bp=bp, m4=m4, m=m: e.matmul(bp[:, m4, :], lhsT=la[d][:, m * 128:(m + 1) * 128],
                                                               rhs=tri[d][:], start=True, stop=True),
                         reads=[("la", d, m // 4), ("tri", d)], writes=[("bps", bi)])
                ms = slice(mh * 4, mh * 4 + 4)
                P.act(lambda e, bp=bp, ms=ms: e.activation(out=Eq[:, ms, :], in_=bp[:], func=AF.Exp, bias=LN_QS, scale=1.0),
                      reads=[("bps", bi)], writes=[("Eq", mh)])
                P.act(lambda e, bp=bp, ms=ms: e.activation(out=Einv[:, ms, :], in_=bp[:], func=AF.Exp, scale=-1.0),
                      reads=[("bps", bi)], writes=[("Einv", mh)])
                P.act(lambda e, bp=bp, ms=ms: e.activation(out=el[d][:, ms, c:c + 1], in_=bp[:, :, last:last + 1], func=AF.Exp),
                      reads=[("bps", bi)], writes=[("el", d, c, mh)])
            i2 = cnt["cd"] % 2
            cnt["cd"] += 1
            P.dve(lambda e, i2=i2: e.tensor_tensor(out=qtl[i2][:], in0=qT_sb[:, :, tok], in1=Eq[:], op=ALU.mult),
                  reads=[("qT", m) for m in range(8)] + [("Eq", 0), ("Eq", 1)], writes=[("qtl", i2)])
            P.dve(lambda e: e.tensor_tensor(out=ktmp[:], in0=kT_sb[:, :, tok], in1=Einv[:], op=ALU.mult),
                  reads=[("kT", m) for m in range(8)] + [("Einv", 0), ("Einv", 1)], writes=["ktmp"])
            P.pool(lambda e, i2=i2: e.tensor_copy(out=ktl[i2][:], in_=ktmp[:]), reads=["ktmp"], writes=[("ktl", i2)])
            P.pool(lambda e, i2=i2: e.tensor_tensor(out=khT[i2][:], in0=ktmp[:],
                                                    in1=el[d][:, :, c:c + 1].to_broadcast([128, 8, 128]), op=ALU.mult),
                   reads=["ktmp", ("el", d, c, 0), ("el", d, c, 1)], writes=[("khT", i2)])
            for m in range(8):
                P.pe(lambda e, m=m, i2=i2: e.transpose(trp[:, m * 128:(m + 1) * 128], khT[i2][:, m, :], ident[:]),
                     reads=[("khT", i2), "ident"], writes=["trp"])
            P.act(lambda e, i2=i2: e.copy(out=khat[i2][:], in_=trp[:]), reads=["trp"], writes=[("khat", i2)])
            csl = slice(c * 128, (c + 1) * 128)
            P.dma("pool", QT[d][:, csl].rearrange("(m p) t -> p m t", p=128), qtl[i2][:],
                  reads=[("qtl", i2)], writes=[("QT", d, c)])
            P.dma("pool", KT[d][:, csl].rearrange("(m p) t -> p m t", p=128), ktl[i2][:],
                  reads=[("ktl", i2)], writes=[("KT", d, c)])
            P.dma("pool", KH[d][csl, :], khat[i2][:], reads=[("khat", i2)], writes=[("KH", d, c)])

        def sweep1_tile(t):
            hb = t % 2
            h = hT[hb]
            rb = t % 2
            for k in range(KC):
                P.pe(lambda e, k=k, h=h: e.matmul(rps[:], lhsT=w1b[:, k, :], rhs=h[:, k, :], start=(k == 0), stop=(k == KC - 1)),
                     reads=["w1b", (("ghT", hb), k)], writes=["rps"])
            P.act(lambda e: e.copy(out=rT[rb][0:16, :], in_=rps[0:16, :]), reads=["rps"], writes=[("rT", rb)])
            P.act(lambda e: e.copy(out=rT[rb][32:48, :], in_=rps[32:48, :]), reads=["rps"], writes=[("rT", rb)])
            proj_fm(t, hb, qT_sb, "qT")
            proj_fm(t, hb, kT_sb, "kT")

            def v_group(sl):
                slab, skey = ws.next()
                for j in range(4):
                    ps, pk = next_mm()
                    for k in range(KC):
                        P.pe(lambda e, ps=ps, slab=slab, k=k, j=j, h=h: e.matmul(
                            ps[:, 0:256], lhsT=h[:, k, j * 128:(j + 1) * 128], rhs=slab[:, k, :],
                            start=(k == 0), stop=(k == KC - 1)), reads=[skey, (("ghT", hb), k)], writes=[pk])
                    eng = "act" if (sl + j) % 2 == 0 else "dve"
                    if eng == "act":
                        P.act(lambda e, ps=ps, j=j, sl=sl: e.copy(out=vbuf[:, j, sl * 256:(sl + 1) * 256], in_=ps[:, 0:256]),
                              reads=[pk], writes=[("gvbuf", j, sl)])
                    else:
                        P.dve(lambda e, ps=ps, j=j, sl=sl: e.tensor_copy(out=vbuf[:, j, sl * 256:(sl + 1) * 256], in_=ps[:, 0:256]),
                              reads=[pk], writes=[("gvbuf", j, sl)])
                if sl == 7:
                    P.dma("pool", Vs[t * TT:(t + 1) * TT, :].rearrange("(j p) n -> p j n", p=128), vbuf[:],
                          reads=[("gvbuf", j, s_) for j in range(4) for s_ in range(8)], writes=[("Vs", t)])

            def g_group(sl):
                slab, skey = ws.next()
                for ml in range(2):
                    m = sl * 2 + ml
                    ps, pk = next_mm()
                    for k in range(KC):
                        P.pe(lambda e, ps=ps, slab=slab, k=k, ml=ml, h=h: e.matmul(
                            ps[:], lhsT=slab[:, k, ml * 128:(ml + 1) * 128], rhs=h[:, k, :],
                            start=(k == 0), stop=(k == KC - 1)), reads=[skey, (("ghT", hb), k)], writes=[pk])
                    i3 = cnt["sg"] % 3
                    cnt["sg"] += 1
                    P.act(lambda e, ps=ps, i3=i3: e.activation(out=sgt[i3][:], in_=ps[:], func=AF.Silu),
                          reads=[pk], writes=[("sgt", i3)])
                    P.dma("pool", SG[m * 128:(m + 1) * 128, t * TT:(t + 1) * TT], sgt[i3][:],
                          reads=[("sgt", i3)], writes=[("SG", m, t)])

            cds = [(cc, d) for cc in range(4) for d in "fb"]
            for gi in range(16):
                if gi % 2 == 0:
                    chunk_dir(t, cds[gi // 2][0], cds[gi // 2][1], rb)
                if gi < 8:
                    v_group(gi)
                else:
                    g_group(gi - 8)

        def norm(t):
            emit_norm_mod(P, C, xT, TT * t, 512, hT[t % 2], ("ghT", t % 2), nb)

        norm(0)
        for t in range(NT):
            if t + 1 < NT:
                norm(t + 1)
            sweep1_tile(t)
        P.release(m0)

        msk = {}
        for d in "fb":
            msk[d] = P.sb([128, 128], F32, name=f"msk{d}")
            P.dma("sp", msk[d][:], io["mask" + d], writes=[("msk", d)])
        S32 = {d: P.sb([128, 8, 512], F32, name=f"S32{d}") for d in "fb"}
        Sb = {d: P.sb([128, 8, 512], BF16, name=f"Sb{d}") for d in "fb"}
        ecum = {d: P.sb([128, 8], F32, name=f"ecum{d}") for d in "fb"}
        for d in "fb":
            P.dve(lambda e, d=d: e.memset(S32[d][:], 0.0), writes=[("S32", d, i) for i in range(8)])
            P.pool(lambda e, d=d: e.memset(Sb[d][:], 0.0), writes=[("Sb", d, i) for i in range(8)])
            P.dve(lambda e, d=d: e.memset(ecum[d][:], 1.0), writes=[("ecum", d)])
        qt = {d: [P.sb([128, 8, 128], BF16) for _ in range(2)] for d in "fb"}
        kt = {d: [P.sb([128, 8, 128], BF16) for _ in range(2)] for d in "fb"}
        kh = {d: [P.sb([128, GK], BF16) for _ in range(2)] for d in "fb"}
        vv = {d: [P.sb([128, GV], BF16) for _ in range(2)] for d in "fb"}
        Am = {d: P.sb([128, 4, 128], BF16, name=f"Am{d}") for d in "fb"}
        qc = {d: P.sb([128, 8, 128], BF16, name=f"qc{d}") for d in "fb"}
        osb = {d: P.sb([128, 16, 128], F32, name=f"osb{d}") for d in "fb"}
        aps = P.ps([128, 4, 128], F32, name="aps")
        ops_ = P.ps([128, 16, 128], F32, name="ops")
        sps = [P.ps([128, 512], F32) for _ in range(3)]
        spi = 0
        for step in range(NCH):
            for d in "fb":
                c = step if d == "f" else NCH - 1 - step
                b2 = step % 2
                csl = slice(c * 128, (c + 1) * 128)
                P.dma("sp", qt[d][b2][:], QT[d][:, csl].rearrange("(m p) t -> p m t", p=128),
                      reads=[("QT", d, c)], writes=[("qt", d, b2)])
                P.dma("sp", kt[d][b2][:], KT[d][:, csl].rearrange("(m p) t -> p m t", p=128),
                      reads=[("KT", d, c)], writes=[("kt", d, b2)])
                P.dma("sp", kh[d][b2][:], KH[d][csl, :], reads=[("KH", d, c)], writes=[("kh", d, b2)])
                P.dma("sp", vv[d][b2][:], Vs[csl, :], reads=[("Vs", c // 4)], writes=[("vv", d, b2)])
                for hd in range(4):
                    for dc in range(2):
                        i8 = hd * 2 + dc
                        P.pe(lambda e, hd=hd, dc=dc, i8=i8, d=d, b2=b2: e.matmul(
                            aps[:, hd, :], lhsT=kt[d][b2][:, i8, :], rhs=qt[d][b2][:, i8, :], start=(dc == 0), stop=(dc == 1)),
                            reads=[("kt", d, b2), ("qt", d, b2)], writes=["aps"])
                P.dve(lambda e, d=d: e.tensor_tensor(out=Am[d][:], in0=aps[:],
                                                     in1=msk[d][:, :].unsqueeze(1).to_broadcast([128, 4, 128]), op=ALU.mult),
                      reads=["aps", ("msk", d)], writes=[("Am", d)])
                for hd in range(4):
                    for ec in range(4):
                        o16 = hd * 4 + ec
                        for dc in range(2):
                            i8 = hd * 2 + dc
                            P.pe(lambda e, o16=o16, i8=i8, ec=ec, dc=dc, d=d, b2=b2: e.matmul(
                                ops_[:, o16, :], lhsT=Sb[d][:, i8, ec * 128:(ec + 1) * 128], rhs=qt[d][b2][:, i8, :],
                                start=(dc == 0), stop=False), reads=[("Sb", d, i8), ("qt", d, b2)], writes=["ops"])
                        P.pe(lambda e, o16=o16, hd=hd, ec=ec, d=d, b2=b2: e.matmul(
                            ops_[:, o16, :], lhsT=vv[d][b2][:, hd * 512 + ec * 128:hd * 512 + (ec + 1) * 128], rhs=Am[d][:, hd, :],
                            start=False, stop=True), reads=[("vv", d, b2), ("Am", d)], writes=["ops"])
                for q4 in range(4):
                    if q4 % 2 == 0:
                        P.act(lambda e, d=d, q4=q4: e.copy(out=osb[d][:, q4 * 4:(q4 + 1) * 4, :], in_=ops_[:, q4 * 4:(q4 + 1) * 4, :]),
                              reads=["ops"], writes=[("osb", d)])
                    else:
                        P.dve(lambda e, d=d, q4=q4: e.tensor_copy(out=osb[d][:, q4 * 4:(q4 + 1) * 4, :], in_=ops_[:, q4 * 4:(q4 + 1) * 4, :]),
                              reads=["ops"], writes=[("osb", d)])
                P.dma("pool", OT[d][:, csl].rearrange("(m p) t -> p m t", p=128), osb[d][:],
                      reads=[("osb", d)], writes=[("OT", d, c)])
                P.pool(lambda e, d=d, b2=b2: e.tensor_tensor(out=qc[d][:], in0=qt[d][b2][:],
                                                             in1=ecum[d][:, :].unsqueeze(2).to_broadcast([128, 8, 128]), op=ALU.mult),
                       reads=[("qt", d, b2), ("ecum", d)], writes=[("qc", d)])
                P.dma("pool", QC[d][:, csl].rearrange("(m p) t -> p m t", p=128), qc[d][:],
                      reads=[("qc", d)], writes=[("QC", d, c)])
                P.pool(lambda e, d=d, c=c: e.tensor_tensor(out=ecum[d][:], in0=ecum[d][:], in1=el[d][:, :, c], op=ALU.mult),
                       reads=[("ecum", d), ("el", d, c, 0), ("el", d, c, 1)], writes=[("ecum", d)])
                for i8 in range(8):
                    hd = i8 // 2
                    s3 = spi % 3
                    spi += 1
                    P.pe(lambda e, s3=s3, i8=i8, hd=hd, d=d, b2=b2: e.matmul(
                        sps[s3][:], lhsT=kh[d][b2][:, i8 * 128:(i8 + 1) * 128], rhs=vv[d][b2][:, hd * 512:(hd + 1) * 512],
                        start=True, stop=True), reads=[("kh", d, b2), ("vv", d, b2)], writes=[("sps", s3)])
                    P.dve(lambda e, s3=s3, i8=i8, d=d, c=c: e.scalar_tensor_tensor(
                        out=S32[d][:, i8, :], in0=S32[d][:, i8, :], scalar=el[d][:, i8, c:c + 1], in1=sps[s3][:],
                        op0=ALU.mult, op1=ALU.add), reads=[("sps", s3), ("S32", d, i8), ("el", d, c, i8 // 4)],
                        writes=[("S32", d, i8)])
                    P.act(lambda e, i8=i8, d=d: e.copy(out=Sb[d][:, i8, :], in_=S32[d][:, i8, :]),
                          reads=[("S32", d, i8)], writes=[("Sb", d, i8)])
        if "S_end_f" in io:
            for d in "fb":
                P.dma("pool", io["S_end_" + d], S32[d][:], reads=[("S32", d, i) for i in range(8)], writes=[("S_end", d)])
    if not full:
        P.release(mtop)
        return {"QC": QC, "OT": OT, "SG": SG, "wb": wb}
    P.release(m0)

    C = Common(P, io["mod"])
    ng = P.sb([128, 4], F32, name="ng")
    P.dma("sp", ng[:], io["ng"], writes=["ng"])
    Sin = {}
    s32 = P.sb([128, 8, 512], F32, name="Sin32")
    hm = P.sb([128, 2], F32, name="ghm")
    if "hmask" in io:
        P.dma("sp", hm[:], io["hmask"], writes=["ghm"])
    else:
        P.dve(lambda e: e.memset(hm[:], 1.0), writes=["ghm"])
    for di, d in enumerate("fb"):
        P.dma("sp", s32[:], io["S_in_" + d], writes=["Sin32"])
        Sin[d] = P.sb([128, 8, 512], BF16, name=f"Sin{d}")
        P.dve(lambda e, d=d, di=di: e.tensor_scalar(out=Sin[d][:], in0=s32[:], scalar1=hm[:, di:di + 1], scalar2=None,
                                                    op0=ALU.mult), reads=["Sin32", "ghm"], writes=[("Sin", d)])
    ws3 = WStream(P, [slab_src(wb_out, 256 * j) for j in range(8)] * NT, ring=3, name="ws3")
    qcl = {d: [P.sb([128, 8, 512], BF16) for _ in range(1)] for d in "fb"}
    of_ = [P.sb([128, 4, 512], F32) for _ in range(2)]
    ob_ = [P.sb([128, 4, 512], F32) for _ in range(2)]
    osum = [P.sb([128, 4, 512], F32) for _ in range(2)]
    osq = [P.sb([128, 512], BF16) for _ in range(2)]
    sgl = [P.sb([128, 4, 512], BF16) for _ in range(2)]
    rsd = [P.sb([128, 512], F32) for _ in range(2)]
    z = [P.sb([128, KC, 512], BF16) for _ in range(2)]
    xres = [P.sb([128, 512], F32) for _ in range(3)]
    xo_t = [P.sb([128, 512], F32) for _ in range(3)]
    mm3 = [P.ps([128, 512], F32) for _ in range(4)]
    ssp = [P.ps([128, 512], F32) for _ in range(2)]
    c3 = {"mm": 0}

    def next_mm3():
        i = c3["mm"] % len(mm3)
        c3["mm"] += 1
        return mm3[i], ("mm3", i)

    def heads_gen(t):
        tsl = slice(t * TT, (t + 1) * TT)
        tb = t % 2
        qb = 0
        for d in "fb":
            P.dma("sp", qcl[d][qb][:], QC[d][:, tsl].rearrange("(m p) t -> p m t", p=128),
                  reads=[("QC", d, c) for c in range(4 * t, 4 * t + 4)], writes=[("qcl", d, qb)])
        for hd in range(4):
            hb2 = hd % 2
            rows = slice(hd * 512, (hd + 1) * 512)
            P.dma("sp", of_[hb2][:], OT["f"][rows, tsl].rearrange("(m p) t -> p m t", p=128),
                  reads=[("OT", "f", c) for c in range(4 * t, 4 * t + 4)], writes=[("of", hb2)])
            P.dma("sp", ob_[hb2][:], OT["b"][rows, tsl].rearrange("(m p) t -> p m t", p=128),
                  reads=[("OT", "b", c) for c in range(4 * t, 4 * t + 4)], writes=[("ob", hb2)])
            P.dma("sp", sgl[hb2][:], SG[rows, tsl].rearrange("(m p) t -> p m t", p=128),
                  reads=[("SG", hd * 4 + ec, t) for ec in range(4)], writes=[("sgl", hb2)])
            P.dve(lambda e, hb2=hb2: e.tensor_tensor(out=of_[hb2][:], in0=of_[hb2][:], in1=ob_[hb2][:], op=ALU.add),
                  reads=[("of", hb2), ("ob", hb2)], writes=[("of", hb2)])
            for ec in range(4):
                ps, pk = next_mm3()
                n = 0
                for d in "fb":
                    for dc in range(2):
                        i8 = hd * 2 + dc
                        P.pe(lambda e, ps=ps, d=d, i8=i8, ec=ec, qb=qb, n=n: e.matmul(
                            ps[:], lhsT=Sin[d][:, i8, ec * 128:(ec + 1) * 128], rhs=qcl[d][qb][:, i8, :],
                            start=(n == 0), stop=(n == 3)), reads=[("Sin", d), ("qcl", d, qb)], writes=[pk])
                        n += 1
                P.dve(lambda e, ps=ps, hb2=hb2, ec=ec: e.tensor_tensor(out=osum[hb2][:, ec, :], in0=ps[:], in1=of_[hb2][:, ec, :],
                                                                       op=ALU.add),
                      reads=[pk, ("of", hb2)], writes=[("osum", hb2, ec)])
                sb2 = ec % 2
                P.act(lambda e, hb2=hb2, ec=ec, sb2=sb2: e.activation(out=osq[sb2][:], in_=osum[hb2][:, ec, :], func=AF.Square),
                      reads=[("osum", hb2, ec)], writes=[("osq", sb2)])
                P.pe(lambda e, hb2=hb2, ec=ec, sb2=sb2: e.matmul(ssp[hb2][:], lhsT=C.ones[:], rhs=osq[sb2][:],
                                                                start=(ec == 0), stop=(ec == 3)),
                     reads=["ones", ("osq", sb2)], writes=[("ssp", hb2)])
                yield
            P.act(lambda e, hb2=hb2: e.activation(out=rsd[hb2][:], in_=ssp[hb2][:], func=AF.Ln, scale=1.0 / 512, bias=EPS),
                  reads=[("ssp", hb2)], writes=[("rsd", hb2)])
            P.act(lambda e, hb2=hb2: e.activation(out=rsd[hb2][:], in_=rsd[hb2][:], func=AF.Exp, scale=-0.5),
                  reads=[("rsd", hb2)], writes=[("rsd", hb2)])
            for ec in range(4):
                mg = hd * 4 + ec
                P.dve(lambda e, hb2=hb2, ec=ec: e.scalar_tensor_tensor(
                    out=osum[hb2][:, ec, :], in0=osum[hb2][:, ec, :], scalar=ng[:, ec:ec + 1], in1=rsd[hb2][:],
                    op0=ALU.mult, op1=ALU.mult), reads=[("osum", hb2, ec), "ng", ("rsd", hb2)], writes=[("osum", hb2, ec)])
                P.dve(lambda e, hb2=hb2, ec=ec, mg=mg, tb=tb: e.tensor_tensor(
                    out=z[tb][:, mg, :], in0=osum[hb2][:, ec, :], in1=sgl[hb2][:, ec, :], op=ALU.mult),
                    reads=[("osum", hb2, ec), ("sgl", hb2)], writes=[("z3", tb, mg)])
            yield

    def outproj_gen(t):
        tb = t % 2
        c0 = TT * t

        def load_res(m):
            rb = m % len(xres)
            P.dma("sp", xres[rb][:], xT[m * 128:(m + 1) * 128, c0:c0 + 512], writes=[("xres", rb)])
        for m in range(len(xres)):
            load_res(m)
        for sl in range(8):
            slab, skey = ws3.next()
            for ml in range(2):
                m = sl * 2 + ml
                ps, pk = next_mm3()
                for k in range(KC):
                    P.pe(lambda e, ps=ps, slab=slab, k=k, ml=ml, tb=tb: e.matmul(
                        ps[:], lhsT=slab[:, k, ml * 128:(ml + 1) * 128], rhs=z[tb][:, k, :],
                        start=(k == 0), stop=(k == KC - 1)), reads=[skey, ("z3", tb, k)], writes=[pk])
                rb = m % len(xres)
                ob = m % len(xo_t)
                P.dve(lambda e, ps=ps, m=m, rb=rb, ob=ob: e.scalar_tensor_tensor(
                    out=xo_t[ob][:], in0=ps[:], scalar=C.gate(m), in1=xres[rb][:],
                    op0=ALU.mult, op1=ALU.add), reads=[pk, "mod", ("xres", rb)], writes=[("xo_t", ob)])
                P.dma("pool", io["xo"][m * 128:(m + 1) * 128, TT * t:TT * (t + 1)], xo_t[ob][:],
                      reads=[("xo_t", ob)], writes=[("xo", t, m)])
                if m + len(xres) < KC:
                    load_res(m + len(xres))
                yield

    for _ in heads_gen(0):
        pass
    for t in range(NT):
        ga = outproj_gen(t)
        gb = heads_gen(t + 1) if t + 1 < NT else iter(())
        done_a = done_b = False
        while not (done_a and done_b):
            if not done_b:
                try:
                    next(gb)
                except StopIteration:
                    done_b = True
            if not done_a:
                try:
                    next(ga)
                except StopIteration:
                    done_a = True
    P.release(mtop)


def emit_attn_fused(P, nc, segs, io, pfx="f"):
    m0 = P.mark()
    wb_in = nc.dram_tensor(pfx + "awb_in", [D, 5120], BF16).ap()
    convert_w(P, io["w_in"], wb_in, D, 5120, "acw_in")
    wb_out = nc.dram_tensor(pfx + "awb_out", [D, D], BF16).ap()
    convert_w(P, io["w_out"], wb_out, D, D, "acw_out")
    ns = len(segs)
    Qs = [nc.dram_tensor(pfx + f"aQs{i}", [16, 128, TC], BF16).ap() for i in range(ns)]
    SG = [nc.dram_tensor(pfx + f"aSG{i}", [D, TC], BF16).ap() for i in range(ns)]
    Zs = [nc.dram_tensor(pfx + f"aZs{i}", [D, TC], BF16).ap() for i in range(ns)]
    KTs = [nc.dram_tensor(pfx + f"aKT{i}", [4, 128, TC], BF16).ap() for i in range(ns)]
    Vss = [nc.dram_tensor(pfx + f"aV{i}", [TC, 512], BF16).ap() for i in range(ns)]
    P.release(m0)

    for si, seg in enumerate(segs):
        xT = seg["xT"]
        qtiles = set(seg["qtiles"])
        C = Common(P, io["mod"])
        gains = P.sb([128, 2], F32, name="gains")
        P.dma("sp", gains[:], io["gains"], writes=["gains"])
        rot32 = P.sb([128, 128], F32, name="rot32")
        P.dma("sp", rot32[:], io["rotT"], writes=["rot32"])
        rotb = P.sb([128, 128], BF16, name="rotb")
        P.dve(lambda e, rotb=rotb, rot32=rot32: e.tensor_copy(out=rotb[:], in_=rot32[:]), reads=["rot32"], writes=["rotb"])
        order = []
        for t in range(NT):
            if t in qtiles:
                order += [slab_src(wb_in, 256 * j) for j in range(8)]
            order += [slab_src(wb_in, 2048 + 256 * j) for j in range(4)]
            if t in qtiles:
                order += [slab_src(wb_in, 3072 + 256 * j) for j in range(8)]
        ws = WStream(P, order, ring=4, name=f"wsA{si}")
        nb = norm_bufs(P, 512)
        hT = [P.sb([128, KC, 512], BF16, name=f"ahT{i}") for i in range(2)]
        cs = [P.sb([128, 512], F32) for _ in range(2)]
        sn = [P.sb([128, 512], F32) for _ in range(2)]
        qg_t = [P.sb([128, 512], BF16) for _ in range(2)]
        sq_t = [P.sb([128, 512], BF16) for _ in range(2)]
        rs_t = [P.sb([128, 512], F32) for _ in range(2)]
        t1 = [P.sb([128, 512], F32) for _ in range(2)]
        t2 = [P.sb([128, 512], F32) for _ in range(2)]
        qo = [P.sb([128, 512], BF16) for _ in range(3)]
        vbuf = [P.sb([128, 4, 512], BF16) for _ in range(2)]
        sgt = [P.sb([128, 512], BF16) for _ in range(3)]
        mm = [P.ps([128, 512], F32) for _ in range(3)]
        ssq = [P.ps([128, 512], F32) for _ in range(2)]
        rotp = [P.ps([128, 512], F32) for _ in range(2)]
        cnt = {"mm": 0, "hd": 0, "qo": 0, "sg": 0}

        def next_mm():
            i = cnt["mm"] % len(mm)
            cnt["mm"] += 1
            return mm[i], ("mm", i)

        def qk_head(t, hb, slab, skey, ml, gcol, dst, dkey):
            h = hT[hb]
            ps, pk = next_mm()
            for k in range(KC):
                P.pe(lambda e, ps=ps, slab=slab, k=k, ml=ml, h=h: e.matmul(
                    ps[:], lhsT=slab[:, k, ml * 128:(ml + 1) * 128], rhs=h[:, k, :],
                    start=(k == 0), stop=(k == KC - 1)), reads=[skey, (("ahT", hb), k)], writes=[pk])
            i2 = cnt["hd"] % 2
            cnt["hd"] += 1
            P.act(lambda e, ps=ps, i2=i2, gains=gains, qg_t=qg_t: e.activation(
                out=qg_t[i2][:], in_=ps[:], func=AF.Copy, scale=gains[:, gcol:gcol + 1]),
                reads=[pk, "gains"], writes=[("qg", i2)])
            P.act(lambda e, ps=ps, i2=i2, sq_t=sq_t: e.activation(out=sq_t[i2][:], in_=ps[:], func=AF.Square),
                  reads=[pk], writes=[("sq_t", i2)])
            return lambda: qk_tail(t, i2, dst, dkey)

        def qk_tail(t, i2, dst, dkey):
            P.pe(lambda e, i2=i2, C=C, ssq=ssq, sq_t=sq_t: e.matmul(ssq[i2][:], lhsT=C.ones[:], rhs=sq_t[i2][:], start=True, stop=True),
                 reads=["ones", ("sq_t", i2)], writes=[("ssq", i2)])
            P.pe(lambda e, i2=i2, rotp=rotp, rotb=rotb, qg_t=qg_t: e.matmul(rotp[i2][:], lhsT=rotb[:], rhs=qg_t[i2][:], start=True, stop=True),
                 reads=["rotb", ("qg", i2)], writes=[("rotp", i2)])
            P.act(lambda e, i2=i2, rs_t=rs_t, ssq=ssq: e.activation(out=rs_t[i2][:], in_=ssq[i2][:], func=AF.Ln, scale=1.0 / 128, bias=EPS),
                  reads=[("ssq", i2)], writes=[("rs_t", i2)])
            P.act(lambda e, i2=i2, rs_t=rs_t: e.activation(out=rs_t[i2][:], in_=rs_t[i2][:], func=AF.Exp, scale=-0.5),
                  reads=[("rs_t", i2)], writes=[("rs_t", i2)])
            cb = t % 2
            P.pool(lambda e, i2=i2, cb=cb, t1=t1, qg_t=qg_t, cs=cs: e.tensor_tensor(out=t1[i2][:], in0=qg_t[i2][:], in1=cs[cb][:], op=ALU.mult),
                   reads=[("qg", i2), ("cs", cb)], writes=[("t1", i2)])
            P.dve(lambda e, i2=i2, cb=cb, t2=t2, rotp=rotp, sn=sn: e.tensor_tensor(out=t2[i2][:], in0=rotp[i2][:], in1=sn[cb][:], op=ALU.mult),
                  reads=[("rotp", i2), ("sn", cb)], writes=[("t2", i2)])
            P.pool(lambda e, i2=i2, t1=t1, t2=t2: e.tensor_tensor(out=t1[i2][:], in0=t1[i2][:], in1=t2[i2][:], op=ALU.add),
                   reads=[("t1", i2), ("t2", i2)], writes=[("t1", i2)])
            i3 = cnt["qo"] % 3
            cnt["qo"] += 1
            P.dve(lambda e, i2=i2, i3=i3, qo=qo, t1=t1, rs_t=rs_t: e.tensor_tensor(out=qo[i3][:], in0=t1[i2][:], in1=rs_t[i2][:], op=ALU.mult),
                  reads=[("t1", i2), ("rs_t", i2)], writes=[("qo", i3)])
            P.dma("pool", dst, qo[i3][:], reads=[("qo", i3)], writes=[dkey])

        def projections(t):
            hb = t % 2
            h = hT[hb]
            cb = t % 2
            wantq = t in qtiles
            P.dma("sp", cs[cb][:], seg["cos"][:, t * TT:(t + 1) * TT], writes=[("cs", cb)])
            P.dma("sp", sn[cb][:], seg["sin"][:, t * TT:(t + 1) * TT], writes=[("sn", cb)])
            pend = [None]

            def flush(nxt=None):
                if pend[0] is not None:
                    pend[0]()
                pend[0] = nxt
            if wantq:
                for sl in range(8):
                    slab, skey = ws.next()
                    for ml in range(2):
                        hd = sl * 2 + ml
                        flush(qk_head(t, hb, slab, skey, ml, 0, Qs[si][hd][:, t * TT:(t + 1) * TT], ("Qs", si, hd, t)))
            for sl in range(2):
                slab, skey = ws.next()
                for ml in range(2):
                    kh = sl * 2 + ml
                    flush(qk_head(t, hb, slab, skey, ml, 1, KTs[si][kh][:, t * TT:(t + 1) * TT], ("KT", si, kh, t)))
            vb = t % 2
            for sl in range(2):
                slab, skey = ws.next()
                if sl == 1:
                    flush()
                for j in range(4):
                    ps, pk = next_mm()
                    for k in range(KC):
                        P.pe(lambda e, ps=ps, slab=slab, k=k, j=j, h=h: e.matmul(
                            ps[:, 0:256], lhsT=h[:, k, j * 128:(j + 1) * 128], rhs=slab[:, k, :],
                            start=(k == 0), stop=(k == KC - 1)), reads=[skey, (("ahT", hb), k)], writes=[pk])
                    P.act(lambda e, ps=ps, j=j, sl=sl, vb=vb, vbuf=vbuf: e.copy(out=vbuf[vb][:, j, sl * 256:(sl + 1) * 256],
                                                                                in_=ps[:, 0:256]),
                          reads=[pk], writes=[("vbuf", vb, j, sl)])
            P.dma("pool", Vss[si][t * TT:(t + 1) * TT, :].rearrange("(j p) n -> p j n", p=128), vbuf[vb][:],
                  reads=[("vbuf", vb, j, sl) for j in range(4) for sl in range(2)], writes=[("V", si, t)])
            if wantq:
                for sl in range(8):
                    slab, skey = ws.next()
                    for ml in range(2):
                        m = sl * 2 + ml
                        ps, pk = next_mm()
                        for k in range(KC):
                            P.pe(lambda e, ps=ps, slab=slab, k=k, ml=ml, h=h: e.matmul(
                                ps[:], lhsT=slab[:, k, ml * 128:(ml + 1) * 128], rhs=h[:, k, :],
                                start=(k == 0), stop=(k == KC - 1)), reads=[skey, (("ahT", hb), k)], writes=[pk])
                        i3 = cnt["sg"] % 3
                        cnt["sg"] += 1
                        P.act(lambda e, ps=ps, i3=i3, sgt=sgt: e.activation(out=sgt[i3][:], in_=ps[:], func=AF.Silu),
                              reads=[pk], writes=[("sgt", i3)])
                        P.dma("pool", SG[si][m * 128:(m + 1) * 128, t * TT:(t + 1) * TT], sgt[i3][:],
                              reads=[("sgt", i3)], writes=[("SG", si, m, t)])

        def norm(t):
            emit_norm_mod(P, C, xT, TT * t, 512, hT[t % 2], ("ahT", t % 2), nb)

        norm(0)
        for t in range(NT):
            if t + 1 < NT:
                norm(t + 1)
            projections(t)
        P.release(m0)

    qlist = [(si, t) for si, seg in enumerate(segs) for t in seg["qtiles"]]

    ones32 = P.sb([128, 128], F32, name="ones32")
    P.dve(lambda e: e.memset(ones32[:], 1.0), writes=["ones32"])
    nsc = ns * TC // 128
    kt = [P.sb([128, ns * TC], BF16, name=f"kt{i}") for i in range(2)]
    vt = [P.sb([128, nsc, 128], BF16, name=f"vt{i}") for i in range(2)]
    qt = [P.sb([128, 512], BF16) for _ in range(3)]
    sgq = [P.sb([128, 512], BF16) for _ in range(3)]
    pt = [P.sb([128, 2, 512], BF16) for _ in range(3)]
    accD = [P.sb([128, 2, 512], F32) for _ in range(2)]
    accP = [P.sb([128, 2, 512], F32) for _ in range(2)]
    rinv = [P.sb([128, 512], F32) for _ in range(2)]
    zt = [P.sb([128, 512], BF16) for _ in range(2)]
    st = [P.ps([128, 2, 512], F32) for _ in range(2)]
    ot = [P.ps([128, 512], F32) for _ in range(2)]
    rsp = [P.ps([128, 512], F32) for _ in range(2)]
    onesb = P.sb([128, 128], BF16, name="onesb")
    P.dve(lambda e: e.memset(onesb[:], 1.0), writes=["onesb"])
    it = 0
    sti = 0
    pti = 0
    cps = TC // 128
    npair = nsc // 2
    for kvh in range(4):
        kb = kvh % 2
        for r in range(ns):
            P.dma("sp", kt[kb][:, r * TC:(r + 1) * TC], KTs[r][kvh], reads=[("KT", r, kvh, t) for t in range(NT)],
                  writes=[("kt", kb, r)])
            P.dma("sp", vt[kb][:, r * cps:(r + 1) * cps, :],
                  Vss[r][:, kvh * 128:(kvh + 1) * 128].rearrange("(c p) d -> p c d", p=128),
                  reads=[("V", r, t) for t in range(NT)], writes=[("vt", kb, r)])
        for gq_ in range(4):
            hd = kvh * 4 + gq_
            for (si, t) in qlist:
                ab = it % 2
                q3 = it % 3
                it += 1
                P.dma("sp", qt[q3][:], Qs[si][hd][:, t * TT:(t + 1) * TT], reads=[("Qs", si, hd, t)], writes=[("qt", q3)])
                P.dma("sp", sgq[q3][:], SG[si][hd * 128:(hd + 1) * 128, t * TT:(t + 1) * TT],
                      reads=[("SG", si, hd, t)], writes=[("sgq", q3)])
                LOOK = 1
                p3s = {}
                used = {"dve": False, "pool": False}
                for step in range(npair + LOOK):
                    if step < npair:
                        pr = step
                        s2 = sti % 2
                        sti += 1
                        p3 = pti % 3
                        pti += 1
                        p3s[pr] = p3
                        for j in range(2):
                            sc = 2 * pr + j
                            P.pe(lambda e, s2=s2, kb=kb, sc=sc, q3=q3, j=j: e.matmul(
                                st[s2][:, j, :], lhsT=kt[kb][:, sc * 128:(sc + 1) * 128], rhs=qt[q3][:], start=True, stop=True),
                                reads=[("kt", kb, sc // cps), ("qt", q3)], writes=[("st", s2, j)])
                        P.act(lambda e, s2=s2, p3=p3: e.activation(out=pt[p3][:], in_=st[s2][:], func=AF.Exp, scale=A_SCALE),
                              reads=[("st", s2, 0), ("st", s2, 1)], writes=[("pt", p3)])
                    if step >= LOOK:
                        pr = step - LOOK
                        p3 = p3s[pr]
                        for j in range(2):
                            sc = 2 * pr + j
                            P.pe(lambda e, ab=ab, kb=kb, sc=sc, p3=p3, j=j: e.matmul(
                                ot[ab][:], lhsT=vt[kb][:, sc, :], rhs=pt[p3][:, j, :], start=(sc == 0), stop=(sc == nsc - 1)),
                                reads=[("vt", kb, sc // cps), ("pt", p3)], writes=[("ot", ab)])
                        if pr % 3 == 0:
                            for j in range(2):
                                P.pe(lambda e, ab=ab, p3=p3, j=j, first=(pr == 0 and j == 0): e.matmul(
                                    rsp[ab][:], lhsT=onesb[:], rhs=pt[p3][:, j, :], start=first, stop=False),
                                    reads=["onesb", ("pt", p3)], writes=[("rsp", ab)])
                            continue
                        eng, acc, akey = ("dve", accD[ab], ("accD", ab)) if pr % 3 == 1 else ("pool", accP[ab], ("accP", ab))
                        if not used[eng]:
                            used[eng] = True
                            P.add(eng, lambda e, acc=acc, p3=p3: e.tensor_copy(out=acc[:], in_=pt[p3][:]),
                                  reads=[("pt", p3)], writes=[akey])
                        else:
                            P.add(eng, lambda e, acc=acc, p3=p3: e.tensor_tensor(out=acc[:], in0=acc[:], in1=pt[p3][:], op=ALU.add),
                                  reads=[("pt", p3), akey], writes=[akey])
                n4 = 0
                for acc, akey in ((accD[ab], ("accD", ab)), (accP[ab], ("accP", ab))):
                    for j in range(2):
                        P.pe(lambda e, acc=acc, j=j, n4=n4, ab=ab: e.matmul(rsp[ab][:], lhsT=ones32[:], rhs=acc[:, j, :],
                                                                            start=False, stop=(n4 == 3)),
                             reads=["ones32", akey], writes=[("rsp", ab)])
                        n4 += 1
                P.dve(lambda e, ab=ab: e.reciprocal(out=rinv[ab][:], in_=rsp[ab][:]), reads=[("rsp", ab)], writes=[("rinv", ab)])
                P.pool(lambda e, ab=ab, q3=q3: e.tensor_tensor(out=rinv[ab][:], in0=rinv[ab][:], in1=sgq[q3][:], op=ALU.mult),
                       reads=[("rinv", ab), ("sgq", q3)], writes=[("rinv", ab)])
                P.dve(lambda e, ab=ab: e.tensor_tensor(out=zt[ab][:], in0=ot[ab][:], in1=rinv[ab][:], op=ALU.mult),
                      reads=[("ot", ab), ("rinv", ab)], writes=[("zt", ab)])
                P.dma("pool", Zs[si][hd * 128:(hd + 1) * 128, t * TT:(t + 1) * TT], zt[ab][:],
                      reads=[("zt", ab)], writes=[("Zs", si, hd, t)])
    P.release(m0)

    C = Common(P, io["mod"])
    ws2 = WStream(P, [slab_src(wb_out, 256 * j) for j in range(8)] * len(qlist), ring=4, name="wsC")
    mmC = [P.ps([128, 512], F32) for _ in range(4)]
    z = [P.sb([128, KC, 512], BF16) for _ in range(2)]
    xres = [P.sb([128, 512], F32) for _ in range(3)]
    xo_t = [P.sb([128, 512], F32) for _ in range(3)]
    mmi = 0
    for qi, (si, t) in enumerate(qlist):
        xT, xo = segs[si]["xT"], segs[si]["xo"]
        zb = qi % 2
        for k in range(KC):
            P.dma("sp", z[zb][:, k, :], Zs[si][k * 128:(k + 1) * 128, t * TT:(t + 1) * TT],
                  reads=[("Zs", si, k, t)], writes=[("zc", zb, k)])
        c0 = TT * t

        def load_res(m, xT=xT, c0=c0):
            rb = m % len(xres)
            P.dma("sp", xres[rb][:], xT[m * 128:(m + 1) * 128, c0:c0 + 512], writes=[("xres", rb)])
        for m in range(len(xres)):
            load_res(m)
        for sl in range(8):
            slab, skey = ws2.next()
            for ml in range(2):
                m = sl * 2 + ml
                ps, pk = mmC[mmi % 4], ("mmC", mmi % 4)
                mmi += 1
                for k in range(KC):
                    P.pe(lambda e, ps=ps, slab=slab, k=k, ml=ml, zb=zb: e.matmul(
                        ps[:], lhsT=slab[:, k, ml * 128:(ml + 1) * 128], rhs=z[zb][:, k, :],
                        start=(k == 0), stop=(k == KC - 1)), reads=[skey, ("zc", zb, k)], writes=[pk])
                rb = m % len(xres)
                ob = m % len(xo_t)
                P.dve(lambda e, ps=ps, m=m, rb=rb, ob=ob, C=C: e.scalar_tensor_tensor(
                    out=xo_t[ob][:], in0=ps[:], scalar=C.gate(m), in1=xres[rb][:],
                    op0=ALU.mult, op1=ALU.add), reads=[pk, "mod", ("xres", rb)], writes=[("xo_t", ob)])
                P.dma("pool", xo[m * 128:(m + 1) * 128, TT * t:TT * (t + 1)], xo_t[ob][:],
                      reads=[("xo_t", ob)], writes=[("xo", si, t, m)])
                if m + len(xres) < KC:
                    load_res(m + len(xres))
    P.release(m0)


def emit_modulation(P, nc, io, moddram):
    m0 = P.mark()
    ct = P.sb([128, 16, 1], F32, name="m_ct")
    sc = P.sb([128, 16, 1], F32, name="m_sc")
    one = P.sb([1, 1], F32, name="m_one")
    P.dve(lambda e: e.memset(one[:], 1.0), writes=["m_one"])
    P.dma("sp", ct[:], io["cT"], writes=["m_ct"])
    P.act(lambda e: e.activation(out=sc[:], in_=ct[:], func=AF.Silu), reads=["m_ct"], writes=["m_sc"])
    wt = [P.sb([128, 3072], F32) for _ in range(3)]
    brow = P.sb([1, 6144], F32, name="m_brow")
    mrow = P.sb([1, 6144], F32, name="m_mrow")
    modsb = P.sb([128, 48], F32, name="m_modsb")
    acc = [P.ps([1, 512], F32) for _ in range(6)]
    tp = P.ps([128, 48], F32, name="m_tp")
    wi = 0
    for l in range(4):
        P.dma("sp", brow[:], io["b_mod"][l:l + 1, :], writes=["m_brow"])
        for hf in range(2):
            for kc in range(16):
                b = wi % 3
                wi += 1
                P.dma("sp", wt[b][:], io["w_mod"][l, kc * 128:(kc + 1) * 128, hf * 3072:(hf + 1) * 3072], writes=[("m_wt", b)])
                for n in range(6):
                    P.pe(lambda e, kc=kc, n=n, b=b: e.matmul(acc[n][:], lhsT=sc[:, kc, :], rhs=wt[b][:, n * 512:(n + 1) * 512],
                                                             start=(kc == 0), stop=(kc == 15)),
                         reads=["m_sc", ("m_wt", b)], writes=[("m_acc", n)])
            for n in range(6):
                cs_ = slice(hf * 3072 + n * 512, hf * 3072 + (n + 1) * 512)
                P.dve(lambda e, n=n, cs_=cs_: e.tensor_tensor(out=mrow[:, cs_], in0=acc[n][:], in1=brow[:, cs_], op=ALU.add),
                      reads=[("m_acc", n), "m_brow"], writes=[("m_mrow", hf, n)])
        for j in range(48):
            P.pe(lambda e, j=j: e.matmul(tp[:, j:j + 1], lhsT=mrow[0:1, j * 128:(j + 1) * 128], rhs=one[0:1, 0:1],
                                         start=True, stop=True),
                 reads=[("m_mrow", hf, n) for hf in range(2) for n in range(6)] + ["m_one"], writes=["m_tp"])
        P.dve(lambda e: e.tensor_copy(out=modsb[:], in_=tp[:]), reads=["m_tp"], writes=["m_modsb"])
        P.dma("pool", moddram[l].rearrange("p s k -> p (s k)"), modsb[:], reads=["m_modsb"], writes=[("moddram", l)])
    P.release(m0)


SEQ = 8192
_NC_CACHE = {}


def _new_nc():
    return bass.Bass("TRN2", target_bir_lowering=False)


def build_fused():
    nc = _new_nc()
    E = {}

    def inp(name, shape, dt=F32):
        E[name] = nc.dram_tensor(name, shape, dt, kind="ExternalInput").ap()
    for s in "AB":
        inp("xT" + s, [2048, TC + 16]); inp("hmask" + s, [128, 2]); inp("icnt" + s, [128, 4, 16])
        inp("cos" + s, [128, TC]); inp("sin" + s, [128, TC])
    inp("cT", [128, 16, 1]); inp("w_mod", [4, 2048, 6144]); inp("b_mod", [4, 6144])
    for p in ("p0", "p3"):
        inp(p + "_w_in", [2048, 4096]); inp(p + "_w_grp", [2048, 512]); inp(p + "_w_out", [2048, 2048])
        inp(p + "_pscale", [128, 16])
    inp("fg", [128, 16])
    inp("g_w_in", [2048, 6144]); inp("g_w_out", [2048, 2048]); inp("w1cat", [2048, 64]); inp("w2aug", [64, 1024])
    inp("trif", [128, 128]); inp("trib", [128, 128]); inp("maskf", [128, 128]); inp("maskb", [128, 128])
    inp("ident", [128, 128]); inp("ng", [128, 4])
    inp("a_w_in", [2048, 5120]); inp("a_w_out", [2048, 2048]); inp("gains", [128, 2]); inp("rotT", [128, 128])
    y = nc.dram_tensor("y", [2048, TC], F32, kind="ExternalOutput").ap()

    def scr(name, shape, dt=F32):
        return nc.dram_tensor(name, shape, dt).ap()
    moddram = scr("moddram", [4, 128, 3, 16])
    X1 = [scr("X1A", [2048, TC]), scr("X1B", [2048, TC])]
    X2 = [scr("X2A", [2048, TC]), scr("X2B", [2048, TC])]
    X3A = scr("X3A", [2048, TC + 16])
    X3B = scr("X3B", [2048, TC])
    Se = [{d: scr(f"Se{s}{d}", [128, 8, 512]) for d in "fb"} for s in "AB"]

    P = Prog(nc)
    emit_modulation(P, nc, E, moddram)

    wb = None
    for si, s in enumerate("AB"):
        io = {"xT": E["xT" + s], "mod": moddram[0], "w_in": E["p0_w_in"], "w_grp": E["p0_w_grp"], "w_out": E["p0_w_out"],
              "pscale": E["p0_pscale"], "hmask": E["hmask" + s], "icnt": E["icnt" + s], "xo": X1[si]}
        wb = emit_pool_layer(P, nc, io, final=False, pfx="p0", wb=wb)

    gio = {"mod": moddram[1], "w_in": E["g_w_in"], "w_out": E["g_w_out"], "w1cat": E["w1cat"], "w2aug": E["w2aug"],
           "trif": E["trif"], "trib": E["trib"], "maskf": E["maskf"], "maskb": E["maskb"], "ident": E["ident"],
           "ng": E["ng"]}
    scrs = []
    gwb = None
    for si, s in enumerate("AB"):
        io = dict(gio, xT=X1[si], S_end_f=Se[si]["f"], S_end_b=Se[si]["b"])
        r = emit_gla_layer(P, nc, io, "state", pfx="g" + s, wb=gwb)
        gwb = r["wb"]
        scrs.append(r)
    for si, s in enumerate("AB"):
        io = dict(gio, xT=X1[si], hmask=E["hmask" + s], S_in_f=Se[1 - si]["f"], S_in_b=Se[1 - si]["b"], xo=X2[si])
        emit_gla_layer(P, nc, io, "sweep3", pfx="g" + s, wb=gwb, scr=scrs[si])

    segs = [dict(xT=X2[0], cos=E["cosA"], sin=E["sinA"], qtiles=list(range(NT)), xo=X3A[:, 8:8 + TC]),
            dict(xT=X2[1], cos=E["cosB"], sin=E["sinB"], qtiles=[0, NT - 1], xo=X3B)]
    aio = {"mod": moddram[2], "w_in": E["a_w_in"], "w_out": E["a_w_out"], "gains": E["gains"], "rotT": E["rotT"]}
    emit_attn_fused(P, nc, segs, aio, pfx="f")
    P.dma("sp", X3A[:, 0:8], X3B[:, TC - 8:TC], writes=["halo_l"], allow_slow_non_contiguous=True)
    P.dma("sp", X3A[:, TC + 8:TC + 16], X3B[:, 0:8], writes=["halo_r"], allow_slow_non_contiguous=True)

    io = {"xT": X3A, "mod": moddram[3], "w_in": E["p3_w_in"], "w_grp": E["p3_w_grp"], "w_out": E["p3_w_out"],
          "pscale": E["p3_pscale"], "hmask": E["hmaskA"], "icnt": E["icntA"], "fg": E["fg"], "xo": y}
    emit_pool_layer(P, nc, io, final=True, pfx="p3")
    P.emit()
    return nc


def _rope_tables(t0):
    t = np.arange(t0, t0 + TC)
    row = (t // 64 - (SEQ // 64) // 2).astype(np.float32)
    col = (t % 64 - 32).astype(np.float32)
    inv = (np.float32(10000.0) ** (-np.arange(0, 64, 2, dtype=np.float32) / np.float32(64))).astype(np.float32)
    ang = np.concatenate([row[:, None] * inv, col[:, None] * inv], axis=-1)
    cos = np.repeat(np.cos(ang).astype(np.float32), 2, axis=1).T
    sin = np.repeat(np.sin(ang).astype(np.float32), 2, axis=1).T
    return np.ascontiguousarray(cos), np.ascontiguousarray(sin)


def _rot_const():
    r = np.zeros((128, 128), np.float32)
    for i in range(64):
        r[2 * i + 1, 2 * i] = -1.0
        r[2 * i, 2 * i + 1] = 1.0
    return r


def _gla_consts():
    j = np.arange(128)[:, None]
    i = np.arange(128)[None, :]
    return {
        "trif": np.where(j <= i, -1.0 / 16, 0.0).astype(np.float32),
        "trib": np.where(j >= i, -1.0 / 16, 0.0).astype(np.float32),
        "maskf": (i >= j).astype(np.float32),
        "maskb": (i < j).astype(np.float32),
        "ident": np.eye(128, dtype=np.float32),
    }


def _pool_consts(h):
    t0 = h * TC
    ic = np.zeros((4, 16), np.float32)
    for gi in range(4):
        w = 2 << gi
        for j in range(8):
            for side, t in ((0, t0 + j), (1, t0 + TC - 8 + j)):
                cnt = min(t + w // 2, SEQ) - max(t - w // 2, 0)
                ic[gi, side * 8 + j] = 1.0 / cnt
    hmask = np.broadcast_to(np.array([float(h == 1), float(h == 0)], np.float32), (128, 2))
    return np.ascontiguousarray(hmask), np.ascontiguousarray(np.broadcast_to(ic, (128, 4, 16)))


def _col(v, k=16):
    return np.ascontiguousarray(np.asarray(v, np.float32).reshape(k, 128).T)


def _halo_x(x, b, h):
    t0 = h * TC
    xp = np.zeros((TC + 16, 2048), np.float32)
    lo, hi = max(t0 - 8, 0), min(t0 + TC + 8, SEQ)
    xp[lo - (t0 - 8):hi - (t0 - 8)] = x[b, lo:hi]
    return np.ascontiguousarray(xp.T)


def kernel(x, c, w_mod, b_mod, pool_w_in, pool_w_grp, pool_scale, pool_w_out,
           gla_w_in, gla_fwd_w1, gla_fwd_w2, gla_fwd_b, gla_bwd_w1, gla_bwd_w2, gla_bwd_b,
           gla_norm_g, gla_w_out, attn_w_in, attn_q_norm_g, attn_k_norm_g, attn_w_out, final_norm_g):
    f32 = lambda a: np.ascontiguousarray(np.asarray(a, dtype=np.float32))
    x = f32(x)
    c = f32(c)
    w1 = np.zeros((2048, 64), np.float32)
    w1[:, 0:16] = gla_fwd_w1[0]
    w1[:, 32:48] = gla_bwd_w1[0]
    w2 = np.zeros((64, 1024), np.float32)
    w2[0:16] = gla_fwd_w2[0]
    w2[16] = gla_fwd_b[0]
    w2[32:48] = gla_bwd_w2[0]
    w2[48] = gla_bwd_b[0]
    shared = {
        "w_mod": f32(w_mod), "b_mod": f32(b_mod),
        "p0_w_in": f32(pool_w_in[0]), "p0_w_grp": f32(np.asarray(pool_w_grp[0]).reshape(2048, 512)),
        "p0_w_out": f32(pool_w_out[0]), "p0_pscale": _col(pool_scale[0]),
        "p3_w_in": f32(pool_w_in[1]), "p3_w_grp": f32(np.asarray(pool_w_grp[1]).reshape(2048, 512)),
        "p3_w_out": f32(pool_w_out[1]), "p3_pscale": _col(pool_scale[1]),
        "fg": _col(final_norm_g),
        "g_w_in": f32(gla_w_in[0]), "g_w_out": f32(gla_w_out[0]), "w1cat": w1, "w2aug": w2, "ng": _col(gla_norm_g[0], 4),
        "a_w_in": f32(attn_w_in[0]), "a_w_out": f32(attn_w_out[0]),
        "gains": np.ascontiguousarray(np.stack([np.asarray(attn_q_norm_g[0], np.float32),
                                                np.asarray(attn_k_norm_g[0], np.float32)], axis=1)),
        "rotT": _rot_const(),
    }
    shared.update(_gla_consts())
    rope = [_rope_tables(0), _rope_tables(TC)]
    pc = [_pool_consts(0), _pool_consts(1)]
    in_maps = []
    for core in range(8):
        b, h = core // 2, core % 2
        d = dict(shared)
        for s, hh in (("A", h), ("B", 1 - h)):
            d["xT" + s] = _halo_x(x, b, hh)
            d["hmask" + s], d["icnt" + s] = pc[hh]
            d["cos" + s], d["sin" + s] = rope[hh]
        d["cT"] = np.ascontiguousarray(c[b].reshape(16, 128).T.reshape(128, 16, 1))
        in_maps.append(d)
    if "fused" not in _NC_CACHE:
        _NC_CACHE["fused"] = build_fused()
    res = run_bass_kernel_spmd(_NC_CACHE["fused"], in_maps, core_ids=list(range(8)))
    out = np.empty((4, SEQ, 2048), np.float32)
    for core in range(8):
        out[core // 2, (core % 2) * TC:(core % 2 + 1) * TC] = np.asarray(res.results[core]["y"]).T
    return out
```
